# Optimizing a Trainium2 kernel written in Bass

```python
import math
import jax
import jax.numpy as jnp
from jax import lax
import numpy as np

D_MODEL = 1024
BATCH = 32
SEQ = 256
DEPTH = 4
DEC_BATCH = 2
DEC_SEQ = 1024
PAST_LEN = 256

GRID_W = 64
N_HEADS = 8
N_KV = 2
HEAD_DIM = 64
GQA_G = N_HEADS // N_KV
ATT_W = N_HEADS * HEAD_DIM
KV_W = N_KV * HEAD_DIM
WINDOW = 128
BLOCK = 128
ROPE_BASE = 10000.0
LRU_W = 512
LRU_BLOCKS = 8
LRU_BD = LRU_W // LRU_BLOCKS
LRU_CONV = 4
LRU_C = 8.0
FNET_GROUPS = 4
FNET_W = 512
FNET_GD = FNET_W // FNET_GROUPS
HY_W = 512
HY_ORDER = 2
HY_CONV = 3
HY_BANDS = 16
HY_EMB = 1 + 2 * HY_BANDS
HY_HID = 64
HY_FILT = 2 * HY_ORDER * HY_W
HY_DECAY_MIN = math.log(100.0) / 1.5
HY_DECAY_MAX = math.log(100.0) / 0.3
EVEN_IN = ATT_W + 2 * KV_W + 2 * LRU_W
EVEN_OUT = ATT_W + LRU_W
ODD_IN = FNET_W + (HY_ORDER + 1) * HY_W
ODD_OUT = FNET_W + HY_W
D_FF = 4 * D_MODEL
N_MOD = 6
N_EVEN = (DEPTH + 1) // 2
N_ODD = DEPTH // 2
EPS = 1e-6
NEG = -1e30

kernel_name = 'hybrid_diffusion_prefix_step'


def _rmsnorm(x, g):
    xf = x.astype(jnp.float32)
    y = xf * lax.rsqrt(jnp.mean(xf * xf, axis=-1, keepdims=True) + EPS)
    return (y * g.astype(jnp.float32)).astype(x.dtype)


def _modulation(cvec, w, b):
    m = jax.nn.silu(cvec) @ w + b
    return jnp.split(m[:, None, :], N_MOD, axis=-1)


def _dwconv(x, w, b, left):
    width = w.shape[0]
    L = x.shape[1]
    xp = jnp.pad(x, ((0, 0), (left, width - 1 - left), (0, 0)))
    y = b
    for k in range(width):
        y = y + xp[:, k:k + L] * w[k]
    return y


def _axial_rope(x):
    L = x.shape[1]
    rows = L // GRID_W
    row = jnp.repeat(jnp.arange(rows), GRID_W).astype(jnp.float32)
    col = jnp.tile(jnp.arange(GRID_W), rows).astype(jnp.float32)
    n = HEAD_DIM // 4
    inv = ROPE_BASE ** (-jnp.arange(n, dtype=jnp.float32) / n)
    shape = (1, L) + (1,) * (x.ndim - 3) + (n,)
    xf = x.astype(jnp.float32)
    outs = []
    for a, pos in enumerate((row, col)):
        ang = (pos[:, None] * inv[None, :]).reshape(shape)
        cos, sin = jnp.cos(ang), jnp.sin(ang)
        seg = xf[..., a * 2 * n:(a + 1) * 2 * n]
        x1, x2 = seg[..., :n], seg[..., n:]
        outs += [x1 * cos - x2 * sin, x2 * cos + x1 * sin]
    return jnp.concatenate(outs, axis=-1).astype(x.dtype)


def _sink_attend(q, k, v, sink, mask):
    s = jnp.einsum('bqhgd,bkhd->bhgqk', q, k).astype(jnp.float32) * (HEAD_DIM ** -0.5)
    if mask is not None:
        s = jnp.where(mask, s, NEG)
    sk = jnp.broadcast_to(sink.astype(jnp.float32)[None, :, :, None, None], s.shape[:-1] + (1,))
    p = jax.nn.softmax(jnp.concatenate([s, sk], axis=-1), axis=-1)[..., :-1]
    return jnp.einsum('bhgqk,bkhd->bqhgd', p.astype(v.dtype), v)


def _ctx_attention(q, k, v, sink):
    B, L = q.shape[:2]
    nb = L // BLOCK
    qb = q.reshape(B, nb, BLOCK, N_KV, GQA_G, HEAD_DIM).swapaxes(0, 1)
    o = lax.map(lambda qi: _sink_attend(qi, k, v, sink, None), qb)
    return o.swapaxes(0, 1).reshape(B, L, N_KV, GQA_G, HEAD_DIM)


def _lat_attention(q, k, v, k_ctx, v_ctx, sink):
    B, L = q.shape[:2]
    nb = L // BLOCK
    lc = k_ctx.shape[1]
    qb = q.reshape(B, nb, BLOCK, N_KV, GQA_G, HEAD_DIM).swapaxes(0, 1)
    pad = ((0, 0), (BLOCK, BLOCK), (0, 0), (0, 0))
    kp, vp = jnp.pad(k, pad), jnp.pad(v, pad)
    k_ctx = k_ctx.astype(k.dtype)
    v_ctx = v_ctx.astype(v.dtype)
    rel = jnp.arange(3 * BLOCK)[None, :] - BLOCK - jnp.arange(BLOCK)[:, None]
    win_ok = jnp.abs(rel) <= WINDOW
    ctx_ok = jnp.ones((BLOCK, lc), dtype=bool)

    def one(args):
        qi, bi = args
        start = bi * BLOCK
        kw = lax.dynamic_slice_in_dim(kp, start, 3 * BLOCK, axis=1)
        vw = lax.dynamic_slice_in_dim(vp, start, 3 * BLOCK, axis=1)
        kpos = start - BLOCK + jnp.arange(3 * BLOCK)
        in_seq = (kpos >= 0) & (kpos < L)
        mask = jnp.concatenate([win_ok & in_seq[None, :], ctx_ok], axis=1)
        return _sink_attend(qi, jnp.concatenate([kw, k_ctx], axis=1),
                            jnp.concatenate([vw, v_ctx], axis=1), sink, mask)

    o = lax.map(one, (qb, jnp.arange(nb)))
    return o.swapaxes(0, 1).reshape(B, L, N_KV, GQA_G, HEAD_DIM)


def _combine(e1, e2):
    a1, b1 = e1
    a2, b2 = e2
    return a1 * a2, a2 * b1 + b2


def _rglru_dir(x, h0, w_r, b_r, w_i, b_i, lam, reverse):
    B, L, _ = x.shape
    xb = x.reshape(B, L, LRU_BLOCKS, LRU_BD)
    r = jax.nn.sigmoid(jnp.einsum('blhi,hij->blhj', xb, w_r.astype(jnp.float32)).reshape(B, L, LRU_W) + b_r)
    gi = jax.nn.sigmoid(jnp.einsum('blhi,hij->blhj', xb, w_i.astype(jnp.float32)).reshape(B, L, LRU_W) + b_i)
    log_a = -LRU_C * r * jax.nn.softplus(-lam.astype(jnp.float32))
    a = jnp.exp(log_a)
    u = jnp.sqrt(-jnp.expm1(2.0 * log_a)) * (gi * x)
    if reverse:
        a, u = a[:, ::-1], u[:, ::-1]
    A, S = lax.associative_scan(_combine, (a, u), axis=1)
    h = A * h0.astype(jnp.float32)[:, None, :] + S
    return h[:, ::-1] if reverse else h


def _lru_branch(xr, g, h0f, h0b, conv_w, conv_b, w_r, b_r, w_i, b_i, lam):
    xc = _dwconv(xr, conv_w, conv_b, LRU_CONV // 2).astype(jnp.float32)
    hf = _rglru_dir(xc, h0f, w_r[0], b_r[0], w_i[0], b_i[0], lam[0], False)
    hb = _rglru_dir(xc, h0b, w_r[1], b_r[1], w_i[1], b_i[1], lam[1], True)
    y = (hf + hb) * jax.nn.gelu(g.astype(jnp.float32))
    return y.astype(xr.dtype), hf[:, -1], hb[:, 0]


def _even_mixer(h, k_ctx, v_ctx, h0f, h0b, w_in, w_out, sink, conv_w, conv_b, w_r, b_r, w_i, b_i, lam):
    B, L, _ = h.shape
    p = h @ w_in
    o1, o2, o3, o4 = ATT_W, ATT_W + KV_W, ATT_W + 2 * KV_W, ATT_W + 2 * KV_W + LRU_W
    q = p[..., :o1].reshape(B, L, N_KV, GQA_G, HEAD_DIM)
    k = p[..., o1:o2].reshape(B, L, N_KV, HEAD_DIM)
    v = p[..., o2:o3].reshape(B, L, N_KV, HEAD_DIM)
    xr, g = p[..., o3:o4], p[..., o4:]
    sink = sink.reshape(N_KV, GQA_G)
    if k_ctx is None:
        att = _ctx_attention(q, k, v, sink)
        h0f = h0b = jnp.zeros((B, LRU_W), jnp.float32)
    else:
        att = _lat_attention(_axial_rope(q), _axial_rope(k), v, k_ctx, v_ctx, sink)
    y_lru, sf, sb = _lru_branch(xr, g, h0f, h0b, conv_w, conv_b, w_r, b_r, w_i, b_i, lam)
    out = jnp.concatenate([att.reshape(B, L, ATT_W), y_lru], axis=-1) @ w_out
    return out, k, v, sf, sb


def _hyena_filters(L, w1, b1, w2, b2, w3, freq, log_decay):
    t = jnp.arange(L, dtype=jnp.float32)
    tn = t / L
    bands = jnp.linspace(1e-4, HY_BANDS - 1, HY_BANDS, dtype=jnp.float32)
    w = (2.0 * math.pi / L) * t
    z = jnp.concatenate([tn[:, None], jnp.cos(w[:, None] * bands), -jnp.sin(w[:, None] * bands)], axis=-1)
    hid = jnp.sin(freq[0].astype(jnp.float32) * (z @ w1.astype(jnp.float32) + b1))
    hid = jnp.sin(freq[1].astype(jnp.float32) * (hid @ w2.astype(jnp.float32) + b2))
    filt = (hid @ w3.astype(jnp.float32)) * jnp.exp(-tn[:, None] * jnp.exp(log_decay.astype(jnp.float32)))
    filt = filt.reshape(L, 2, HY_ORDER, HY_W)
    two = jnp.concatenate([filt[:, 0], filt[::-1, 1]], axis=0)
    two = two * lax.rsqrt(jnp.sum(two * two, axis=0, keepdims=True) + EPS)
    return jnp.fft.rfft(two, axis=0)


def _fft_conv(u, kf):
    L = u.shape[1]
    U = jnp.fft.rfft(u, n=2 * L, axis=1)
    return jnp.fft.irfft(U * kf[None], n=2 * L, axis=1)[:, :L]


def _odd_mixer(h, w_in, w_out, conv_w, conv_b, w1, b1, w2, b2, w3, freq, log_decay, hy_bias):
    B, L, _ = h.shape
    p = h @ w_in
    f = p[..., :FNET_W].astype(jnp.float32).reshape(B, L, FNET_GROUPS, FNET_GD)
    yf = jnp.fft.fft2(f, axes=(1, 3), norm='ortho').real.reshape(B, L, FNET_W)
    u = _dwconv(p[..., FNET_W:], conv_w, conv_b, HY_CONV // 2).astype(jnp.float32)
    v, x1, x2 = jnp.split(u, HY_ORDER + 1, axis=-1)
    kf = _hyena_filters(L, w1, b1, w2, b2, w3, freq, log_decay)
    z = v
    for n, gate in enumerate((x1, x2)):
        z = gate * (_fft_conv(z, kf[:, n]) + hy_bias[n].astype(jnp.float32) * z)
    return jnp.concatenate([yf, z], axis=-1).astype(h.dtype) @ w_out


def _mlp(h, w1, w2):
    return jnp.square(jax.nn.relu(h @ w1)) @ w2


def setup_inputs(seed: int = 0) -> dict:
    key = jax.random.key(seed)
    ks = iter(jax.random.split(key, 40))

    def nrm(shape, s):
        return jax.random.normal(next(ks), shape, jnp.float32) * s

    lam_u = jax.random.uniform(next(ks), (N_EVEN, 2, LRU_W), jnp.float32, 0.9, 0.999)
    lam_s = lam_u ** (1.0 / LRU_C)
    decay0 = jnp.log(jnp.linspace(HY_DECAY_MIN, HY_DECAY_MAX, HY_FILT, dtype=jnp.float32))
    return {
        'x_prompt': nrm((BATCH, SEQ, D_MODEL), 1.0),
        'x_sample': nrm((DEC_BATCH, DEC_SEQ, D_MODEL), 1.0),
        'c': nrm((DEC_BATCH, D_MODEL), 1.0),
        'cache_k': nrm((DEC_BATCH, N_EVEN, PAST_LEN, N_KV, HEAD_DIM), 1.0),
        'cache_v': nrm((DEC_BATCH, N_EVEN, PAST_LEN, N_KV, HEAD_DIM), 1.0),
        'state_lru': nrm((DEC_BATCH, N_EVEN, 2, LRU_W), 0.5),
        'c_ctx': nrm((D_MODEL,), 1.0),
        'mod_w': nrm((DEPTH, D_MODEL, N_MOD * D_MODEL), 0.5 * D_MODEL ** -0.5),
        'mod_b': nrm((DEPTH, N_MOD * D_MODEL), 0.02),
        'norm_mix': 1.0 + nrm((DEPTH, D_MODEL), 0.02),
        'norm_mlp': 1.0 + nrm((DEPTH, D_MODEL), 0.02),
        'norm_final': 1.0 + nrm((D_MODEL,), 0.02),
        'mlp_w1': nrm((DEPTH, D_MODEL, D_FF), D_MODEL ** -0.5),
        'mlp_w2': nrm((DEPTH, D_FF, D_MODEL), D_FF ** -0.5),
        'ev_w_in': nrm((N_EVEN, D_MODEL, EVEN_IN), D_MODEL ** -0.5),
        'ev_w_out': nrm((N_EVEN, EVEN_OUT, D_MODEL), EVEN_OUT ** -0.5),
        'attn_sink': nrm((N_EVEN, N_HEADS), 0.5),
        'lru_conv_w': nrm((N_EVEN, LRU_CONV, LRU_W), LRU_CONV ** -0.5),
        'lru_conv_b': nrm((N_EVEN, LRU_W), 0.02),
        'lru_w_r': nrm((N_EVEN, 2, LRU_BLOCKS, LRU_BD, LRU_BD), LRU_BD ** -0.5),
        'lru_b_r': nrm((N_EVEN, 2, LRU_W), 0.02),
        'lru_w_i': nrm((N_EVEN, 2, LRU_BLOCKS, LRU_BD, LRU_BD), LRU_BD ** -0.5),
        'lru_b_i': nrm((N_EVEN, 2, LRU_W), 0.02),
        'lru_lambda': jnp.log(lam_s) - jnp.log1p(-lam_s),
        'od_w_in': nrm((N_ODD, D_MODEL, ODD_IN), D_MODEL ** -0.5),
        'od_w_out': nrm((N_ODD, ODD_OUT, D_MODEL), ODD_OUT ** -0.5),
        'hy_conv_w': nrm((N_ODD, HY_CONV, (HY_ORDER + 1) * HY_W), HY_CONV ** -0.5),
        'hy_conv_b': nrm((N_ODD, (HY_ORDER + 1) * HY_W), 0.02),
        'hy_w1': nrm((N_ODD, HY_EMB, HY_HID), HY_EMB ** -0.5),
        'hy_b1': nrm((N_ODD, HY_HID), 0.02),
        'hy_w2': nrm((N_ODD, HY_HID, HY_HID), HY_HID ** -0.5),
        'hy_b2': nrm((N_ODD, HY_HID), 0.02),
        'hy_w3': nrm((N_ODD, HY_HID, HY_FILT), HY_HID ** -0.5),
        'hy_freq': 1.0 + nrm((N_ODD, 2, HY_HID), 0.02),
        'hy_log_decay': decay0[None, :] + nrm((N_ODD, HY_FILT), 0.02),
        'hy_bias': nrm((N_ODD, HY_ORDER, HY_W), 0.1),
    }


def reference(x_prompt, x_sample, c, cache_k, cache_v, state_lru, c_ctx, mod_w, mod_b, norm_mix, norm_mlp,
              norm_final, mlp_w1, mlp_w2, ev_w_in, ev_w_out, attn_sink, lru_conv_w, lru_conv_b, lru_w_r,
              lru_b_r, lru_w_i, lru_b_i, lru_lambda, od_w_in, od_w_out, hy_conv_w, hy_conv_b, hy_w1, hy_b1,
              hy_w2, hy_b2, hy_w3, hy_freq, hy_log_decay, hy_bias):
    xp, xs = x_prompt, x_sample
    k_list, v_list, s_list = [], [], []
    for l in range(DEPTH):
        mp = _modulation(c_ctx[None, :], mod_w[l], mod_b[l])
        ms = _modulation(c, mod_w[l], mod_b[l])
        hp = _rmsnorm(xp, norm_mix[l]) * (1.0 + mp[1]) + mp[0]
        hs = _rmsnorm(xs, norm_mix[l]) * (1.0 + ms[1]) + ms[0]
        j = l // 2
        if l % 2 == 0:
            ev = (ev_w_in[j], ev_w_out[j], attn_sink[j], lru_conv_w[j], lru_conv_b[j], lru_w_r[j],
                  lru_b_r[j], lru_w_i[j], lru_b_i[j], lru_lambda[j])
            op, kc, vc, sf, sb = _even_mixer(hp, None, None, None, None, *ev)
            os_ = _even_mixer(hs, cache_k[:, j], cache_v[:, j], state_lru[:, j, 0], state_lru[:, j, 1], *ev)[0]
            k_list.append(kc)
            v_list.append(vc)
            s_list.append(jnp.stack([sf, sb], axis=1))
        else:
            od = (od_w_in[j], od_w_out[j], hy_conv_w[j], hy_conv_b[j], hy_w1[j], hy_b1[j], hy_w2[j],
                  hy_b2[j], hy_w3[j], hy_freq[j], hy_log_decay[j], hy_bias[j])
            op = _odd_mixer(hp, *od)
            os_ = _odd_mixer(hs, *od)
        xp = xp + mp[2] * op
        xs = xs + ms[2] * os_
        hp = _rmsnorm(xp, norm_mlp[l]) * (1.0 + mp[4]) + mp[3]
        hs = _rmsnorm(xs, norm_mlp[l]) * (1.0 + ms[4]) + ms[3]
        xp = xp + mp[5] * _mlp(hp, mlp_w1[l], mlp_w2[l])
        xs = xs + ms[5] * _mlp(hs, mlp_w1[l], mlp_w2[l])
    y_prompt = _rmsnorm(xp, norm_final)
    y_sample = _rmsnorm(xs, norm_final)
    k_state = jnp.stack(k_list, axis=1)
    v_state = jnp.stack(v_list, axis=1)
    lru_state = jnp.stack(s_list, axis=1).astype(x_prompt.dtype)
    return (y_prompt, y_sample, k_state, v_state, lru_state)
```

```python
import math
from contextlib import ExitStack

import numpy as np
import ml_dtypes
import concourse.bass as bass
import concourse.mybir as mybir
from concourse.bass_utils import run_bass_kernel_spmd

F32 = mybir.dt.float32
BF16 = mybir.dt.bfloat16
AF = mybir.ActivationFunctionType
ALU = mybir.AluOpType

NCORES = 8
JOV = {}
D = 1024
NT = 2048
HALF = 1024
TT = 512
DEPTH = 4
EPS = 1e-6
MAGIC = 12582912.0
TWO_PI = 2.0 * math.pi
EV_Q, EV_KD, EV_KV, EV_LRU, EV_QP, EV_KDP = 0, 512, 768, 1024, 2048, 2560
EV_COLS = 2816
OD_COLS = 2048


class Tk:
    __slots__ = ("w", "r", "dsem", "dn", "name")

    def __init__(self, name=""):
        self.w = None
        self.r = {}
        self.dsem = None
        self.dn = 0
        self.name = name


class Eng:
    def __init__(self, name, sem):
        self.name, self.sem, self.n = name, sem, 0
        self.waited = {}
        self.prog = []


class Ctx:
    def __init__(self, nc, es):
        self.nc, self.es = nc, es
        self.sem_id = 0
        mk = lambda n: Eng(n, self.new_sem(n))
        self.pe, self.act, self.dve, self.pool, self.sp = mk("pe"), mk("act"), mk("dve"), mk("pool"), mk("sp")
        self.banks = []
        self.bank_i = 0
        self.held = set()
        self.out_dmas = []

    def new_sem(self, name):
        self.sem_id += 1
        return self.es.enter_context(self.nc.semaphore(f"s{self.sem_id}_{name}"))

    def sb(self, name, shape, dt):
        return self.es.enter_context(self.nc.sbuf_tensor("sb_" + name, shape, dt))

    def _wait(self, eng, deps):
        for sem, val in deps.values():
            key = id(sem)
            if eng.waited.get(key, 0) < val:
                eng.prog.append(("w", sem, val))
                eng.waited[key] = val

    @staticmethod
    def _add(deps, tok):
        if tok is None:
            return
        sem, val = tok
        k = id(sem)
        if k not in deps or deps[k][1] < val:
            deps[k] = (sem, val)

    def op(self, eng, reads, writes, meth, **kw):
        deps = {}
        for t in reads:
            self._add(deps, t.w)
        for t in writes:
            self._add(deps, t.w)
            for tok in t.r.values():
                self._add(deps, tok)
        if eng is self.pe:
            deps.pop(id(eng.sem), None)
        self._wait(eng, deps)
        eng.n += 1
        eng.prog.append(("i", meth, kw, eng.sem, 1))
        tok = (eng.sem, eng.n)
        for t in reads:
            t.r[id(eng.sem)] = tok
        for t in writes:
            t.w = tok
            t.r = {}

    def dma(self, eng, out, in_, wt=None, rt=None, nowaw=False):
        deps = {}
        own = wt if wt is not None else rt
        if wt is not None:
            if not nowaw:
                self._add(deps, wt.w)
            for tok in wt.r.values():
                self._add(deps, tok)
        if rt is not None:
            self._add(deps, rt.w)
        self._wait(eng, deps)
        if own.dsem is None:
            own.dsem = self.new_sem("d")
        own.dn += 16
        eng.prog.append(("i", "dma_start", dict(out=out, in_=in_), own.dsem, 16))
        tok = (own.dsem, own.dn)
        if wt is not None:
            wt.w = tok
            wt.r = {}
            if rt is not None:
                rt.r[id(own.dsem)] = tok
        else:
            rt.r[id(own.dsem)] = tok
            self.out_dmas.append(tok)

    def allgather(self, src_ap, dst_ap, groups, rt, wt, kind="AllGather", op=None):
        eng = self.pool
        deps = {}
        self._add(deps, rt.w)
        self._add(deps, wt.w)
        for tok in wt.r.values():
            self._add(deps, tok)
        self._wait(eng, deps)
        if wt.dsem is None:
            wt.dsem = self.new_sem("cc")
        wt.dn += 1
        eng.prog.append(("i", "collective_compute", dict(kind=kind, op=(op or ALU.bypass), replica_groups=groups,
                                                         ins=[src_ap], outs=[dst_ap]), wt.dsem, 1))
        tok = (wt.dsem, wt.dn)
        wt.w = tok
        wt.r = {}
        rt.r[id(wt.dsem)] = tok

    def replay(self, eng, h):
        for it in eng.prog:
            if it[0] == "w":
                h.wait_ge(it[1], it[2])
            else:
                _, meth, kw, sem, inc = it
                getattr(h, meth)(**kw).then_inc(sem, inc)

    def bank(self, hold=False):
        while (self.bank_i % 8) in self.held:
            self.bank_i += 1
        i = self.bank_i % 8
        b = self.banks[i]
        self.bank_i += 1
        if hold:
            self.held.add(i)
        return b

    def release(self, b):
        for i, bb in enumerate(self.banks):
            if bb[1] is b[1]:
                self.held.discard(i)

    def mm(self, out, lhsT, rhs, start, stop, reads, writes):
        self.op(self.pe, reads, writes, "matmul", out=out, lhsT=lhsT, rhs=rhs, start=start, stop=stop)

    def actv(self, out, in_, func, reads, writes, bias=0.0, scale=1.0):
        self.op(self.act, reads, writes, "activation", out=out, in_=in_, func=func, bias=bias, scale=scale)

    def tt(self, out, in0, in1, op, reads, writes):
        self.op(self.dve, reads, writes, "tensor_tensor", out=out, in0=in0, in1=in1, op=op)

    def ts(self, out, in0, s1, s2, op0, op1, reads, writes):
        if op1 is None:
            self.op(self.dve, reads, writes, "tensor_scalar", out=out, in0=in0, scalar1=s1, scalar2=None, op0=op0)
        else:
            self.op(self.dve, reads, writes, "tensor_scalar", out=out, in0=in0, scalar1=s1, scalar2=s2, op0=op0, op1=op1)

    def stt(self, out, in0, scalar, in1, op0, op1, reads, writes):
        self.op(self.dve, reads, writes, "scalar_tensor_tensor", out=out, in0=in0, scalar=scalar, in1=in1, op0=op0, op1=op1)

    def cp(self, out, in_, reads, writes):
        self.op(self.dve, reads, writes, "tensor_copy", out=out, in_=in_)

    def recip(self, out, in_, reads, writes):
        self.op(self.dve, reads, writes, "reciprocal", out=out, in_=in_)


class SPool:
    def __init__(self, C, name, n, dt):
        self.items = [(C.sb(f"{name}{i}", [128, 1024], dt), Tk(f"{name}{i}")) for i in range(n)]
        self.free = list(range(n))
        self.name = name

    def get(self):
        assert self.free, f"pool {self.name} exhausted"
        i = self.free.pop(0)
        t, k = self.items[i]
        return (t, k, i)

    def put(self, *hs):
        for h in hs:
            assert h[2] not in self.free
            self.free.append(h[2])


def fm(v, nch):
    v = np.asarray(v, np.float32)
    lead = v.shape[:-1]
    v = v.reshape(lead + (nch, 128))
    return np.ascontiguousarray(np.moveaxis(v, -1, 0))


def _bf(a):
    return np.ascontiguousarray(np.asarray(a, np.float32)).astype(ml_dtypes.bfloat16)


_CONST_CACHE = {}


def _const_tables():
    if _CONST_CACHE:
        return _CONST_CACHE
    c = {}
    k = np.arange(128, dtype=np.float64)
    ph = 2.0 * np.pi * np.outer(k, k) / 128.0
    c["fn_ch"] = _bf(np.concatenate([np.cos(ph), np.sin(ph)], axis=1))
    for L in (256, 1024):
        l = np.arange(L, dtype=np.float64)
        th = 2.0 * np.pi * (np.outer(l, l) % L) / L
        t = np.stack([np.cos(th), -np.sin(th)], axis=0)
        c[f"fn_{L}"] = _bf(t.reshape(2, L // 128, 128, L).transpose(2, 0, 1, 3))
        a = (l + 0.5)
        w = 2.0 * np.pi * np.outer(a, a) / (2.0 * L)
        t = np.stack([np.cos(w), np.sin(w)], axis=0)
        c[f"hy_{L}"] = _bf(t.reshape(2, L // 128, 128, L).transpose(2, 0, 1, 3))
        wf = np.pi * a / (2.0 * L)
        ph2 = np.stack([np.cos(wf), np.sin(wf)], axis=0).reshape(2, L // 128, 128).transpose(2, 0, 1)
        c[f"hyph_{L}"] = np.ascontiguousarray(ph2.astype(np.float32))
        c[f"ntn_{L}"] = np.ascontiguousarray((-(l / L)).reshape(L // 128, 128).T.astype(np.float32))
        t32 = np.arange(L, dtype=np.float32)
        tn = t32 / np.float32(L)
        bands = np.linspace(1e-4, 16 - 1, 16, dtype=np.float32)
        wv = np.float32(2.0 * math.pi / L) * t32
        z = np.concatenate([tn[:, None], np.cos(wv[:, None] * bands), -np.sin(wv[:, None] * bands)], axis=-1)
        c[f"zf_{L}"] = np.ascontiguousarray(z.T.astype(np.float32))
    Ls = 1024
    pos_row = (np.arange(Ls) // 64).astype(np.float32)
    pos_col = (np.arange(Ls) % 64).astype(np.float32)
    n = 16
    inv = (np.float32(10000.0) ** (-np.arange(n, dtype=np.float32) / np.float32(n))).astype(np.float32)
    cosT = np.zeros((64, Ls), np.float64)
    sinT = np.zeros((64, Ls), np.float64)
    for dd in range(64):
        a_ = dd // 32
        i = dd % 16
        pos = pos_row if a_ == 0 else pos_col
        ang = (pos * inv[i]).astype(np.float32).astype(np.float64)
        cosT[dd] = np.cos(ang)
        sinT[dd] = np.sin(ang) * (-1.0 if (dd % 32) < 16 else 1.0)
    c["rope"] = np.ascontiguousarray(np.stack([np.tile(cosT, (2, 1)), np.tile(sinT, (2, 1))], axis=1).astype(np.float32))
    kk = np.arange(128)[:, None]
    qq = np.arange(128)[None, :]
    c["amask"] = _bf(np.concatenate([(kk <= qq), np.ones((128, 128), bool), (kk >= qq)], axis=1).astype(np.float32))
    c["ident"] = np.eye(128, dtype=np.float32)
    _CONST_CACHE.update(c)
    return _CONST_CACHE


ROPE_PARTNER = np.array([(dd + 16) if (dd % 32) < 16 else (dd - 16) for dd in range(64)])


def _prep_shared(inp):
    s = {}
    f32 = lambda a: np.ascontiguousarray(np.asarray(a, np.float32))
    s["mod_w"] = f32(inp["mod_w"])
    s["mod_b"] = fm(inp["mod_b"], 48)
    s["norm_mix"] = fm(inp["norm_mix"], 8)
    s["norm_mlp"] = fm(inp["norm_mlp"], 8)
    s["norm_final"] = fm(inp["norm_final"], 8)
    s["mlp_w1"] = f32(inp["mlp_w1"])
    s["mlp_w2"] = f32(inp["mlp_w2"])
    w = f32(inp["ev_w_in"])
    q, k, v, xr, g = w[..., :512], w[..., 512:640], w[..., 640:768], w[..., 768:1280], w[..., 1280:]
    k0, k1 = k[..., :64], k[..., 64:]
    qp = q.reshape(2, 1024, 8, 64)[..., ROPE_PARTNER].reshape(2, 1024, 512)
    k0p, k1p = k0[..., ROPE_PARTNER], k1[..., ROPE_PARTNER]
    lru_cols = []
    for c in range(4):
        lru_cols += [xr[..., c * 128:(c + 1) * 128], g[..., c * 128:(c + 1) * 128]]
    s["ev_w_in"] = np.ascontiguousarray(np.concatenate(
        [q, k0, k0, k1, k1, k, v] + lru_cols + [qp, k0p, k0p, k1p, k1p], axis=-1))
    assert s["ev_w_in"].shape[-1] == EV_COLS
    s["ev_w_out"] = f32(inp["ev_w_out"])
    sk = f32(inp["attn_sink"])
    s["sink"] = np.ascontiguousarray(np.repeat(sk.reshape(2, 4, 2, 1), 64, axis=3).reshape(2, 4, 128).transpose(2, 0, 1))
    s["lru_conv_w"] = fm(inp["lru_conv_w"], 4)
    s["lru_conv_b"] = fm(inp["lru_conv_b"], 4)
    s["lru_b_r"] = fm(inp["lru_b_r"], 4)
    s["lru_b_i"] = fm(inp["lru_b_i"], 4)
    s["lru_lam"] = fm(inp["lru_lambda"], 4)
    bd = np.zeros((2, 2, 2, 4, 128, 128), np.float32)
    for gi_, wsrc in enumerate((inp["lru_w_r"], inp["lru_w_i"])):
        wsrc = np.asarray(wsrc, np.float32)
        for c in range(4):
            bd[:, gi_, :, c, 0:64, 0:64] = wsrc[:, :, 2 * c]
            bd[:, gi_, :, c, 64:128, 64:128] = wsrc[:, :, 2 * c + 1]
    s["lru_bd"] = np.ascontiguousarray(bd.transpose(4, 0, 1, 2, 3, 5).reshape(128, 2, 2, 1024))
    wo = f32(inp["od_w_in"])
    cols = [wo[..., 0:512]]
    for c in range(4):
        for si in range(3):
            cols.append(wo[..., 512 + si * 512 + c * 128:512 + si * 512 + (c + 1) * 128])
    s["od_w_in"] = np.ascontiguousarray(np.concatenate(cols, axis=-1))
    s["od_w_out"] = f32(inp["od_w_out"])
    s["hy_conv_w"] = fm(inp["hy_conv_w"], 12)
    s["hy_conv_b"] = fm(inp["hy_conv_b"], 12)
    s["hy_bias"] = fm(inp["hy_bias"], 4)
    s["hy_w1"] = np.ascontiguousarray(f32(inp["hy_w1"]).transpose(1, 0, 2))
    s["hy_w2"] = np.ascontiguousarray(f32(inp["hy_w2"]).transpose(1, 0, 2))
    s["hy_b12"] = np.ascontiguousarray(np.stack([f32(inp["hy_b1"]), f32(inp["hy_b2"])], axis=1).transpose(2, 0, 1))
    s["hy_freq"] = np.ascontiguousarray(f32(inp["hy_freq"]).transpose(2, 0, 1))
    w3 = f32(inp["hy_w3"]).reshape(2, 64, 2, 2, 4, 128)
    s["hy_w3"] = np.ascontiguousarray(w3.transpose(1, 0, 4, 3, 2, 5).reshape(64, 2, 2048))
    ld = f32(inp["hy_log_decay"]).reshape(2, 2, 2, 4, 128)
    s["hy_ld"] = np.ascontiguousarray(ld.transpose(0, 3, 2, 1, 4).reshape(1, 2 * 2048))
    s.update(_const_tables())
    return s


def _prep_core(inp, i):
    bs = i // 4
    xp = np.asarray(inp["x_prompt"], np.float32)[4 * i:4 * i + 4].reshape(1024, D)
    xs = np.asarray(inp["x_sample"], np.float32)[bs]
    X = np.concatenate([xp, xs], axis=0)
    m = {}
    m["xT"] = np.ascontiguousarray(X.reshape(NT, 8, 128).transpose(2, 1, 0))
    cv = np.stack([np.asarray(inp["c_ctx"], np.float32), np.asarray(inp["c"], np.float32)[bs]], axis=0)
    m["cvec"] = np.ascontiguousarray(cv.reshape(2, 8, 128).transpose(2, 1, 0))
    ck = np.asarray(inp["cache_k"], np.float32)[bs]
    kc = ck.transpose(3, 0, 2, 1)
    m["kctx"] = np.ascontiguousarray(np.concatenate([kc, kc], axis=0))
    cvv = np.asarray(inp["cache_v"], np.float32)[bs].reshape(2, 2, 128, 128)
    m["vctx"] = np.ascontiguousarray(cvv.transpose(2, 0, 1, 3))
    m["h0"] = fm(np.asarray(inp["state_lru"], np.float32)[bs], 4)
    r = i % 4
    hkv = r // 2
    we = np.asarray(inp["ev_w_in"], np.float32)
    q_, k_, v_, xr_, g_ = we[..., :512], we[..., 512:640], we[..., 640:768], we[..., 768:1280], we[..., 1280:]
    qr = q_[..., r * 128:(r + 1) * 128]
    qrp = qr.reshape(2, 1024, 2, 64)[..., ROPE_PARTNER].reshape(2, 1024, 128)
    m["s_ev_w"] = np.ascontiguousarray(np.concatenate(
        [qr, qrp, xr_[..., r * 128:(r + 1) * 128], g_[..., r * 128:(r + 1) * 128]], axis=-1))
    kh = k_[..., hkv * 64:(hkv + 1) * 64]
    khp = kh[..., ROPE_PARTNER]
    m["s_kv_w"] = np.ascontiguousarray(np.concatenate([kh, kh, khp, khp, v_[..., hkv * 64:(hkv + 1) * 64]], axis=-1))
    sk = np.asarray(inp["attn_sink"], np.float32)
    m["s_sink"] = np.ascontiguousarray(np.repeat(sk[:, 2 * r:2 * r + 2], 64, axis=1).T)
    m["s_kctx"] = np.ascontiguousarray(m["kctx"][:, :, hkv, :])
    m["s_vctx"] = np.ascontiguousarray(m["vctx"][:, :, :, hkv * 64:(hkv + 1) * 64])
    bd = np.zeros((2, 2, 2, 128, 128), np.float32)
    for gi_, wsrc in enumerate((inp["lru_w_r"], inp["lru_w_i"])):
        wsrc = np.asarray(wsrc, np.float32)
        bd[:, gi_, :, 0:64, 0:64] = wsrc[:, :, 2 * r]
        bd[:, gi_, :, 64:128, 64:128] = wsrc[:, :, 2 * r + 1]
    m["s_lbd"] = np.ascontiguousarray(bd.transpose(3, 0, 1, 2, 4).reshape(128, 2, 2, 256))
    ch = lambda a: np.asarray(a, np.float32).reshape(a.shape[:-1] + (4, 128))[..., r, :]
    m["s_lcw"] = np.ascontiguousarray(np.moveaxis(ch(inp["lru_conv_w"]), -1, 0))
    m["s_lcb"] = np.ascontiguousarray(np.moveaxis(ch(inp["lru_conv_b"]), -1, 0))
    m["s_lbr"] = np.ascontiguousarray(np.moveaxis(ch(inp["lru_b_r"]), -1, 0))
    m["s_lbi"] = np.ascontiguousarray(np.moveaxis(ch(inp["lru_b_i"]), -1, 0))
    m["s_llam"] = np.ascontiguousarray(np.moveaxis(ch(inp["lru_lambda"]), -1, 0))
    m["s_h0"] = np.ascontiguousarray(np.moveaxis(ch(np.asarray(inp["state_lru"], np.float32)[bs]), -1, 0))
    m["s_w1"] = np.ascontiguousarray(np.asarray(inp["mlp_w1"], np.float32)[:, :, r * 1024:(r + 1) * 1024])
    m["s_w2"] = np.ascontiguousarray(np.asarray(inp["mlp_w2"], np.float32)[:, r * 1024:(r + 1) * 1024, :])
    wo = np.asarray(inp["od_w_in"], np.float32)
    m["s_od_w"] = np.ascontiguousarray(np.concatenate(
        [wo[..., r * 128:(r + 1) * 128]] + [wo[..., 512 + si * 512 + r * 128:512 + si * 512 + (r + 1) * 128] for si in range(3)],
        axis=-1))
    hcw = np.asarray(inp["hy_conv_w"], np.float32).reshape(2, 3, 3, 4, 128)[:, :, :, r, :]
    m["s_hcw"] = np.ascontiguousarray(hcw.transpose(3, 0, 1, 2))
    hcb = np.asarray(inp["hy_conv_b"], np.float32).reshape(2, 3, 4, 128)[:, :, r, :]
    m["s_hcb"] = np.ascontiguousarray(hcb.transpose(2, 0, 1))
    hb = np.asarray(inp["hy_bias"], np.float32).reshape(2, 2, 4, 128)[:, :, r, :]
    m["s_hbias"] = np.ascontiguousarray(hb.transpose(2, 0, 1))
    w3 = np.asarray(inp["hy_w3"], np.float32).reshape(2, 64, 2, 2, 4, 128)[:, :, :, :, r, :]
    m["s_hw3"] = np.ascontiguousarray(w3.transpose(1, 0, 3, 2, 4).reshape(64, 2, 512))
    ld = np.asarray(inp["hy_log_decay"], np.float32).reshape(2, 2, 2, 4, 128)[:, :, :, r, :]
    m["s_hld"] = np.ascontiguousarray(ld.transpose(0, 2, 1, 3).reshape(1, 1024))
    return m


def build(depth=DEPTH, debug=False, skip=(), layers=None):
    nc = bass.Bass("TRN2", target_bir_lowering=False)
    with ExitStack() as es:
        _build_body(nc, es, depth, debug, skip, layers)
    return nc


def _build_body(nc, es, depth, debug, skip, layers=None):
    def din(name, shape, dt=F32):
        return nc.dram_tensor(name, list(shape), dt, kind="ExternalInput").ap()

    def dout(name, shape):
        return nc.dram_tensor(name, list(shape), F32, kind="ExternalOutput").ap()

    d = {}
    d["xT"] = din("xT", [128, 8, NT])
    d["cvec"] = din("cvec", [128, 8, 2])
    d["kctx"] = din("kctx", [128, 2, 2, 256])
    d["vctx"] = din("vctx", [128, 2, 2, 128])
    d["h0"] = din("h0", [128, 2, 2, 4])
    d["s_od_w"] = din("s_od_w", [2, 1024, 512])
    d["s_w1"] = din("s_w1", [4, 1024, 1024])
    d["s_w2"] = din("s_w2", [4, 1024, 1024])
    ar_src = nc.dram_tensor("ar_src", [1024, HALF], F32).ap()
    ar_dst = nc.dram_tensor("ar_dst", [1024, HALF], F32).ap()
    d["s_ev_w"] = din("s_ev_w", [2, 1024, 512])
    d["s_kv_w"] = din("s_kv_w", [2, 1024, 320])
    d["s_sink"] = din("s_sink", [128, 2])
    d["s_kctx"] = din("s_kctx", [128, 2, 256])
    d["s_vctx"] = din("s_vctx", [128, 2, 2, 64])
    d["s_lbd"] = din("s_lbd", [128, 2, 2, 256])
    d["s_lcw"] = din("s_lcw", [128, 2, 4])
    d["s_lcb"] = din("s_lcb", [128, 2])
    d["s_lbr"] = din("s_lbr", [128, 2, 2])
    d["s_lbi"] = din("s_lbi", [128, 2, 2])
    d["s_llam"] = din("s_llam", [128, 2, 2])
    d["s_h0"] = din("s_h0", [128, 2, 2])
    d["s_hcw"] = din("s_hcw", [128, 2, 3, 3])
    d["s_hcb"] = din("s_hcb", [128, 2, 3])
    d["s_hbias"] = din("s_hbias", [128, 2, 2])
    d["s_hw3"] = din("s_hw3", [64, 2, 512])
    d["s_hld"] = din("s_hld", [1, 1024])
    cc_src = nc.dram_tensor("cc_src", [2 * 128, HALF], BF16).ap()
    cc_dst = nc.dram_tensor("cc_dst", [4 * 2 * 128, HALF], BF16).ap()
    d["mod_w"] = din("mod_w", [4, 1024, 6144])
    d["mod_b"] = din("mod_b", [128, 4, 48])
    d["norm_mix"] = din("norm_mix", [128, 4, 8])
    d["norm_mlp"] = din("norm_mlp", [128, 4, 8])
    d["norm_final"] = din("norm_final", [128, 8])
    d["mlp_w1"] = din("mlp_w1", [4, 1024, 4096])
    d["mlp_w2"] = din("mlp_w2", [4, 4096, 1024])
    d["ev_w_in"] = din("ev_w_in", [2, 1024, EV_COLS])
    d["ev_w_out"] = din("ev_w_out", [2, 1024, 1024])
    d["sink"] = din("sink", [128, 2, 4])
    d["lru_conv_w"] = din("lru_conv_w", [128, 2, 4, 4])
    d["lru_conv_b"] = din("lru_conv_b", [128, 2, 4])
    d["lru_b_r"] = din("lru_b_r", [128, 2, 2, 4])
    d["lru_b_i"] = din("lru_b_i", [128, 2, 2, 4])
    d["lru_lam"] = din("lru_lam", [128, 2, 2, 4])
    d["lru_bd"] = din("lru_bd", [128, 2, 2, 1024])
    d["od_w_in"] = din("od_w_in", [2, 1024, OD_COLS])
    d["od_w_out"] = din("od_w_out", [2, 1024, 1024])
    d["hy_conv_w"] = din("hy_conv_w", [128, 2, 3, 12])
    d["hy_conv_b"] = din("hy_conv_b", [128, 2, 12])
    d["hy_bias"] = din("hy_bias", [128, 2, 2, 4])
    d["hy_w1"] = din("hy_w1", [33, 2, 64])
    d["hy_w2"] = din("hy_w2", [64, 2, 64])
    d["hy_b12"] = din("hy_b12", [64, 2, 2])
    d["hy_freq"] = din("hy_freq", [64, 2, 2])
    d["hy_w3"] = din("hy_w3", [64, 2, 2048])
    d["hy_ld"] = din("hy_ld", [1, 4096])
    d["fn_ch"] = din("fn_ch", [128, 256], BF16)
    for L in (256, 1024):
        d[f"fn_{L}"] = din(f"fn_{L}", [128, 2, L // 128, L], BF16)
        d[f"hy_{L}"] = din(f"hy_{L}", [128, 2, L // 128, L], BF16)
        d[f"hyph_{L}"] = din(f"hyph_{L}", [128, 2, L // 128])
        d[f"ntn_{L}"] = din(f"ntn_{L}", [128, L // 128])
        d[f"zf_{L}"] = din(f"zf_{L}", [33, L])
    d["rope"] = din("rope", [128, 2, 1024])
    d["amask"] = din("amask", [128, 384], BF16)
    d["ident"] = din("ident", [128, 128])
    o_y = dout("yT", [128, 8, NT])
    o_kv = dout("kv", [4, 2, 2, 128, 256])
    o_lru = dout("lruT", [128, 2, 2, 4, 4])
    o_dbg = dout("dbg", [depth, 128, 8, NT]) if debug else None
    o_dbg2 = dout("dbg2", [128, 2, 2, 4]) if debug else None

    C = Ctx(nc, es)
    sb = C.sb
    marks = []
    build.marks = marks

    def mark(name):
        marks.append((name, C.pe.n))
    for i in range(8):
        C.banks.append((es.enter_context(nc.psum_tensor(f"bank{i}", [128, 512], F32)), Tk(f"bank{i}")))

    x = sb("x", [128, 8, NT], F32)
    xk = [[Tk(f"x{c}_{t}") for t in range(4)] for c in range(8)]
    hm = sb("hm", [128, 8, NT], BF16)
    hk = [Tk("h0"), Tk("h1")]
    mixk = [Tk(f"mix{c}") for c in range(8)]

    class Off:
        def __init__(self, t, off, n):
            self.t, self.off, self.n = t, off, n

        def __getitem__(self, idx):
            p, c, sl = idx
            a = 0 if sl.start is None else sl.start
            b = self.n if sl.stop is None else sl.stop
            return self.t[p, c, self.off + a:self.off + b]

    hbuf = Off(hm, 0, NT)
    mixb = Off(hm, HALF, HALF)

    def hkl(lt):
        return [hk[lt]] if lt < 2 else list(mixk)
    RING_N = 6
    ring = [sb(f"ring{i}", [128, 4096], BF16) for i in range(RING_N)]
    ringk = [Tk(f"ring{i}") for i in range(RING_N)]
    ring_i = [0]
    ring_allowed = [list(range(RING_N))]
    side_gen = [None, False]

    class BigTab:
        def get(self, cs_, tl, a, b):
            si = 2 + cs_ * 2 + tl // 4
            o = (tl % 4) * 1024
            return ring[si][:, o + a:o + b], ringk[si]

        def load(self, name):
            for cs_ in range(2):
                for hh in range(2):
                    si = 2 + cs_ * 2 + hh
                    C.dma(C.sp, ring[si][:].rearrange("p (a b) -> p a b", a=4), d[name][:, cs_, hh * 4:(hh + 1) * 4, :],
                          wt=ringk[si])

    class SmallTab:
        def __init__(self, t, k):
            self.t, self.k = t, k

        def get(self, cs_, tl, a, b):
            return self.t[:, cs_, tl, a:b], self.k

    bigtab = BigTab()
    FP = SPool(C, "fp", 8, F32)
    BP = SPool(C, "bp", 7, BF16)

    def load_w(src2d, nk, ncols):
        al = ring_allowed[0]
        i = al[ring_i[0] % len(al)]
        ring_i[0] += 1
        slot, tk = ring[i], ringk[i]
        view = slot[:, 0:nk * ncols].rearrange("p (k n) -> p k n", k=nk)
        C.dma(C.pool, view, src2d.rearrange("(k p) n -> p k n", p=128), wt=tk)
        if side_gen[0] is not None and not side_gen[1]:
            side_gen[1] = True
            next(side_gen[0], None)
            side_gen[1] = False
        return view, tk

    def ld_const(name, shape, src, dt=F32, eng=None):
        t = sb(name, shape, dt)
        tk = Tk(name)
        C.dma(eng or C.sp, t[:], src, wt=tk)
        return t, tk

    xload = Tk("xload")
    for c in range(8):
        C.dma(C.sp, x[:, c, :], d["xT"][:, c, :], wt=xload)
    for c in range(8):
        for t in range(4):
            xk[c][t].w = xload.w

    cvec, cvec_k = ld_const("cvec", [128, 8, 2], d["cvec"])
    mod_b, mod_b_k = ld_const("mod_b", [128, 4, 48], d["mod_b"])
    nmix, nmix_k = ld_const("nmix", [128, 4, 8], d["norm_mix"])
    nmlp, nmlp_k = ld_const("nmlp", [128, 4, 8], d["norm_mlp"])
    nfin, nfin_k = ld_const("nfin", [128, 8], d["norm_final"])
    sink, sink_k = ld_const("sink", [128, 2, 4], d["sink"])
    lcw, lcw_k = ld_const("lcw", [128, 2, 4, 4], d["lru_conv_w"])
    lcb, lcb_k = ld_const("lcb", [128, 2, 4], d["lru_conv_b"])
    lbr, lbr_k = ld_const("lbr", [128, 2, 2, 4], d["lru_b_r"])
    lbi, lbi_k = ld_const("lbi", [128, 2, 2, 4], d["lru_b_i"])
    llam, llam_k = ld_const("llam", [128, 2, 2, 4], d["lru_lam"])
    hcw, hcw_k = ld_const("hcw", [128, 2, 3, 12], d["hy_conv_w"])
    hcb, hcb_k = ld_const("hcb", [128, 2, 12], d["hy_conv_b"])
    hbias, hbias_k = ld_const("hbias", [128, 2, 2, 4], d["hy_bias"])
    shcw, shcw_k = ld_const("shcw", [128, 2, 3, 3], d["s_hcw"])
    shcb, shcb_k = ld_const("shcb", [128, 2, 3], d["s_hcb"])
    shbias, shbias_k = ld_const("shbias", [128, 2, 2], d["s_hbias"])
    ccs_k, ccd_k = Tk("cc_src"), Tk("cc_dst")
    ars_k, ard_k = Tk("ar_src"), Tk("ar_dst")
    GROUPS = [[0, 1, 2, 3], [4, 5, 6, 7]]

    hw1, hw1_k = ld_const("hw1", [33, 2, 64], d["hy_w1"])
    hw2, hw2_k = ld_const("hw2", [64, 2, 64], d["hy_w2"])
    hb12, hb12_k = ld_const("hb12", [64, 2, 2], d["hy_b12"])
    hfreq, hfreq_k = ld_const("hfreq", [64, 2, 2], d["hy_freq"])
    amask, amask_k = ld_const("amask", [128, 384], d["amask"], BF16)
    ident, ident_k = ld_const("ident", [128, 128], d["ident"])
    fnch, fnch_k = ld_const("fnch", [128, 256], d["fn_ch"], BF16)
    fn256, fn256_k = ld_const("fn256", [128, 2, 2, 256], d["fn_256"], BF16)
    hy256, hy256_k = ld_const("hy256", [128, 2, 2, 256], d["hy_256"], BF16)
    hyph = {L: ld_const(f"hyph{L}", [128, 2, L // 128], d[f"hyph_{L}"]) for L in (256, 1024)}
    ntn = {L: ld_const(f"ntn{L}", [128, L // 128], d[f"ntn_{L}"]) for L in (256, 1024)}
    kctx, kctx_k = ld_const("kctx", [128, 2, 256], d["s_kctx"], BF16, eng=C.pool)
    vctx, vctx_k = ld_const("vctx", [128, 2, 2, 64], d["s_vctx"], BF16, eng=C.pool)
    ssink, ssink_k = ld_const("ssink", [128, 2], d["s_sink"])
    slcw, slcw_k = ld_const("slcw", [128, 2, 4], d["s_lcw"])
    slcb, slcb_k = ld_const("slcb", [128, 2], d["s_lcb"])
    slbr, slbr_k = ld_const("slbr", [128, 2, 2], d["s_lbr"])
    slbi, slbi_k = ld_const("slbi", [128, 2, 2], d["s_lbi"])
    sllam, sllam_k = ld_const("sllam", [128, 2, 2], d["s_llam"])
    sh0, sh0_k = ld_const("sh0", [128, 2, 2], d["s_h0"])

    def exchange_start():
        C.dma(C.pool, cc_src[0:128, :], mixb[:, 0, :], wt=ccs_k, rt=mixk[0])
        C.dma(C.pool, cc_src[128:256, :], mixb[:, 4, :], wt=ccs_k, rt=mixk[4])
        C.allgather(cc_src, cc_dst, GROUPS, rt=ccs_k, wt=ccd_k)

    def exchange_finish():
        for r_ in range(4):
            for w_ in range(2):
                cix = w_ * 4 + r_
                C.dma(C.sp if (cix % 2) else C.pool, mixb[:, cix, :], cc_dst[(r_ * 2 + w_) * 128:(r_ * 2 + w_ + 1) * 128, :],
                      wt=mixk[cix], rt=ccd_k)

    ones_b = sb("ones_b", [128, 128], BF16)
    ones_k = Tk("ones")
    C.op(C.dve, [], [ones_k], "memset", ap=ones_b[:], constant=1.0)
    epst = sb("epst", [128, 1], F32)
    eps_k = Tk("eps")
    C.op(C.dve, [], [eps_k], "memset", ap=epst[:], constant=EPS)
    zerot = sb("zerot", [128, 1], F32)
    zero_k = Tk("zero")
    C.op(C.dve, [], [zero_k], "memset", ap=zerot[:], constant=0.0)
    onet = sb("onet", [128, 1], F32)
    one_k = Tk("one")
    C.op(C.dve, [], [one_k], "memset", ap=onet[:], constant=1.0)

    modM_ = [sb(f"modM{i}", [128, 48, 2], F32) for i in range(2)]
    modMk_ = [Tk(f"modM{i}") for i in range(2)]
    modA_ = [sb(f"modA{i}", [128, 2, 8, 2], F32) for i in range(2)]
    modAk_ = [Tk(f"modA{i}") for i in range(2)]
    cur = {"par": 0}
    silu_c = sb("silu_c", [128, 8, 2], BF16)
    silu_k = Tk("silu_c")
    C.actv(silu_c[:], cvec[:], AF.Silu, [cvec_k], [silu_k])


    def modulation_gen(l, staged=False):
        par = l % 2
        mM, mMk, mA, mAk = modM_[par], modMk_[par], modA_[par], modAk_[par]
        pbb = C.bank(hold=True)
        pb, pk = pbb
        pv = pb[:, 0:96].rearrange("p (n v) -> p n v", v=2)

        def fin(c0, c1, nis):
            for v in range(2):
                C.tt(mM[:, c0:c1, v], pv[:, c0:c1, v], mod_b[:, l, c0:c1], ALU.add, [pk, mod_b_k], [mMk])
            for ni in nis:
                gt, gk, off = ((nmix, nmix_k, 8), (nmlp, nmlp_k, 32))[ni]
                for v in range(2):
                    C.stt(mA[:, ni, :, v], mM[:, off:off + 8, v], 1.0, gt[:, l, :], ALU.add, ALU.mult,
                          [mMk, gk], [mAk])

        for nb in range(12):
            wv, wk = load_w(d["mod_w"][l, 0:1024, nb * 512:(nb + 1) * 512], 8, 512)
            for nn in range(4):
                n = nb * 4 + nn
                for k in range(8):
                    C.mm(pb[:, 2 * n:2 * n + 2], wv[:, k, nn * 128:(nn + 1) * 128], silu_c[:, k, :],
                         k == 0, k == 7, [wk, silu_k], [pk])
            if staged and nb == 3:
                fin(0, 16, [0])
            if staged and nb == 5:
                fin(16, 24, [])
            if staged and nb == 11:
                fin(24, 48, [1])
            if nb == 11:
                if not staged:
                    fin(0, 48, [0, 1])
                C.release(pbb)
            yield

    def rstd_tile(t):
        ts_ = slice(t * TT, (t + 1) * TT)
        pb, pk = C.bank()
        sq = BP.get()
        for c in range(8):
            o = (c % 2) * TT
            C.actv(sq[0][:, o:o + TT], x[:, c, ts_], AF.Square, [xk[c][t]], [sq[1]])
            C.mm(pb[:], ones_b[:], sq[0][:, o:o + TT], c == 0, c == 7, [ones_k, sq[1]], [pk])
        BP.put(sq)
        rs = FP.get()
        C.actv(rs[0][:, TT:2 * TT], pb[:], AF.Ln, [pk, eps_k], [rs[1]], bias=epst[:, 0:1], scale=1.0 / D)
        C.actv(rs[0][:, 0:TT], rs[0][:, TT:2 * TT], AF.Exp, [rs[1]], [rs[1]], scale=-0.5)
        return rs

    def norm_mod(t, lt, ni, v, B_off):
        mark(f"norm t{t}")
        ts_ = slice(t * TT, (t + 1) * TT)
        ls_ = slice(lt * TT, (lt + 1) * TT)
        rs = rstd_tile(t)
        tmp = FP.get()
        for c in range(8):
            o = (c % 2) * TT
            C.tt(tmp[0][:, o:o + TT], x[:, c, ts_], rs[0][:, 0:TT], ALU.mult, [xk[c][t], rs[1]], [tmp[1]])
            C.actv(hbuf[:, c, ls_], tmp[0][:, o:o + TT], AF.Identity, [tmp[1], modAk_[cur['par']], modMk_[cur['par']]], hkl(lt),
                   bias=modM_[cur['par']][:, B_off + c, v:v + 1], scale=modA_[cur['par']][:, ni, c, v:v + 1])
        FP.put(rs, tmp)

    def proj_fm(wv, wk, col0, width=128):
        res = []
        for lt in range(2):
            pb, pk = C.bank()
            for k in range(8):
                C.mm(pb[0:width, :], wv[:, k, col0:col0 + width], hbuf[:, k, lt * TT:(lt + 1) * TT], k == 0, k == 7,
                     [wk, hk[lt]], [pk])
            res.append((pb, pk))
        return res

    def resid(pb, pk, c, t, gate_ap):
        C.stt(x[:, c, t * TT:(t + 1) * TT], pb[:], gate_ap, x[:, c, t * TT:(t + 1) * TT], ALU.mult, ALU.add,
              [pk, modMk_[cur['par']], xk[c][t]], [xk[c][t]])

    def w_out_phase(wdram, j, half):
        mark(f'h{half} w_out')
        for nb in range(2):
            wO, wOk = load_w(wdram[j, 0:1024, nb * 512:(nb + 1) * 512], 8, 512)
            for nn in range(4):
                n = nb * 4 + nn
                for lt in range(2):
                    pb, pk = C.bank()
                    for k in range(8):
                        C.mm(pb[:], wO[:, k, nn * 128:(nn + 1) * 128], mixb[:, k, lt * TT:(lt + 1) * TT], k == 0, k == 7,
                             [wOk, mixk[k]], [pk])
                    resid(pb, pk, n, 2 * half + lt, modM_[cur['par']][:, 16 + n, half:half + 1])

    kvst = [sb("kvst0", [128, 256], F32)] * 2
    kvstk = [Tk("kvst0")] * 2
    lst = sb("lst", [128, 2, 2, 4, 4], F32)
    lstk = Tk("lst")
    lcst = sb("lcst", [128, 2, 2, 4], F32)
    lcstk = Tk("lcst")
    sinkE = sb("sinkE", [128, 2, 4], F32)
    sinkEk = Tk("sinkE")
    C.actv(sinkE[:], sink[:], AF.Exp, [sink_k], [sinkEk])
    ltmp = sb("ltmp", [128, 2, 2, 4], F32)
    ltmpk = Tk("ltmp")
    C.actv(ltmp[:], llam[:], AF.Exp, [llam_k], [ltmpk], scale=-1.0)
    C.actv(ltmp[:], ltmp[:], AF.Ln, [ltmpk, one_k], [ltmpk], bias=onet[:, 0:1])
    C.ts(lcst[:], ltmp[:], -8.0, None, ALU.mult, None, [ltmpk], [lcstk])
    if debug:
        C.dma(C.sp, o_dbg2, lcst[:], rt=lcstk)

    ssinkE = sb("ssinkE", [128, 2], F32)
    ssinkEk = Tk("ssinkE")
    C.actv(ssinkE[:], ssink[:], AF.Exp, [ssink_k], [ssinkEk])
    slcst = sb("slcst", [128, 2, 2], F32)
    slcstk = Tk("slcst")
    C.actv(slcst[:], sllam[:], AF.Exp, [sllam_k], [slcstk], scale=-1.0)
    C.actv(slcst[:], slcst[:], AF.Ln, [slcstk, one_k], [slcstk], bias=onet[:, 0:1])
    C.ts(slcst[:], slcst[:], -8.0, None, ALU.mult, None, [slcstk], [slcstk])

    def attn_norm(pn, pnk, num_ap, den_ap, prow, width, j, hq, out_ap):
        s1 = FP.get()
        C.actv(s1[0][prow, 0:width], den_ap, AF.Ln, [pnk, sinkEk], [s1[1]], bias=sinkE[prow, j, hq // 2:hq // 2 + 1])
        C.actv(s1[0][prow, 0:width], s1[0][prow, 0:width], AF.Exp, [s1[1]], [s1[1]], scale=-1.0)
        C.tt(out_ap, num_ap, s1[0][prow, 0:width], ALU.mult, [pnk, s1[1]], [mixk[hq // 2]])
        FP.put(s1)

    def even_mixer(l, half):
        j = l // 2
        mark(f'L{l}h{half} even:attn')
        jw = JOV.get('jw', j); jl = JOV.get('jl', j); jc = JOV.get('jc', j); jo = JOV.get('jo', j); js = JOV.get('js', j); jsa = JOV.get('jsa', js); jsb = JOV.get('jsb', js); jsc = JOV.get('jsc', js); jsc2 = JOV.get('jsc2', jsc)
        t0 = half * HALF
        sample = half == 1
        nseq = 1 if sample else 4
        Ls = 1024 if sample else 256
        scale = 0.125
        wKV, wKVk = load_w(d["ev_w_in"][jw, 0:1024, EV_KD:EV_KD + 512], 8, 512)
        vtm = BP.get()
        vt = vtm[0][:].rearrange("p (a b) -> p a b", a=8)
        for tl in range(8):
            pb, pk = C.bank()
            for k in range(8):
                C.mm(pb[:, 0:256], hbuf[:, k, tl * 128:(tl + 1) * 128], wKV[:, k, 256:512], k == 0, k == 7,
                     [wKVk, hk[tl // 4]], [pk])
            C.actv(vt[:, tl, :], pb[:, 128:256], AF.Copy, [pk], [vtm[1]])
            if not sample:
                i = tl % 2
                C.cp(kvst[i][:], pb[:, 0:256], [pk], [kvstk[i]])
                C.dma(C.sp, o_kv[tl // 2, jo, tl % 2], kvst[i][:], rt=kvstk[i])
        if sample:
            wKP, wKPk = load_w(d["ev_w_in"][jw, 0:1024, EV_KDP:EV_KDP + 256], 8, 256)
            rp = [FP.get(), FP.get()]
            for i_ in range(2):
                C.dma(C.sp, rp[i_][0][:], d["rope"][:, i_, :], wt=rp[i_][1])

        def roped(dst, ps, pp):
            for lt in range(2):
                sl = slice(lt * TT, (lt + 1) * TT)
                s1 = FP.get()
                C.tt(s1[0][:, 0:TT], ps[lt][0][:], rp[0][0][:, sl], ALU.mult, [ps[lt][1], rp[0][1]], [s1[1]])
                C.tt(s1[0][:, TT:2 * TT], pp[lt][0][:], rp[1][0][:, sl], ALU.mult, [pp[lt][1], rp[1][1]], [s1[1]])
                C.tt(dst[0][:, sl], s1[0][:, 0:TT], s1[0][:, TT:2 * TT], ALU.add, [s1[1]], [dst[1]])
                FP.put(s1)

        def plain(dst, ps):
            C.actv(dst[0][:, 0:TT], ps[0][0][:], AF.Copy, [ps[0][1]], [dst[1]])
            C.cp(dst[0][:, TT:2 * TT], ps[1][0][:], [ps[1][1]], [dst[1]])

        kTs = []
        for hkv in range(2):
            kT = BP.get()
            ps = proj_fm(wKV, wKVk, hkv * 128)
            if sample:
                pp = proj_fm(wKP, wKPk, hkv * 128)
                roped(kT, ps, pp)
            else:
                plain(kT, ps)
            kTs.append(kT)
        wQ, wQk = load_w(d["ev_w_in"][jw, 0:1024, EV_Q:EV_Q + 512], 8, 512)
        if sample:
            wQP, wQPk = load_w(d["ev_w_in"][jw, 0:1024, EV_QP:EV_QP + 512], 8, 512)
        for qc_i in range(4):
            hkv = qc_i // 2
            kT = kTs[hkv]
            qT = BP.get()
            ps = proj_fm(wQ, wQk, qc_i * 128)
            if sample:
                pp = proj_fm(wQP, wQPk, qc_i * 128)
                roped(qT, ps, pp)
            else:
                plain(qT, ps)
            for hq in (2 * qc_i, 2 * qc_i + 1):
                pbase = (hq % 2) * 64
                prow = slice(pbase, pbase + 64)
                vcol = slice(hkv * 64, (hkv + 1) * 64)
                if not sample:
                    def st1(s, prow=prow):
                        ssl = slice(s * 256, (s + 1) * 256)
                        pb, pk = C.bank()
                        for kt in range(2):
                            C.mm(pb[:, kt * 256:(kt + 1) * 256],
                                 kT[0][prow, s * 256 + kt * 128:s * 256 + (kt + 1) * 128],
                                 qT[0][prow, ssl], True, True, [kT[1], qT[1]], [pk])
                        eb = BP.get()
                        C.actv(eb[0][:, 0:512], pb[:], AF.Exp, [pk], [eb[1]], scale=scale)
                        return eb

                    def st2(s, eb, prow=prow, vcol=vcol, hq=hq):
                        ssl = slice(s * 256, (s + 1) * 256)
                        pn, pnk = C.bank()
                        for kt in range(2):
                            C.mm(pn[prow, 0:256], vt[:, s * 2 + kt, vcol], eb[0][:, kt * 256:(kt + 1) * 256],
                                 kt == 0, kt == 1, [vtm[1], eb[1]], [pnk])
                        for kt in range(2):
                            C.mm(pn[prow, 256:512], ones_b[:, 0:64], eb[0][:, kt * 256:(kt + 1) * 256],
                                 kt == 0, kt == 1, [ones_k, eb[1]], [pnk])
                        BP.put(eb)
                        attn_norm(pn, pnk, pn[prow, 0:256], pn[prow, 256:512], prow, 256, jsa, hq,
                                  mixb[prow, hq // 2, ssl])

                    ebs = st1(0)
                    for s in range(4):
                        nxt_eb = st1(s + 1) if s + 1 < 4 else None
                        st2(s, ebs)
                        ebs = nxt_eb
                else:
                    for qh in range(2):
                        el = [BP.get(), BP.get()]
                        ec = BP.get()
                        kts = [kt for kt in range(4 * qh - 1, 4 * qh + 5) if 0 <= kt <= 7]
                        einfo = {}
                        esi, eoff = 0, 0
                        for kt in kts:
                            q0 = max(kt - 1, 4 * qh)
                            q1 = min(kt + 1, 4 * qh + 3)
                            nq = (q1 - q0 + 1) * 128
                            m0 = (q0 - (kt - 1)) * 128
                            if eoff + nq > 1024:
                                esi += 1
                                eoff = 0
                            pb, pk = C.bank()
                            C.mm(pb[:, 0:nq], kT[0][prow, kt * 128:(kt + 1) * 128], qT[0][prow, q0 * 128:q0 * 128 + nq],
                                 True, True, [kT[1], qT[1]], [pk])
                            eh = el[esi]
                            eo = eoff
                            C.actv(eh[0][:, eo:eo + nq], pb[:, 0:nq], AF.Exp, [pk], [eh[1]], scale=scale)
                            C.tt(eh[0][:, eo:eo + nq], eh[0][:, eo:eo + nq], amask[:, m0:m0 + nq], ALU.mult,
                                 [eh[1], amask_k], [eh[1]])
                            einfo[kt] = (eh, eo, q0)
                            eoff += nq
                        for ct in range(2):
                            pb, pk = C.bank()
                            C.mm(pb[:], kctx[prow, jc, hkv, ct * 128:(ct + 1) * 128], qT[0][prow, qh * 512:(qh + 1) * 512],
                                 True, True, [kctx_k, qT[1]], [pk])
                            C.actv(ec[0][:, ct * 512:(ct + 1) * 512], pb[:], AF.Exp, [pk], [ec[1]], scale=scale)
                        pn, pnk = C.bank()
                        pd, pdk = C.bank()
                        srcs = []
                        for ct in range(2):
                            srcs.append((vctx[:, jc, ct, vcol], ec[0][:, ct * 512:(ct + 1) * 512], vctx_k, ec[1], 0, 512))
                        for kt in kts:
                            eh, eo, q0 = einfo[kt]
                            q1 = min(kt + 1, 4 * qh + 3)
                            nq = (q1 - q0 + 1) * 128
                            srcs.append((vt[:, kt, vcol], eh[0][:, eo:eo + nq], vtm[1], eh[1], (q0 - 4 * qh) * 128, nq))
                        for si, (vv, ee, vk_, ek_, c0, cn) in enumerate(srcs):
                            C.mm(pn[prow, c0:c0 + cn], vv, ee, si == 0, si == len(srcs) - 1, [vk_, ek_], [pnk])
                        for si, (vv, ee, vk_, ek_, c0, cn) in enumerate(srcs):
                            C.mm(pd[prow, c0:c0 + cn], ones_b[:, 0:64], ee, si == 0, si == len(srcs) - 1, [ones_k, ek_], [pdk])
                        BP.put(el[0], el[1], ec)
                        s1 = FP.get()
                        C.actv(s1[0][prow, 0:512], pd[prow, :], AF.Ln, [pdk, sinkEk], [s1[1]],
                               bias=sinkE[prow, jsa, hq // 2:hq // 2 + 1])
                        C.actv(s1[0][prow, 0:512], s1[0][prow, 0:512], AF.Exp, [s1[1]], [s1[1]], scale=-1.0)
                        C.tt(mixb[prow, hq // 2, qh * 512:(qh + 1) * 512], pn[prow, :], s1[0][prow, 0:512], ALU.mult,
                             [pnk, s1[1]], [mixk[hq // 2]])
                        FP.put(s1)
            BP.put(qT)
        BP.put(kTs[0], kTs[1])
        BP.put(vtm)
        if sample:
            FP.put(rp[0], rp[1])
        mark(f'L{l}h{half} even:lru')
        lbdt = [BP.get(), BP.get()]
        for ri in range(2):
            C.dma(C.pool, lbdt[ri][0][:], d["lru_bd"][:, jl, ri, :], wt=lbdt[ri][1])
        for c in range(4):
            if c % 2 == 0:
                wL, wLk = load_w(d["ev_w_in"][jw, 0:1024, EV_LRU + c * 256:EV_LRU + c * 256 + 512], 8, 512)
            cb = (c % 2) * 256
            psx = proj_fm(wL, wLk, cb)
            xr = FP.get()
            for lt in range(2):
                C.actv(xr[0][:, lt * TT:(lt + 1) * TT], psx[lt][0][:], AF.Copy, [psx[lt][1]], [xr[1]])
            xc = FP.get()
            C.actv(xc[0][:], xr[0][:], AF.Identity, [xr[1], lcw_k, lcb_k], [xc[1]], bias=lcb[:, jsb, c:c + 1],
                   scale=lcw[:, jsb, 2, c:c + 1])
            xrv = xr[0][:].rearrange("p (s t) -> p s t", s=nseq)
            xcv = xc[0][:].rearrange("p (s t) -> p s t", s=nseq)
            for tap, off in ((0, -2), (1, -1), (3, 1)):
                if off < 0:
                    o_sl, i_sl = slice(-off, Ls), slice(0, Ls + off)
                else:
                    o_sl, i_sl = slice(0, Ls - off), slice(off, Ls)
                C.stt(xcv[:, :, o_sl], xrv[:, :, i_sl], lcw[:, jsb, tap, c:c + 1], xcv[:, :, o_sl], ALU.mult, ALU.add,
                      [xr[1], xc[1], lcw_k], [xc[1]])
            FP.put(xr)
            xcb = BP.get()
            C.actv(xcb[0][:], xc[0][:], AF.Copy, [xc[1]], [xcb[1]])
            R, G, A = {}, {}, {}
            for dr in range(2):
                gr = []
                for ri in range(2):
                    st_ = FP.get()
                    bias_t, bias_k = ((lbr, lbr_k), (lbi, lbi_k))[ri]
                    mi = (dr * 4 + c) * 128
                    for lt in range(2):
                        pb, pk = C.bank()
                        C.mm(pb[:], lbdt[ri][0][:, mi:mi + 128], xcb[0][:, lt * TT:(lt + 1) * TT], True, True,
                             [lbdt[ri][1], xcb[1]], [pk])
                        C.actv(st_[0][:, lt * TT:(lt + 1) * TT], pb[:], AF.Sigmoid, [pk, bias_k], [st_[1]],
                               bias=bias_t[:, jsc, dr, c:c + 1])
                    gr.append(st_)
                R[dr], G[dr] = gr
                A[dr] = FP.get()
            for dr in range(2):
                C.tt(G[dr][0][:], G[dr][0][:], xc[0][:], ALU.mult, [G[dr][1], xc[1]], [G[dr][1]])
            for dr in range(2):
                C.actv(R[dr][0][:], R[dr][0][:], AF.Identity, [R[dr][1], lcstk, zero_k], [R[dr][1]], bias=zerot[:, 0:1],
                       scale=lcst[:, jsc2, dr, c:c + 1])
            for dr in range(2):
                C.actv(A[dr][0][:], R[dr][0][:], AF.Exp, [R[dr][1]], [A[dr][1]])
            for dr in range(2):
                C.tt(R[dr][0][:], A[dr][0][:], A[dr][0][:], ALU.mult, [A[dr][1]], [R[dr][1]])
            for dr in range(2):
                C.actv(R[dr][0][:], R[dr][0][:], AF.Sqrt, [R[dr][1], one_k], [R[dr][1]], bias=onet[:, 0:1], scale=-1.0)
            for dr in range(2):
                C.tt(G[dr][0][:], G[dr][0][:], R[dr][0][:], ALU.mult, [G[dr][1], R[dr][1]], [G[dr][1]])
            for dr in range(2):
                r_, g_, a_ = R[dr], G[dr], A[dr]
                for s in range(nseq):
                    sl = slice(s * Ls, (s + 1) * Ls)
                    if sample:
                        init = h0[:, jsa, dr, c:c + 1]
                        rd = [a_[1], g_[1], h0_k]
                    else:
                        init = 0.0
                        rd = [a_[1], g_[1]]
                    if dr == 0:
                        C.op(C.dve, rd, [r_[1]], "tensor_tensor_scan", out=r_[0][:, sl], data0=a_[0][:, sl],
                             data1=g_[0][:, sl], initial=init, op0=ALU.mult, op1=ALU.add)
                    else:
                        C.op(C.dve, rd, [r_[1]], "tensor_tensor_scan", out=r_[0][:, sl][:, ::-1],
                             data0=a_[0][:, sl][:, ::-1], data1=g_[0][:, sl][:, ::-1], initial=init,
                             op0=ALU.mult, op1=ALU.add)
                if not sample:
                    rv = r_[0][:].rearrange("p (s t) -> p s t", s=4)
                    col = Ls - 1 if dr == 0 else 0
                    C.cp(lst[:, jo, dr, c, :], rv[:, :, col], [r_[1]], [lstk])
            hsum = R[0]
            C.tt(hsum[0][:], R[0][0][:], R[1][0][:], ALU.add, [R[0][1], R[1][1]], [hsum[1]])
            FP.put(R[1], G[0], G[1], A[0], A[1])
            FP.put(xc)
            BP.put(xcb)
            psg = proj_fm(wL, wLk, cb + 128)
            gg = FP.get()
            for lt in range(2):
                C.actv(gg[0][:, lt * TT:(lt + 1) * TT], psg[lt][0][:], AF.Gelu, [psg[lt][1]], [gg[1]])
            C.tt(mixb[:, 4 + c, :], hsum[0][:], gg[0][:], ALU.mult, [hsum[1], gg[1]], [mixk[4 + c]])
            FP.put(hsum, gg)
        BP.put(lbdt[0], lbdt[1])
        w_out_phase(d["ev_w_out"], jw, half)

    def even_mixer_sample(l):
        j = l // 2
        half = 1
        mark(f'L{l}h1 even:attn')
        scale = 0.125
        Ls = 1024
        wKV, wKVk = load_w(d["s_kv_w"][j, 0:1024, 0:320], 8, 320)
        wS, wSk = load_w(d["s_ev_w"][j, 0:1024, 0:512], 8, 512)
        vtm = BP.get()
        vt = vtm[0][:].rearrange("p (a b) -> p a b", a=8)
        for tl in range(8):
            pb, pk = C.bank()
            for k in range(8):
                C.mm(pb[:, 0:64], hbuf[:, k, tl * 128:(tl + 1) * 128], wKV[:, k, 256:320], k == 0, k == 7,
                     [wKVk, hk[tl // 4]], [pk])
            C.actv(vt[:, tl, 0:64], pb[:, 0:64], AF.Copy, [pk], [vtm[1]])
        rp = [FP.get(), FP.get()]
        for i_ in range(2):
            C.dma(C.sp, rp[i_][0][:], d["rope"][:, i_, :], wt=rp[i_][1])

        def roped(dst, ps, pp):
            for lt in range(2):
                sl = slice(lt * TT, (lt + 1) * TT)
                s1 = FP.get()
                C.tt(s1[0][:, 0:TT], ps[lt][0][:], rp[0][0][:, sl], ALU.mult, [ps[lt][1], rp[0][1]], [s1[1]])
                C.tt(s1[0][:, TT:2 * TT], pp[lt][0][:], rp[1][0][:, sl], ALU.mult, [pp[lt][1], rp[1][1]], [s1[1]])
                C.tt(dst[0][:, sl], s1[0][:, 0:TT], s1[0][:, TT:2 * TT], ALU.add, [s1[1]], [dst[1]])
                FP.put(s1)

        kT = BP.get()
        roped(kT, proj_fm(wKV, wKVk, 0), proj_fm(wKV, wKVk, 128))
        qT = BP.get()
        roped(qT, proj_fm(wS, wSk, 0), proj_fm(wS, wSk, 128))
        FP.put(rp[0], rp[1])
        vcol = slice(0, 64)
        for hq in range(2):
            prow = slice(hq * 64, hq * 64 + 64)
            for qh in range(2):
                el = [BP.get(), BP.get()]
                ec = BP.get()
                kts = [kt for kt in range(4 * qh - 1, 4 * qh + 5) if 0 <= kt <= 7]
                einfo = {}
                esi, eoff = 0, 0
                for kt in kts:
                    q0 = max(kt - 1, 4 * qh)
                    q1 = min(kt + 1, 4 * qh + 3)
                    nq = (q1 - q0 + 1) * 128
                    m0 = (q0 - (kt - 1)) * 128
                    if eoff + nq > 1024:
                        esi += 1
                        eoff = 0
                    pb, pk = C.bank()
                    C.mm(pb[:, 0:nq], kT[0][prow, kt * 128:(kt + 1) * 128], qT[0][prow, q0 * 128:q0 * 128 + nq],
                         True, True, [kT[1], qT[1]], [pk])
                    eh = el[esi]
                    eo = eoff
                    C.actv(eh[0][:, eo:eo + nq], pb[:, 0:nq], AF.Exp, [pk], [eh[1]], scale=scale)
                    C.tt(eh[0][:, eo:eo + nq], eh[0][:, eo:eo + nq], amask[:, m0:m0 + nq], ALU.mult,
                         [eh[1], amask_k], [eh[1]])
                    einfo[kt] = (eh, eo, q0, nq)
                    eoff += nq
                for ct in range(2):
                    pb, pk = C.bank()
                    C.mm(pb[:], kctx[prow, j, ct * 128:(ct + 1) * 128], qT[0][prow, qh * 512:(qh + 1) * 512],
                         True, True, [kctx_k, qT[1]], [pk])
                    C.actv(ec[0][:, ct * 512:(ct + 1) * 512], pb[:], AF.Exp, [pk], [ec[1]], scale=scale)
                pn, pnk = C.bank()
                pd, pdk = C.bank()
                srcs = []
                for ct in range(2):
                    srcs.append((vctx[:, j, ct, :], ec[0][:, ct * 512:(ct + 1) * 512], vctx_k, ec[1], 0, 512))
                for kt in kts:
                    eh, eo, q0, nq = einfo[kt]
                    srcs.append((vt[:, kt, vcol], eh[0][:, eo:eo + nq], vtm[1], eh[1], (q0 - 4 * qh) * 128, nq))
                for si, (vv, ee, vk_, ek_, c0, cn) in enumerate(srcs):
                    C.mm(pn[prow, c0:c0 + cn], vv, ee, si == 0, si == len(srcs) - 1, [vk_, ek_], [pnk])
                for si, (vv, ee, vk_, ek_, c0, cn) in enumerate(srcs):
                    C.mm(pd[prow, c0:c0 + cn], ones_b[:, 0:64], ee, si == 0, si == len(srcs) - 1, [ones_k, ek_], [pdk])
                BP.put(el[0], el[1], ec)
                s1 = FP.get()
                C.actv(s1[0][prow, 0:512], pd[prow, :], AF.Ln, [pdk, ssinkEk], [s1[1]], bias=ssinkE[prow, j:j + 1])
                C.actv(s1[0][prow, 0:512], s1[0][prow, 0:512], AF.Exp, [s1[1]], [s1[1]], scale=-1.0)
                C.tt(mixb[prow, 0, qh * 512:(qh + 1) * 512], pn[prow, :], s1[0][prow, 0:512], ALU.mult,
                     [pnk, s1[1]], [mixk[0]])
                FP.put(s1)
        BP.put(qT, kT, vtm)
        mark(f'L{l}h1 even:lru')
        lbdt = [BP.get(), BP.get()]
        for ri in range(2):
            C.dma(C.pool, lbdt[ri][0][:, 0:256], d["s_lbd"][:, j, ri, :], wt=lbdt[ri][1])
        psx = proj_fm(wS, wSk, 256)
        xr = FP.get()
        for lt in range(2):
            C.actv(xr[0][:, lt * TT:(lt + 1) * TT], psx[lt][0][:], AF.Copy, [psx[lt][1]], [xr[1]])
        xc = FP.get()
        C.actv(xc[0][:], xr[0][:], AF.Identity, [xr[1], slcw_k, slcb_k], [xc[1]], bias=slcb[:, j:j + 1],
               scale=slcw[:, j, 2:3])
        for tap, off in ((0, -2), (1, -1), (3, 1)):
            if off < 0:
                o_sl, i_sl = slice(-off, Ls), slice(0, Ls + off)
            else:
                o_sl, i_sl = slice(0, Ls - off), slice(off, Ls)
            C.stt(xc[0][:, o_sl], xr[0][:, i_sl], slcw[:, j, tap:tap + 1], xc[0][:, o_sl], ALU.mult, ALU.add,
                  [xr[1], xc[1], slcw_k], [xc[1]])
        FP.put(xr)
        xcb = BP.get()
        C.actv(xcb[0][:], xc[0][:], AF.Copy, [xc[1]], [xcb[1]])
        R, G, A = {}, {}, {}
        for dr in range(2):
            gr = []
            for ri in range(2):
                st_ = FP.get()
                bias_t, bias_k = ((slbr, slbr_k), (slbi, slbi_k))[ri]
                for lt in range(2):
                    pb, pk = C.bank()
                    C.mm(pb[:], lbdt[ri][0][:, dr * 128:(dr + 1) * 128], xcb[0][:, lt * TT:(lt + 1) * TT], True, True,
                         [lbdt[ri][1], xcb[1]], [pk])
                    C.actv(st_[0][:, lt * TT:(lt + 1) * TT], pb[:], AF.Sigmoid, [pk, bias_k], [st_[1]],
                           bias=bias_t[:, j, dr:dr + 1])
                gr.append(st_)
            R[dr], G[dr] = gr
            A[dr] = FP.get()
        for dr in range(2):
            C.tt(G[dr][0][:], G[dr][0][:], xc[0][:], ALU.mult, [G[dr][1], xc[1]], [G[dr][1]])
        for dr in range(2):
            C.actv(R[dr][0][:], R[dr][0][:], AF.Identity, [R[dr][1], slcstk, zero_k], [R[dr][1]], bias=zerot[:, 0:1],
                   scale=slcst[:, j, dr:dr + 1])
        for dr in range(2):
            C.actv(A[dr][0][:], R[dr][0][:], AF.Exp, [R[dr][1]], [A[dr][1]])
        for dr in range(2):
            C.tt(R[dr][0][:], A[dr][0][:], A[dr][0][:], ALU.mult, [A[dr][1]], [R[dr][1]])
        for dr in range(2):
            C.actv(R[dr][0][:], R[dr][0][:], AF.Sqrt, [R[dr][1], one_k], [R[dr][1]], bias=onet[:, 0:1], scale=-1.0)
        for dr in range(2):
            C.tt(G[dr][0][:], G[dr][0][:], R[dr][0][:], ALU.mult, [G[dr][1], R[dr][1]], [G[dr][1]])
        for dr in range(2):
            r_, g_, a_ = R[dr], G[dr], A[dr]
            rd = [a_[1], g_[1], sh0_k]
            init = sh0[:, j, dr:dr + 1]
            if dr == 0:
                C.op(C.dve, rd, [r_[1]], "tensor_tensor_scan", out=r_[0][:], data0=a_[0][:], data1=g_[0][:],
                     initial=init, op0=ALU.mult, op1=ALU.add)
            else:
                C.op(C.dve, rd, [r_[1]], "tensor_tensor_scan", out=r_[0][:, ::-1], data0=a_[0][:, ::-1],
                     data1=g_[0][:, ::-1], initial=init, op0=ALU.mult, op1=ALU.add)
        hsum = R[0]
        C.tt(hsum[0][:], R[0][0][:], R[1][0][:], ALU.add, [R[0][1], R[1][1]], [hsum[1]])
        FP.put(R[1], G[0], G[1], A[0], A[1], xc)
        BP.put(xcb)
        psg = proj_fm(wS, wSk, 384)
        gg = FP.get()
        for lt in range(2):
            C.actv(gg[0][:, lt * TT:(lt + 1) * TT], psg[lt][0][:], AF.Gelu, [psg[lt][1]], [gg[1]])
        C.tt(mixb[:, 4, :], hsum[0][:], gg[0][:], ALU.mult, [hsum[1], gg[1]], [mixk[4]])
        FP.put(hsum, gg)
        BP.put(lbdt[0], lbdt[1])
        exchange_start()

    hid = {L: (sb(f"hid{L}", [64, L], BF16), Tk(f"hid{L}")) for L in (256, 1024)}
    fb = sb("fb", [64, 2], F32)
    fbk = Tk("fb")

    def sin_reduced(dst_ap, dstk, pb_ap, pk, n_, scale_ap, bias_ap, extra):
        a1 = FP.get()
        rs = slice(0, 64)
        A = a1[0][rs, 0:n_]
        Bv = a1[0][rs, 512:512 + n_]
        C.ts(A, pb_ap, scale_ap, bias_ap, ALU.mult, ALU.add, [pk] + extra, [a1[1]])
        C.ts(Bv, A, 1.0 / TWO_PI, MAGIC, ALU.mult, ALU.add, [a1[1]], [a1[1]])
        C.ts(Bv, Bv, MAGIC, TWO_PI, ALU.subtract, ALU.mult, [a1[1]], [a1[1]])
        C.tt(A, A, Bv, ALU.subtract, [a1[1]], [a1[1]])
        C.ts(A, A, 3.1415925, -3.1415925, ALU.min, ALU.max, [a1[1]], [a1[1]])
        C.actv(dst_ap, A, AF.Sin, [a1[1]], [dstk])
        FP.put(a1)

    def hyena_prep(l):
        j = l // 2
        C.tt(fb[:], hfreq[:, j, :], hb12[:, j, :], ALU.mult, [hfreq_k, hb12_k], [fbk])
        for L in (256, 1024):
            ht, htk = hid[L]
            zf = FP.get()
            C.dma(C.sp, zf[0][0:33, 0:L], d[f"zf_{L}"], wt=zf[1])
            h1 = FP.get()
            for o in range(0, L, 512):
                n_ = min(512, L - o)
                pb, pk = C.bank()
                C.mm(pb[0:64, 0:n_], hw1[:, j, :], zf[0][0:33, o:o + n_], True, True, [hw1_k, zf[1]], [pk])
                sin_reduced(h1[0][0:64, o:o + n_], h1[1], pb[0:64, 0:n_], pk, n_, hfreq[:, j, 0:1], fb[:, 0:1], [hfreq_k, fbk])
            for o in range(0, L, 512):
                n_ = min(512, L - o)
                pb, pk = C.bank()
                C.mm(pb[0:64, 0:n_], hw2[:, j, :], h1[0][0:64, o:o + n_], True, True, [hw2_k, h1[1]], [pk])
                sin_reduced(ht[:, o:o + n_], htk, pb[0:64, 0:n_], pk, n_, hfreq[:, j, 1:2], fb[:, 1:2], [hfreq_k, fbk])
            FP.put(zf, h1)

    def hyena_filter(l, c, n, L, tab, tabk_, kr, ki):
        j = l // 2
        nt_ = L // 128
        ht, htk = hid[L]
        ph, phk = hyph[L]
        nt, ntk = ntn[L]
        col0 = (c * 2 + n) * 256
        if L == 1024:
            w3_src = d["s_hw3"][:, j, n * 256:(n + 1) * 256]
            ld_src = d["s_hld"][:, j * 512 + n * 256:j * 512 + (n + 1) * 256]
        else:
            w3_src = d["hy_w3"][:, j, col0:col0 + 256]
            ld_src = d["hy_ld"][:, j * 2048 + col0:j * 2048 + col0 + 256]
        ks, kd = BP.get(), BP.get()
        ksv = ks[0][:].rearrange("p (a b) -> p a b", a=8)
        kdv = kd[0][:].rearrange("p (a b) -> p a b", a=8)
        w3 = BP.get()
        C.dma(C.pool, w3[0][0:64, 0:256], w3_src, wt=w3[1])
        eld = FP.get()
        C.dma(C.sp, eld[0][:, 0:256], ld_src.partition_broadcast(128), wt=eld[1])
        C.actv(eld[0][:, 0:256], eld[0][:, 0:256], AF.Exp, [eld[1]], [eld[1]])
        pss_b = C.bank(hold=True)
        pss, pssk = pss_b
        dec = FP.get()
        sq = BP.get()
        for tl in range(nt_):
            o = (tl % 4) * 256
            pb, pk = C.bank()
            C.mm(pb[:, 0:256], ht[:, tl * 128:(tl + 1) * 128], w3[0][0:64, 0:256], True, True, [htk, w3[1]], [pk])
            C.actv(dec[0][:, o:o + 256], eld[0][:, 0:256], AF.Exp, [eld[1], ntk], [dec[1]], scale=nt[:, tl:tl + 1])
            C.tt(dec[0][:, o:o + 256], pb[:, 0:256], dec[0][:, o:o + 256], ALU.mult, [pk, dec[1]], [dec[1]])
            C.actv(sq[0][:, o:o + 256], dec[0][:, o:o + 256], AF.Square, [dec[1]], [sq[1]])
            C.mm(pss[:, 0:256], ones_b[:], sq[0][:, o:o + 256], tl == 0, tl == nt_ - 1, [ones_k, sq[1]], [pssk])
            C.tt(ksv[:, tl, :], dec[0][:, o:o + 128], dec[0][:, o + 128:o + 256], ALU.add, [dec[1]], [ks[1]])
            C.tt(kdv[:, tl, :], dec[0][:, o:o + 128], dec[0][:, o + 128:o + 256], ALU.subtract, [dec[1]], [kd[1]])
        BP.put(sq, w3)
        FP.put(eld)
        n1 = dec
        C.cp(n1[0][:, 0:128], pss[:, 0:128], [pssk], [n1[1]])
        C.tt(n1[0][:, 0:128], n1[0][:, 0:128], pss[:, 128:256], ALU.add, [n1[1], pssk], [n1[1]])
        C.actv(n1[0][:, 0:128], n1[0][:, 0:128], AF.Sqrt, [n1[1], eps_k], [n1[1]], bias=epst[:, 0:1], scale=1.0)
        C.recip(n1[0][:, 128:256], n1[0][:, 0:128], [n1[1]], [n1[1]])
        nrm = n1[0][:, 128:256]
        C.release(pss_b)
        krv = kr[0][:].rearrange("p (a b) -> p a b", a=8)
        kiv = ki[0][:].rearrange("p (a b) -> p a b", a=8)
        for ft in range(nt_):
            pg, pgk = C.bank()
            for tl in range(nt_):
                ta, tak = tab.get(0, tl, ft * 128, (ft + 1) * 128)
                C.mm(pg[:, 0:128], ta, ksv[:, tl, :], tl == 0, tl == nt_ - 1, [tak, ks[1]], [pgk])
            for tl in range(nt_):
                ta, tak = tab.get(1, tl, ft * 128, (ft + 1) * 128)
                C.mm(pg[:, 128:256], ta, kdv[:, tl, :], tl == 0, tl == nt_ - 1, [tak, kd[1]], [pgk])
            ca = ph[:, 0, ft:ft + 1]
            sa = ph[:, 1, ft:ft + 1]
            g1 = n1[0][:, 256:512]
            C.ts(g1[:, 0:128], pg[:, 128:256], sa, None, ALU.mult, None, [pgk, phk], [n1[1]])
            C.stt(g1[:, 0:128], pg[:, 0:128], ca, g1[:, 0:128], ALU.mult, ALU.add, [pgk, phk, n1[1]], [n1[1]])
            C.ts(g1[:, 128:256], pg[:, 128:256], ca, None, ALU.mult, None, [pgk, phk], [n1[1]])
            C.stt(g1[:, 128:256], pg[:, 0:128], sa, g1[:, 128:256], ALU.mult, ALU.subtract, [pgk, phk, n1[1]], [n1[1]])
            if L == 256:
                for s_ in range(4):
                    C.stt(krv[:, ft * 4 + s_, :], g1[:, 0:128], 1.0 / L, nrm, ALU.mult, ALU.mult, [n1[1]], [kr[1]])
                    C.stt(kiv[:, ft * 4 + s_, :], g1[:, 128:256], 1.0 / L, nrm, ALU.mult, ALU.mult, [n1[1]], [ki[1]])
            else:
                C.stt(krv[:, ft, :], g1[:, 0:128], 1.0 / L, nrm, ALU.mult, ALU.mult, [n1[1]], [kr[1]])
                C.stt(kiv[:, ft, :], g1[:, 128:256], 1.0 / L, nrm, ALU.mult, ALU.mult, [n1[1]], [ki[1]])
        FP.put(dec)
        BP.put(ks, kd)

    def hyena_conv(zsrc, o0, L, tab, tabk_, kr, ki, cb):
        nt_ = L // 128
        krv = kr[0][:].rearrange("p (a b) -> p a b", a=8)
        kiv = ki[0][:].rearrange("p (a b) -> p a b", a=8)
        ztm = BP.get()
        ztv = ztm[0][:].rearrange("p (a b) -> p a b", a=8)
        for g0 in range(0, nt_, 4):
            gn = min(4, nt_ - g0)
            pb, pk = C.bank()
            for tl in range(g0, g0 + gn):
                C.op(C.pe, [zsrc[1], ident_k], [pk], "transpose", out=pb[:, (tl - g0) * 128:(tl - g0 + 1) * 128],
                     in_=zsrc[0][:, o0 + tl * 128:o0 + (tl + 1) * 128], identity=ident[:])
            C.actv(ztv[:, g0:g0 + gn, :], pb[:, 0:gn * 128].rearrange("p (a b) -> p a b", a=gn), AF.Copy, [pk], [ztm[1]])
        yr, yi = BP.get(), BP.get()
        yrv = yr[0][:].rearrange("p (a b) -> p a b", a=8)
        yiv = yi[0][:].rearrange("p (a b) -> p a b", a=8)
        tq = FP.get()
        for g0 in range(0, nt_, 2):
            gn = min(2, nt_ - g0)
            pz, pzk = C.bank()
            for fi in range(gn):
                ft = g0 + fi
                for cs_ in range(2):
                    for tl in range(nt_):
                        ta, tak = tab.get(cs_, tl, ft * 128, (ft + 1) * 128)
                        C.mm(pz[:, fi * 256 + cs_ * 128:fi * 256 + (cs_ + 1) * 128],
                             ta, ztv[:, tl, :], tl == 0, tl == nt_ - 1, [tak, ztm[1]], [pzk])
            pzv = pz[:, 0:gn * 256].rearrange("p (a b c) -> p a b c", a=gn, b=2)
            zc = pzv[:, :, 0, :]
            zs = pzv[:, :, 1, :]
            kr_ = krv[:, g0:g0 + gn, :]
            ki_ = kiv[:, g0:g0 + gn, :]
            ob = ((g0 // 2) % 2) * 512
            v1 = tq[0][:, ob:ob + gn * 128].rearrange("p (a b) -> p a b", a=gn)
            v2 = tq[0][:, ob + 256:ob + 256 + gn * 128].rearrange("p (a b) -> p a b", a=gn)
            C.tt(v1, zc, kr_, ALU.mult, [pzk, kr[1]], [tq[1]])
            C.tt(v2, zs, ki_, ALU.mult, [pzk, ki[1]], [tq[1]])
            C.tt(yrv[:, g0:g0 + gn, :], v1, v2, ALU.add, [tq[1]], [yr[1]])
            C.tt(v1, zs, kr_, ALU.mult, [pzk, kr[1]], [tq[1]])
            C.tt(v2, zc, ki_, ALU.mult, [pzk, ki[1]], [tq[1]])
            C.tt(yiv[:, g0:g0 + gn, :], v1, v2, ALU.subtract, [tq[1]], [yi[1]])
        FP.put(tq)
        BP.put(ztm)
        for th in range(0, L, 512):
            wd = min(512, L - th)
            pb, pk = C.bank()
            for ft in range(nt_):
                ta, tak = tab.get(0, ft, th, th + wd)
                C.mm(pb[:, 0:wd], yrv[:, ft, :], ta, ft == 0, False, [yr[1], tak], [pk])
                ta, tak = tab.get(1, ft, th, th + wd)
                C.mm(pb[:, 0:wd], yiv[:, ft, :], ta, False, ft == nt_ - 1, [yi[1], tak], [pk])
            cb(th, pb, pk, wd)
        BP.put(yr, yi)

    def hyena_conv_p(zsrc, tab, kr, ki, cb):
        L = 256
        krv = kr[0][:].rearrange("p (a b) -> p a b", a=8)
        kiv = ki[0][:].rearrange("p (a b) -> p a b", a=8)
        ztm = BP.get()
        ztv = ztm[0][:].rearrange("p (t b) -> p t b", t=2)
        for tl in range(2):
            pb, pk = C.bank()
            for s_ in range(4):
                C.op(C.pe, [zsrc[1], ident_k], [pk], "transpose", out=pb[:, s_ * 128:(s_ + 1) * 128],
                     in_=zsrc[0][:, s_ * L + tl * 128:s_ * L + (tl + 1) * 128], identity=ident[:])
            C.actv(ztv[:, tl, :], pb[:], AF.Copy, [pk], [ztm[1]])
        yr, yi = BP.get(), BP.get()
        yrv = yr[0][:].rearrange("p (a b) -> p a b", a=8)
        yiv = yi[0][:].rearrange("p (a b) -> p a b", a=8)
        tq = FP.get()
        for ft in range(2):
            pzs = []
            for cs_ in range(2):
                pz, pzk = C.bank()
                for tl in range(2):
                    ta, tak = tab.get(cs_, tl, ft * 128, (ft + 1) * 128)
                    C.mm(pz[:], ta, ztv[:, tl, :], tl == 0, tl == 1, [tak, ztm[1]], [pzk])
                pzs.append((pz, pzk))
            (zc, zck), (zs, zsk) = pzs
            kr_ = kr[0][:, ft * 512:(ft + 1) * 512]
            ki_ = ki[0][:, ft * 512:(ft + 1) * 512]
            v1 = tq[0][:, 0:512]
            v2 = tq[0][:, 512:1024]
            C.tt(v1, zc[:], kr_, ALU.mult, [zck, kr[1]], [tq[1]])
            C.tt(v2, zs[:], ki_, ALU.mult, [zsk, ki[1]], [tq[1]])
            C.tt(yr[0][:, ft * 512:(ft + 1) * 512], v1, v2, ALU.add, [tq[1]], [yr[1]])
            C.tt(v1, zs[:], kr_, ALU.mult, [zsk, kr[1]], [tq[1]])
            C.tt(v2, zc[:], ki_, ALU.mult, [zck, ki[1]], [tq[1]])
            C.tt(yi[0][:, ft * 512:(ft + 1) * 512], v1, v2, ALU.subtract, [tq[1]], [yi[1]])
        FP.put(tq)
        BP.put(ztm)
        for pair in range(2):
            pb, pk = C.bank()
            for si in range(2):
                s_ = pair * 2 + si
                for ft in range(2):
                    ta, tak = tab.get(0, ft, 0, L)
                    C.mm(pb[:, si * L:(si + 1) * L], yrv[:, ft * 4 + s_, :], ta, ft == 0, False, [yr[1], tak], [pk])
                    ta, tak = tab.get(1, ft, 0, L)
                    C.mm(pb[:, si * L:(si + 1) * L], yiv[:, ft * 4 + s_, :], ta, False, ft == 1, [yi[1], tak], [pk])
            cb(pair * 512, pb, pk, 512)
        BP.put(yr, yi)

    def odd_mixer(l, half):
        j = l // 2
        sample = half == 1
        nseq = 1 if sample else 4
        L = 1024 if sample else 256
        nt_ = L // 128
        mark(f'L{l}h{half} odd:fnet')
        if sample:
            ring_allowed[0] = [0, 1]
        if sample:
            wA, wAk = load_w(d["s_od_w"][j, 0:1024, 0:512], 8, 512)
        else:
            wA, wAk = load_w(d["od_w_in"][j, 0:1024, 0:512], 8, 512)
        if sample:
            bigtab.load("fn_1024")
            ftab = bigtab
        else:
            ftab = SmallTab(fn256, fn256_k)
        sc_ = 1.0 / math.sqrt(128.0 * L)
        for g in ([0] if sample else range(4)):
            ps = proj_fm(wA, wAk, g * 128)
            fT = BP.get()
            for lt in range(2):
                C.actv(fT[0][:, lt * TT:(lt + 1) * TT], ps[lt][0][:], AF.Copy, [ps[lt][1]], [fT[1]])
            ab = [BP.get(), BP.get()]
            for s in range(nseq):
                for tl in range(nt_):
                    pb, pk = C.bank()
                    C.mm(pb[:, 0:256], fT[0][:, s * L + tl * 128:s * L + (tl + 1) * 128], fnch[:], True, True,
                         [fT[1], fnch_k], [pk])
                    dst = ab[tl // 4][0][:, (tl % 4) * 256:(tl % 4 + 1) * 256]
                    if tl % 2 == 0:
                        C.actv(dst, pb[:, 0:256], AF.Copy, [pk], [ab[tl // 4][1]])
                    else:
                        C.cp(dst, pb[:, 0:256], [pk], [ab[tl // 4][1]])
                for th in range(0, L, 512):
                    wd = min(512, L - th)
                    pb, pk = C.bank()
                    for tl in range(nt_):
                        for cs_ in range(2):
                            o = (tl % 4) * 256 + cs_ * 128
                            ta, tak = ftab.get(cs_, tl, th, th + wd)
                            C.mm(pb[:, 0:wd], ab[tl // 4][0][:, o:o + 128], ta,
                                 tl == 0 and cs_ == 0, tl == nt_ - 1 and cs_ == 1, [ab[tl // 4][1], tak], [pk])
                    C.actv(mixb[:, g, s * L + th:s * L + th + wd], pb[:, 0:wd], AF.Copy, [pk], [mixk[g]], scale=sc_)
            BP.put(fT, ab[0], ab[1])
        mark(f'L{l}h{half} odd:hyena')
        if sample:
            bigtab.load("hy_1024")
            htab, htabk = bigtab, None
        else:
            htab, htabk = SmallTab(hy256, hy256_k), None
        for c in ([0] if sample else range(4)):
            mark(f'hy c{c} filt')
            K = [(FP.get(), FP.get()) for _ in range(2)]
            for n in range(2):
                hyena_filter(l, c, n, L, htab, htabk, K[n][0], K[n][1])
            if sample:
                wB, wBk, wbase = wA, wAk, 128
            elif c == 0:
                wB, wBk = load_w(d["od_w_in"][j, 0:1024, 512:1024], 8, 512)
                wbase = 0
            elif c == 1:
                wB, wBk = load_w(d["od_w_in"][j, 0:1024, 512 + 384:512 + 768], 8, 384)
                wbase = 0
            elif c == 2:
                wB, wBk = load_w(d["od_w_in"][j, 0:1024, 512 + 768:512 + 1152], 8, 384)
                wbase = 0
            else:
                wB, wBk = load_w(d["od_w_in"][j, 0:1024, 512 + 1152:512 + 1536], 8, 384)
                wbase = 0
            mark(f'hy c{c} proj')
            sig = []
            for si in range(3):
                ps = proj_fm(wB, wBk, wbase + si * 128)
                raw = FP.get()
                for lt in range(2):
                    C.actv(raw[0][:, lt * TT:(lt + 1) * TT], ps[lt][0][:], AF.Copy, [ps[lt][1]], [raw[1]])
                cv = FP.get()
                ch = si * 4 + c
                if sample:
                    cw_t, cw_k, cb_t, cb_k, ch = shcw, shcw_k, shcb, shcb_k, si
                else:
                    cw_t, cw_k, cb_t, cb_k = hcw, hcw_k, hcb, hcb_k
                C.actv(cv[0][:], raw[0][:], AF.Identity, [raw[1], cw_k, cb_k], [cv[1]], bias=cb_t[:, j, ch:ch + 1],
                       scale=cw_t[:, j, 1, ch:ch + 1])
                rv = raw[0][:].rearrange("p (s t) -> p s t", s=nseq)
                cvv = cv[0][:].rearrange("p (s t) -> p s t", s=nseq)
                for tap, off in ((0, -1), (2, 1)):
                    if off < 0:
                        o_sl, i_sl = slice(-off, L), slice(0, L + off)
                    else:
                        o_sl, i_sl = slice(0, L - off), slice(off, L)
                    C.stt(cvv[:, :, o_sl], rv[:, :, i_sl], cw_t[:, j, tap, ch:ch + 1], cvv[:, :, o_sl], ALU.mult, ALU.add,
                          [raw[1], cv[1], cw_k], [cv[1]])
                FP.put(raw)
                sig.append(cv)
            zv, x1, x2 = sig
            mark(f'hy c{c} conv')
            for n, gate in enumerate((x1, x2)):
                for s in range(nseq if sample else 1):
                    def cb(th, pb, pk, wd, s=s, n=n, gate=gate):
                        sl = slice(s * L + th, s * L + th + wd)
                        tmp = FP.get()
                        hb_ap, hb_k = (shbias[:, j, n:n + 1], shbias_k) if sample else (hbias[:, j, n, c:c + 1], hbias_k)
                        C.stt(tmp[0][:, 0:wd], zv[0][:, sl], hb_ap, pb[:, 0:wd], ALU.mult, ALU.add,
                              [zv[1], hb_k, pk], [tmp[1]])
                        if n == 0:
                            C.tt(zv[0][:, sl], tmp[0][:, 0:wd], gate[0][:, sl], ALU.mult, [tmp[1], gate[1]], [zv[1]])
                        else:
                            C.tt(mixb[:, 4 + c, sl], tmp[0][:, 0:wd], gate[0][:, sl], ALU.mult, [tmp[1], gate[1]],
                                 [mixk[4 + c]])
                        FP.put(tmp)
                    if sample:
                        hyena_conv(zv, s * L, L, htab, htabk, K[n][0], K[n][1], cb)
                    else:
                        hyena_conv_p(zv, htab, K[n][0], K[n][1], cb)
            FP.put(zv, x1, x2, K[0][0], K[0][1], K[1][0], K[1][1])
        ring_allowed[0] = list(range(RING_N))
        if sample:
            exchange_start()
        else:
            w_out_phase(d["od_w_out"], j, half)

    def mlp_sample(l, prefetch_fn=None):
        mark(f'L{l} mlp sample')
        par = l % 2
        for lt in (2, 3):
            norm_mod(lt, lt, 1, 1, 24)
        mark('mlp sample')
        w1s = [load_w(d["s_w1"][l, 0:1024, jb * 512:(jb + 1) * 512], 8, 512) for jb in range(2)]
        w2s = [load_w(d["s_w2"][l, jb * 512:(jb + 1) * 512, :], 4, 1024) for jb in range(2)]
        for lt in (2, 3):
            hs = [BP.get() for _ in range(4)]
            for hc8 in range(8):
                jb, hc = hc8 // 4, hc8 % 4
                w1, w1k = w1s[jb]
                pb, pk = C.bank()
                for k in range(8):
                    C.mm(pb[:], w1[:, k, hc * 128:(hc + 1) * 128], hbuf[:, k, lt * TT:(lt + 1) * TT], k == 0, k == 7,
                         [w1k] + hkl(lt), [pk])
                rl = FP.get()
                C.actv(rl[0][:, 0:TT], pb[:], AF.Relu, [pk], [rl[1]])
                hv = hs[hc8 // 2][0][:, (hc8 % 2) * TT:(hc8 % 2 + 1) * TT]
                if hc8 % 2 == 0:
                    C.tt(hv, rl[0][:, 0:TT], rl[0][:, 0:TT], ALU.mult, [rl[1]], [hs[hc8 // 2][1]])
                else:
                    C.actv(hv, rl[0][:, 0:TT], AF.Square, [rl[1]], [hs[hc8 // 2][1]])
                FP.put(rl)
            for n in range(8):
                pb, pk = C.bank()
                for hc8 in range(8):
                    jb, hc = hc8 // 4, hc8 % 4
                    w2, w2k = w2s[jb]
                    hv = hs[hc8 // 2][0][:, (hc8 % 2) * TT:(hc8 % 2 + 1) * TT]
                    C.mm(pb[:], w2[:, hc, n * 128:(n + 1) * 128], hv, hc8 == 0, hc8 == 7, [w2k, hs[hc8 // 2][1]], [pk])
                yst = FP.get()
                C.ts(yst[0][:, 0:TT], pb[:], modM_[par][:, 40 + n, 1:2], None, ALU.mult, None, [pk, modMk_[par]], [yst[1]])
                C.dma(C.sp, ar_src[n * 128:(n + 1) * 128, (lt - 2) * TT:(lt - 1) * TT], yst[0][:, 0:TT], wt=ars_k, rt=yst[1],
                      nowaw=True)
                FP.put(yst)
            BP.put(*hs)
        if prefetch_fn is not None:
            prefetch_fn()
        C.allgather(ar_src, ar_dst, GROUPS, rt=ars_k, wt=ard_k, kind="AllReduce", op=ALU.add)

    def mlp_prompt(l, side=None, jbs=range(8), do_norm=True, pre=None):
        mark(f'L{l} mlp prompt')
        par = l % 2
        if do_norm:
            for lt in range(2):
                norm_mod(lt, lt, 1, 0, 24)
        mark('mlp body')
        hb = [(BP.get(), BP.get()) for _ in range(2)]
        seq = [(jb, lt) for jb in jbs for lt in range(2)]
        wcache = dict(pre or {})

        def getw1(jb):
            if ("1", jb) not in wcache:
                wcache[("1", jb)] = load_w(d["mlp_w1"][l, 0:1024, jb * 512:(jb + 1) * 512], 8, 512)
            return wcache[("1", jb)]

        def getw2(jb):
            if ("2", jb) not in wcache:
                wcache[("2", jb)] = load_w(d["mlp_w2"][l, jb * 512:(jb + 1) * 512, :], 4, 1024)
            return wcache[("2", jb)]

        def hview(idx, hc):
            pair = hb[idx % 2]
            return pair[hc // 2][0][:, (hc % 2) * TT:(hc % 2 + 1) * TT], pair[hc // 2][1]

        def stageA(idx):
            jb, lt = seq[idx]
            w1, w1k = getw1(jb)
            for hc in range(4):
                pb, pk = C.bank()
                for k in range(8):
                    C.mm(pb[:], w1[:, k, hc * 128:(hc + 1) * 128], hbuf[:, k, lt * TT:(lt + 1) * TT], k == 0, k == 7,
                         [w1k] + hkl(lt), [pk])
                rl = FP.get()
                C.actv(rl[0][:, 0:TT], pb[:], AF.Relu, [pk], [rl[1]])
                hv, hvk = hview(idx, hc)
                if hc % 2 == 0:
                    C.tt(hv, rl[0][:, 0:TT], rl[0][:, 0:TT], ALU.mult, [rl[1]], [hvk])
                else:
                    C.actv(hv, rl[0][:, 0:TT], AF.Square, [rl[1]], [hvk])
                FP.put(rl)

        def stageB(idx):
            jb, lt = seq[idx]
            w2, w2k = getw2(jb)
            for n in range(8):
                pb, pk = C.bank()
                for k in range(4):
                    hv, hvk = hview(idx, k)
                    C.mm(pb[:], w2[:, k, n * 128:(n + 1) * 128], hv, k == 0, k == 3, [w2k, hvk], [pk])
                resid(pb, pk, n, lt, modM_[par][:, 40 + n, 0:1])

        stageA(0)
        for idx in range(len(seq)):
            if idx + 1 < len(seq):
                stageA(idx + 1)
            stageB(idx)
            if side is not None and idx % 2 == 1:
                next(side, None)
        if side is not None and not do_norm:
            for _ in side:
                pass
        for pair in hb:
            BP.put(pair[0], pair[1])

    def sample_add():
        mark('mlp sample add')
        for n in range(8):
            yb = FP.get()
            C.dma(C.sp if n % 2 else C.pool, yb[0][:], ar_dst[n * 128:(n + 1) * 128, :], wt=yb[1], rt=ard_k)
            for lt in (2, 3):
                C.tt(x[:, n, lt * TT:(lt + 1) * TT], yb[0][:, (lt - 2) * TT:(lt - 1) * TT], x[:, n, lt * TT:(lt + 1) * TT],
                     ALU.add, [yb[1], xk[n][lt]], [xk[n][lt]])
            FP.put(yb)

    lay = list(layers if layers is not None else range(depth))
    g0 = modulation_gen(lay[0], staged=True)
    for _ in range(4):
        next(g0)
    pending_add = False
    for li, l in enumerate(lay):
        cur["par"] = l % 2
        nxt = lay[li + 1] if li + 1 < len(lay) else None
        side = modulation_gen(nxt) if nxt is not None else None
        if li == 0:
            def chain(g0=g0, rest=side):
                for _ in g0:
                    yield
                if rest is not None:
                    for _ in rest:
                        yield
            side = chain()
        side_gen[0] = side
        if l % 2 == 1:
            hyena_prep(l)
        for lt in range(2):
            norm_mod(lt, lt, 0, 0, 0)
        if l % 2 == 0:
            even_mixer(l, 0)
        else:
            odd_mixer(l, 0)
        if pending_add:
            sample_add()
            pending_add = False
        for lt in range(2):
            norm_mod(2 + lt, lt, 0, 1, 0)
        if l % 2 == 0:
            even_mixer_sample(l)
        else:
            odd_mixer(l, 1)
        side_gen[0] = None
        if li == 0:
            for _ in g0:
                pass
        mlp_prompt(l, side, range(0, 2), True)
        exchange_finish()
        w_out_phase(d["ev_w_out"] if l % 2 == 0 else d["od_w_out"], l // 2, 1)
        pre = {}

        def pf(l=l, pre=pre):
            pre[("1", 2)] = load_w(d["mlp_w1"][l, 0:1024, 2 * 512:3 * 512], 8, 512)
            pre[("2", 2)] = load_w(d["mlp_w2"][l, 2 * 512:3 * 512, :], 4, 1024)
        mlp_sample(l, pf)
        mlp_prompt(l, side, range(2, 8), False, pre)
        sample_add()
        if debug:
            for c in range(8):
                for t in range(4):
                    C.dma(C.sp, o_dbg[l, :, c, t * TT:(t + 1) * TT], x[:, c, t * TT:(t + 1) * TT], rt=xk[c][t])
    if pending_add:
        sample_add()

    mark('final')
    for t in range(4):
        ts_ = slice(t * TT, (t + 1) * TT)
        rs = rstd_tile(t)
        for c in range(8):
            yst = FP.get()
            C.stt(yst[0][:, 0:TT], x[:, c, ts_], nfin[:, c:c + 1], rs[0][:, 0:TT], ALU.mult, ALU.mult,
                  [xk[c][t], nfin_k, rs[1]], [yst[1]])
            C.dma(C.sp, o_y[:, c, ts_], yst[0][:, 0:TT], rt=yst[1])
            FP.put(yst)
        FP.put(rs)
    C.dma(C.sp, o_lru, lst[:], rt=lstk)
    deps = {}
    for tok in C.out_dmas:
        C._add(deps, tok)
    C._wait(C.sp, deps)

    block = es.enter_context(nc.Block())

    @block.tensor
    def _(e):
        C.replay(C.pe, e)

    @block.scalar
    def _(e):
        C.replay(C.act, e)

    @block.vector
    def _(e):
        C.replay(C.dve, e)

    @block.gpsimd
    def _(e):
        C.replay(C.pool, e)

    @block.sync
    def _(e):
        C.replay(C.sp, e)

    build.stats = {n: len(e.prog) for n, e in (("pe", C.pe), ("act", C.act), ("dve", C.dve), ("pool", C.pool), ("sp", C.sp))}


_NC_CACHE = {}


def _get_nc(depth=DEPTH, debug=False, skip=(), layers=None):
    key = (depth, debug, tuple(skip), tuple(layers) if layers is not None else None)
    if key not in _NC_CACHE:
        _NC_CACHE[key] = build(depth, debug, skip, layers)
    return _NC_CACHE[key]


def run(inputs, depth=DEPTH, debug=False, skip=(), layers=None):
    shared = _prep_shared(inputs)
    in_maps = []
    for i in range(NCORES):
        m = dict(shared)
        m.update(_prep_core(inputs, i))
        in_maps.append(m)
    nc = _get_nc(depth, debug, skip, layers)
    res = run_bass_kernel_spmd(nc, in_maps, core_ids=list(range(NCORES)))
    return res.results


def kernel(**inputs):
    r = run(inputs)
    B_, S_ = 32, 256
    y_prompt = np.zeros((B_, S_, D), np.float32)
    y_sample = np.zeros((2, 1024, D), np.float32)
    k_state = np.zeros((B_, 2, S_, 2, 64), np.float32)
    v_state = np.zeros((B_, 2, S_, 2, 64), np.float32)
    lru_state = np.zeros((B_, 2, 2, 512), np.float32)
    for i in range(NCORES):
        yT = np.asarray(r[i]["yT"])
        Y = yT.transpose(2, 1, 0).reshape(NT, D)
        y_prompt[4 * i:4 * i + 4] = Y[:1024].reshape(4, 256, D)
        if i % 4 == 0:
            y_sample[i // 4] = Y[1024:]
        kv = np.asarray(r[i]["kv"]).reshape(4, 2, 256, 256)
        k_state[4 * i:4 * i + 4] = kv[..., 0:128].reshape(4, 2, 256, 2, 64)
        v_state[4 * i:4 * i + 4] = kv[..., 128:256].reshape(4, 2, 256, 2, 64)
        lt = np.asarray(r[i]["lruT"])
        lru_state[4 * i:4 * i + 4] = lt.transpose(4, 1, 2, 3, 0).reshape(4, 2, 2, 512)
    return (y_prompt, y_sample, k_state, v_state, lru_state)
```

```python
import math
from contextlib import ExitStack

import numpy as np
import ml_dtypes
import concourse.bass as bass
import concourse.mybir as mybir
from concourse.bass_utils import run_bass_kernel_spmd

F32 = mybir.dt.float32
BF16 = mybir.dt.bfloat16
AF = mybir.ActivationFunctionType
ALU = mybir.AluOpType

NCORES = 8
JOV = {}
D = 1024
NT = 2048
HALF = 1024
TT = 512
DEPTH = 4
EPS = 1e-6
MAGIC = 12582912.0
TWO_PI = 2.0 * math.pi
EV_Q, EV_KD, EV_KV, EV_LRU, EV_QP, EV_KDP = 0, 512, 768, 1024, 2048, 2560
EV_COLS = 2816
OD_COLS = 2048


class Tk:
    __slots__ = ("w", "r", "dsem", "dn", "name")

    def __init__(self, name=""):
        self.w = None
        self.r = {}
        self.dsem = None
        self.dn = 0
        self.name = name


class Eng:
    def __init__(self, name, sem):
        self.name, self.sem, self.n = name, sem, 0
        self.waited = {}
        self.prog = []


class Ctx:
    def __init__(self, nc, es):
        self.nc, self.es = nc, es
        self.sem_id = 0
        mk = lambda n: Eng(n, self.new_sem(n))
        self.pe, self.act, self.dve, self.pool, self.sp = mk("pe"), mk("act"), mk("dve"), mk("pool"), mk("sp")
        self.banks = []
        self.bank_i = 0
        self.held = set()
        self.out_dmas = []

    def new_sem(self, name):
        self.sem_id += 1
        return self.es.enter_context(self.nc.semaphore(f"s{self.sem_id}_{name}"))

    def sb(self, name, shape, dt):
        return self.es.enter_context(self.nc.sbuf_tensor("sb_" + name, shape, dt))

    def _wait(self, eng, deps):
        for sem, val in deps.values():
            key = id(sem)
            if eng.waited.get(key, 0) < val:
                eng.prog.append(("w", sem, val))
                eng.waited[key] = val

    @staticmethod
    def _add(deps, tok):
        if tok is None:
            return
        sem, val = tok
        k = id(sem)
        if k not in deps or deps[k][1] < val:
            deps[k] = (sem, val)

    def op(self, eng, reads, writes, meth, **kw):
        deps = {}
        for t in reads:
            self._add(deps, t.w)
        for t in writes:
            self._add(deps, t.w)
            for tok in t.r.values():
                self._add(deps, tok)
        if eng is self.pe:
            deps.pop(id(eng.sem), None)
        self._wait(eng, deps)
        eng.n += 1
        eng.prog.append(("i", meth, kw, eng.sem, 1))
        tok = (eng.sem, eng.n)
        for t in reads:
            t.r[id(eng.sem)] = tok
        for t in writes:
            t.w = tok
            t.r = {}

    def dma(self, eng, out, in_, wt=None, rt=None, nowaw=False):
        deps = {}
        own = wt if wt is not None else rt
        if wt is not None:
            if not nowaw:
                self._add(deps, wt.w)
            for tok in wt.r.values():
                self._add(deps, tok)
        if rt is not None:
            self._add(deps, rt.w)
        self._wait(eng, deps)
        if own.dsem is None:
            own.dsem = self.new_sem("d")
        own.dn += 16
        eng.prog.append(("i", "dma_start", dict(out=out, in_=in_), own.dsem, 16))
        tok = (own.dsem, own.dn)
        if wt is not None:
            wt.w = tok
            wt.r = {}
            if rt is not None:
                rt.r[id(own.dsem)] = tok
        else:
            rt.r[id(own.dsem)] = tok
            self.out_dmas.append(tok)

    def allgather(self, src_ap, dst_ap, groups, rt, wt, kind="AllGather", op=None):
        eng = self.pool
        deps = {}
        self._add(deps, rt.w)
        self._add(deps, wt.w)
        for tok in wt.r.values():
            self._add(deps, tok)
        self._wait(eng, deps)
        if wt.dsem is None:
            wt.dsem = self.new_sem("cc")
        wt.dn += 1
        eng.prog.append(("i", "collective_compute", dict(kind=kind, op=(op or ALU.bypass), replica_groups=groups,
                                                         ins=[src_ap], outs=[dst_ap]), wt.dsem, 1))
        tok = (wt.dsem, wt.dn)
        wt.w = tok
        wt.r = {}
        rt.r[id(wt.dsem)] = tok

    def replay(self, eng, h):
        for it in eng.prog:
            if it[0] == "w":
                h.wait_ge(it[1], it[2])
            else:
                _, meth, kw, sem, inc = it
                getattr(h, meth)(**kw).then_inc(sem, inc)

    def bank(self, hold=False):
        while (self.bank_i % 8) in self.held:
            self.bank_i += 1
        i = self.bank_i % 8
        b = self.banks[i]
        self.bank_i += 1
        if hold:
            self.held.add(i)
        return b

    def release(self, b):
        for i, bb in enumerate(self.banks):
            if bb[1] is b[1]:
                self.held.discard(i)

    def mm(self, out, lhsT, rhs, start, stop, reads, writes):
        self.op(self.pe, reads, writes, "matmul", out=out, lhsT=lhsT, rhs=rhs, start=start, stop=stop)

    def actv(self, out, in_, func, reads, writes, bias=0.0, scale=1.0):
        self.op(self.act, reads, writes, "activation", out=out, in_=in_, func=func, bias=bias, scale=scale)

    def tt(self, out, in0, in1, op, reads, writes):
        self.op(self.dve, reads, writes, "tensor_tensor", out=out, in0=in0, in1=in1, op=op)

    def ts(self, out, in0, s1, s2, op0, op1, reads, writes):
        if op1 is None:
            self.op(self.dve, reads, writes, "tensor_scalar", out=out, in0=in0, scalar1=s1, scalar2=None, op0=op0)
        else:
            self.op(self.dve, reads, writes, "tensor_scalar", out=out, in0=in0, scalar1=s1, scalar2=s2, op0=op0, op1=op1)

    def stt(self, out, in0, scalar, in1, op0, op1, reads, writes):
        self.op(self.dve, reads, writes, "scalar_tensor_tensor", out=out, in0=in0, scalar=scalar, in1=in1, op0=op0, op1=op1)

    def cp(self, out, in_, reads, writes):
        self.op(self.dve, reads, writes, "tensor_copy", out=out, in_=in_)

    def recip(self, out, in_, reads, writes):
        self.op(self.dve, reads, writes, "reciprocal", out=out, in_=in_)


class SPool:
    def __init__(self, C, name, n, dt):
        self.items = [(C.sb(f"{name}{i}", [128, 1024], dt), Tk(f"{name}{i}")) for i in range(n)]
        self.free = list(range(n))
        self.name = name

    def get(self):
        assert self.free, f"pool {self.name} exhausted"
        i = self.free.pop(0)
        t, k = self.items[i]
        return (t, k, i)

    def put(self, *hs):
        for h in hs:
            assert h[2] not in self.free
            self.free.append(h[2])


def fm(v, nch):
    v = np.asarray(v, np.float32)
    lead = v.shape[:-1]
    v = v.reshape(lead + (nch, 128))
    return np.ascontiguousarray(np.moveaxis(v, -1, 0))


def _bf(a):
    return np.ascontiguousarray(np.asarray(a, np.float32)).astype(ml_dtypes.bfloat16)


_CONST_CACHE = {}


def _const_tables():
    if _CONST_CACHE:
        return _CONST_CACHE
    c = {}
    k = np.arange(128, dtype=np.float64)
    ph = 2.0 * np.pi * np.outer(k, k) / 128.0
    c["fn_ch"] = _bf(np.concatenate([np.cos(ph), np.sin(ph)], axis=1))
    for L in (256, 1024):
        l = np.arange(L, dtype=np.float64)
        th = 2.0 * np.pi * (np.outer(l, l) % L) / L
        t = np.stack([np.cos(th), -np.sin(th)], axis=0)
        c[f"fn_{L}"] = _bf(t.reshape(2, L // 128, 128, L).transpose(2, 0, 1, 3))
        a = (l + 0.5)
        w = 2.0 * np.pi * np.outer(a, a) / (2.0 * L)
        t = np.stack([np.cos(w), np.sin(w)], axis=0)
        c[f"hy_{L}"] = _bf(t.reshape(2, L // 128, 128, L).transpose(2, 0, 1, 3))
        wf = np.pi * a / (2.0 * L)
        ph2 = np.stack([np.cos(wf), np.sin(wf)], axis=0).reshape(2, L // 128, 128).transpose(2, 0, 1)
        c[f"hyph_{L}"] = np.ascontiguousarray(ph2.astype(np.float32))
        c[f"ntn_{L}"] = np.ascontiguousarray((-(l / L)).reshape(L // 128, 128).T.astype(np.float32))
        t32 = np.arange(L, dtype=np.float32)
        tn = t32 / np.float32(L)
        bands = np.linspace(1e-4, 16 - 1, 16, dtype=np.float32)
        wv = np.float32(2.0 * math.pi / L) * t32
        z = np.concatenate([tn[:, None], np.cos(wv[:, None] * bands), -np.sin(wv[:, None] * bands)], axis=-1)
        c[f"zf_{L}"] = np.ascontiguousarray(z.T.astype(np.float32))
    Ls = 1024
    pos_row = (np.arange(Ls) // 64).astype(np.float32)
    pos_col = (np.arange(Ls) % 64).astype(np.float32)
    n = 16
    inv = (np.float32(10000.0) ** (-np.arange(n, dtype=np.float32) / np.float32(n))).astype(np.float32)
    cosT = np.zeros((64, Ls), np.float64)
    sinT = np.zeros((64, Ls), np.float64)
    for dd in range(64):
        a_ = dd // 32
        i = dd % 16
        pos = pos_row if a_ == 0 else pos_col
        ang = (pos * inv[i]).astype(np.float32).astype(np.float64)
        cosT[dd] = np.cos(ang)
        sinT[dd] = np.sin(ang) * (-1.0 if (dd % 32) < 16 else 1.0)
    c["rope"] = np.ascontiguousarray(np.stack([np.tile(cosT, (2, 1)), np.tile(sinT, (2, 1))], axis=1).astype(np.float32))
    kk = np.arange(128)[:, None]
    qq = np.arange(128)[None, :]
    c["amask"] = _bf(np.concatenate([(kk <= qq), np.ones((128, 128), bool), (kk >= qq)], axis=1).astype(np.float32))
    c["ident"] = np.eye(128, dtype=np.float32)
    _CONST_CACHE.update(c)
    return _CONST_CACHE


ROPE_PARTNER = np.array([(dd + 16) if (dd % 32) < 16 else (dd - 16) for dd in range(64)])


def _prep_shared(inp):
    s = {}
    f32 = lambda a: np.ascontiguousarray(np.asarray(a, np.float32))
    s["mod_w"] = f32(inp["mod_w"])
    s["mod_b"] = fm(inp["mod_b"], 48)
    s["norm_mix"] = fm(inp["norm_mix"], 8)
    s["norm_mlp"] = fm(inp["norm_mlp"], 8)
    s["norm_final"] = fm(inp["norm_final"], 8)
    s["mlp_w1"] = f32(inp["mlp_w1"])
    s["mlp_w2"] = f32(inp["mlp_w2"])
    w = f32(inp["ev_w_in"])
    q, k, v, xr, g = w[..., :512], w[..., 512:640], w[..., 640:768], w[..., 768:1280], w[..., 1280:]
    k0, k1 = k[..., :64], k[..., 64:]
    qp = q.reshape(2, 1024, 8, 64)[..., ROPE_PARTNER].reshape(2, 1024, 512)
    k0p, k1p = k0[..., ROPE_PARTNER], k1[..., ROPE_PARTNER]
    lru_cols = []
    for c in range(4):
        lru_cols += [xr[..., c * 128:(c + 1) * 128], g[..., c * 128:(c + 1) * 128]]
    s["ev_w_in"] = np.ascontiguousarray(np.concatenate(
        [q, k0, k0, k1, k1, k, v] + lru_cols + [qp, k0p, k0p, k1p, k1p], axis=-1))
    assert s["ev_w_in"].shape[-1] == EV_COLS
    s["ev_w_out"] = f32(inp["ev_w_out"])
    sk = f32(inp["attn_sink"])
    s["sink"] = np.ascontiguousarray(np.repeat(sk.reshape(2, 4, 2, 1), 64, axis=3).reshape(2, 4, 128).transpose(2, 0, 1))
    s["lru_conv_w"] = fm(inp["lru_conv_w"], 4)
    s["lru_conv_b"] = fm(inp["lru_conv_b"], 4)
    s["lru_b_r"] = fm(inp["lru_b_r"], 4)
    s["lru_b_i"] = fm(inp["lru_b_i"], 4)
    s["lru_lam"] = fm(inp["lru_lambda"], 4)
    bd = np.zeros((2, 2, 2, 4, 128, 128), np.float32)
    for gi_, wsrc in enumerate((inp["lru_w_r"], inp["lru_w_i"])):
        wsrc = np.asarray(wsrc, np.float32)
        for c in range(4):
            bd[:, gi_, :, c, 0:64, 0:64] = wsrc[:, :, 2 * c]
            bd[:, gi_, :, c, 64:128, 64:128] = wsrc[:, :, 2 * c + 1]
    s["lru_bd"] = np.ascontiguousarray(bd.transpose(4, 0, 1, 2, 3, 5).reshape(128, 2, 2, 1024))
    wo = f32(inp["od_w_in"])
    cols = [wo[..., 0:512]]
    for c in range(4):
        for si in range(3):
            cols.append(wo[..., 512 + si * 512 + c * 128:512 + si * 512 + (c + 1) * 128])
    s["od_w_in"] = np.ascontiguousarray(np.concatenate(cols, axis=-1))
    s["od_w_out"] = f32(inp["od_w_out"])
    s["hy_conv_w"] = fm(inp["hy_conv_w"], 12)
    s["hy_conv_b"] = fm(inp["hy_conv_b"], 12)
    s["hy_bias"] = fm(inp["hy_bias"], 4)
    s["hy_w1"] = np.ascontiguousarray(f32(inp["hy_w1"]).transpose(1, 0, 2))
    s["hy_w2"] = np.ascontiguousarray(f32(inp["hy_w2"]).transpose(1, 0, 2))
    s["hy_b12"] = np.ascontiguousarray(np.stack([f32(inp["hy_b1"]), f32(inp["hy_b2"])], axis=1).transpose(2, 0, 1))
    s["hy_freq"] = np.ascontiguousarray(f32(inp["hy_freq"]).transpose(2, 0, 1))
    w3 = f32(inp["hy_w3"]).reshape(2, 64, 2, 2, 4, 128)
    s["hy_w3"] = np.ascontiguousarray(w3.transpose(1, 0, 4, 3, 2, 5).reshape(64, 2, 2048))
    ld = f32(inp["hy_log_decay"]).reshape(2, 2, 2, 4, 128)
    s["hy_ld"] = np.ascontiguousarray(ld.transpose(0, 3, 2, 1, 4).reshape(1, 2 * 2048))
    s.update(_const_tables())
    return s


def _prep_core(inp, i):
    bs = i // 4
    xp = np.asarray(inp["x_prompt"], np.float32)[4 * i:4 * i + 4].reshape(1024, D)
    xs = np.asarray(inp["x_sample"], np.float32)[bs]
    X = np.concatenate([xp, xs], axis=0)
    m = {}
    m["xT"] = np.ascontiguousarray(X.reshape(NT, 8, 128).transpose(2, 1, 0))
    cv = np.stack([np.asarray(inp["c_ctx"], np.float32), np.asarray(inp["c"], np.float32)[bs]], axis=0)
    m["cvec"] = np.ascontiguousarray(cv.reshape(2, 8, 128).transpose(2, 1, 0))
    ck = np.asarray(inp["cache_k"], np.float32)[bs]
    kc = ck.transpose(3, 0, 2, 1)
    m["kctx"] = np.ascontiguousarray(np.concatenate([kc, kc], axis=0))
    cvv = np.asarray(inp["cache_v"], np.float32)[bs].reshape(2, 2, 128, 128)
    m["vctx"] = np.ascontiguousarray(cvv.transpose(2, 0, 1, 3))
    m["h0"] = fm(np.asarray(inp["state_lru"], np.float32)[bs], 4)
    r = i % 4
    hkv = r // 2
    we = np.asarray(inp["ev_w_in"], np.float32)
    q_, k_, v_, xr_, g_ = we[..., :512], we[..., 512:640], we[..., 640:768], we[..., 768:1280], we[..., 1280:]
    qr = q_[..., r * 128:(r + 1) * 128]
    qrp = qr.reshape(2, 1024, 2, 64)[..., ROPE_PARTNER].reshape(2, 1024, 128)
    m["s_ev_w"] = np.ascontiguousarray(np.concatenate(
        [qr, qrp, xr_[..., r * 128:(r + 1) * 128], g_[..., r * 128:(r + 1) * 128]], axis=-1))
    kh = k_[..., hkv * 64:(hkv + 1) * 64]
    khp = kh[..., ROPE_PARTNER]
    m["s_kv_w"] = np.ascontiguousarray(np.concatenate([kh, kh, khp, khp, v_[..., hkv * 64:(hkv + 1) * 64]], axis=-1))
    sk = np.asarray(inp["attn_sink"], np.float32)
    m["s_sink"] = np.ascontiguousarray(np.repeat(sk[:, 2 * r:2 * r + 2], 64, axis=1).T)
    m["s_kctx"] = np.ascontiguousarray(m["kctx"][:, :, hkv, :])
    m["s_vctx"] = np.ascontiguousarray(m["vctx"][:, :, :, hkv * 64:(hkv + 1) * 64])
    bd = np.zeros((2, 2, 2, 128, 128), np.float32)
    for gi_, wsrc in enumerate((inp["lru_w_r"], inp["lru_w_i"])):
        wsrc = np.asarray(wsrc, np.float32)
        bd[:, gi_, :, 0:64, 0:64] = wsrc[:, :, 2 * r]
        bd[:, gi_, :, 64:128, 64:128] = wsrc[:, :, 2 * r + 1]
    m["s_lbd"] = np.ascontiguousarray(bd.transpose(3, 0, 1, 2, 4).reshape(128, 2, 2, 256))
    ch = lambda a: np.asarray(a, np.float32).reshape(a.shape[:-1] + (4, 128))[..., r, :]
    m["s_lcw"] = np.ascontiguousarray(np.moveaxis(ch(inp["lru_conv_w"]), -1, 0))
    m["s_lcb"] = np.ascontiguousarray(np.moveaxis(ch(inp["lru_conv_b"]), -1, 0))
    m["s_lbr"] = np.ascontiguousarray(np.moveaxis(ch(inp["lru_b_r"]), -1, 0))
    m["s_lbi"] = np.ascontiguousarray(np.moveaxis(ch(inp["lru_b_i"]), -1, 0))
    m["s_llam"] = np.ascontiguousarray(np.moveaxis(ch(inp["lru_lambda"]), -1, 0))
    m["s_h0"] = np.ascontiguousarray(np.moveaxis(ch(np.asarray(inp["state_lru"], np.float32)[bs]), -1, 0))
    m["s_w1"] = np.ascontiguousarray(np.asarray(inp["mlp_w1"], np.float32)[:, :, r * 1024:(r + 1) * 1024])
    m["s_w2"] = np.ascontiguousarray(np.asarray(inp["mlp_w2"], np.float32)[:, r * 1024:(r + 1) * 1024, :])
    wo = np.asarray(inp["od_w_in"], np.float32)
    m["s_od_w"] = np.ascontiguousarray(np.concatenate(
        [wo[..., r * 128:(r + 1) * 128]] + [wo[..., 512 + si * 512 + r * 128:512 + si * 512 + (r + 1) * 128] for si in range(3)],
        axis=-1))
    hcw = np.asarray(inp["hy_conv_w"], np.float32).reshape(2, 3, 3, 4, 128)[:, :, :, r, :]
    m["s_hcw"] = np.ascontiguousarray(hcw.transpose(3, 0, 1, 2))
    hcb = np.asarray(inp["hy_conv_b"], np.float32).reshape(2, 3, 4, 128)[:, :, r, :]
    m["s_hcb"] = np.ascontiguousarray(hcb.transpose(2, 0, 1))
    hb = np.asarray(inp["hy_bias"], np.float32).reshape(2, 2, 4, 128)[:, :, r, :]
    m["s_hbias"] = np.ascontiguousarray(hb.transpose(2, 0, 1))
    w3 = np.asarray(inp["hy_w3"], np.float32).reshape(2, 64, 2, 2, 4, 128)[:, :, :, :, r, :]
    m["s_hw3"] = np.ascontiguousarray(w3.transpose(1, 0, 3, 2, 4).reshape(64, 2, 512))
    ld = np.asarray(inp["hy_log_decay"], np.float32).reshape(2, 2, 2, 4, 128)[:, :, :, r, :]
    m["s_hld"] = np.ascontiguousarray(ld.transpose(0, 2, 1, 3).reshape(1, 1024))
    return m


def build(depth=DEPTH, debug=False, skip=(), layers=None):
    nc = bass.Bass("TRN2", target_bir_lowering=False)
    with ExitStack() as es:
        _build_body(nc, es, depth, debug, skip, layers)
    return nc


def _build_body(nc, es, depth, debug, skip, layers=None):
    def din(name, shape, dt=F32):
        return nc.dram_tensor(name, list(shape), dt, kind="ExternalInput").ap()

    def dout(name, shape):
        return nc.dram_tensor(name, list(shape), F32, kind="ExternalOutput").ap()

    d = {}
    d["xT"] = din("xT", [128, 8, NT])
    d["cvec"] = din("cvec", [128, 8, 2])
    d["kctx"] = din("kctx", [128, 2, 2, 256])
    d["vctx"] = din("vctx", [128, 2, 2, 128])
    d["h0"] = din("h0", [128, 2, 2, 4])
    d["s_od_w"] = din("s_od_w", [2, 1024, 512])
    d["s_w1"] = din("s_w1", [4, 1024, 1024])
    d["s_w2"] = din("s_w2", [4, 1024, 1024])
    ar_src = nc.dram_tensor("ar_src", [1024, HALF], F32).ap()
    ar_dst = nc.dram_tensor("ar_dst", [1024, HALF], F32).ap()
    d["s_ev_w"] = din("s_ev_w", [2, 1024, 512])
    d["s_kv_w"] = din("s_kv_w", [2, 1024, 320])
    d["s_sink"] = din("s_sink", [128, 2])
    d["s_kctx"] = din("s_kctx", [128, 2, 256])
    d["s_vctx"] = din("s_vctx", [128, 2, 2, 64])
    d["s_lbd"] = din("s_lbd", [128, 2, 2, 256])
    d["s_lcw"] = din("s_lcw", [128, 2, 4])
    d["s_lcb"] = din("s_lcb", [128, 2])
    d["s_lbr"] = din("s_lbr", [128, 2, 2])
    d["s_lbi"] = din("s_lbi", [128, 2, 2])
    d["s_llam"] = din("s_llam", [128, 2, 2])
    d["s_h0"] = din("s_h0", [128, 2, 2])
    d["s_hcw"] = din("s_hcw", [128, 2, 3, 3])
    d["s_hcb"] = din("s_hcb", [128, 2, 3])
    d["s_hbias"] = din("s_hbias", [128, 2, 2])
    d["s_hw3"] = din("s_hw3", [64, 2, 512])
    d["s_hld"] = din("s_hld", [1, 1024])
    cc_src = nc.dram_tensor("cc_src", [2 * 128, HALF], BF16).ap()
    cc_dst = nc.dram_tensor("cc_dst", [4 * 2 * 128, HALF], BF16).ap()
    d["mod_w"] = din("mod_w", [4, 1024, 6144])
    d["mod_b"] = din("mod_b", [128, 4, 48])
    d["norm_mix"] = din("norm_mix", [128, 4, 8])
    d["norm_mlp"] = din("norm_mlp", [128, 4, 8])
    d["norm_final"] = din("norm_final", [128, 8])
    d["mlp_w1"] = din("mlp_w1", [4, 1024, 4096])
    d["mlp_w2"] = din("mlp_w2", [4, 4096, 1024])
    d["ev_w_in"] = din("ev_w_in", [2, 1024, EV_COLS])
    d["ev_w_out"] = din("ev_w_out", [2, 1024, 1024])
    d["sink"] = din("sink", [128, 2, 4])
    d["lru_conv_w"] = din("lru_conv_w", [128, 2, 4, 4])
    d["lru_conv_b"] = din("lru_conv_b", [128, 2, 4])
    d["lru_b_r"] = din("lru_b_r", [128, 2, 2, 4])
    d["lru_b_i"] = din("lru_b_i", [128, 2, 2, 4])
    d["lru_lam"] = din("lru_lam", [128, 2, 2, 4])
    d["lru_bd"] = din("lru_bd", [128, 2, 2, 1024])
    d["od_w_in"] = din("od_w_in", [2, 1024, OD_COLS])
    d["od_w_out"] = din("od_w_out", [2, 1024, 1024])
    d["hy_conv_w"] = din("hy_conv_w", [128, 2, 3, 12])
    d["hy_conv_b"] = din("hy_conv_b", [128, 2, 12])
    d["hy_bias"] = din("hy_bias", [128, 2, 2, 4])
    d["hy_w1"] = din("hy_w1", [33, 2, 64])
    d["hy_w2"] = din("hy_w2", [64, 2, 64])
    d["hy_b12"] = din("hy_b12", [64, 2, 2])
    d["hy_freq"] = din("hy_freq", [64, 2, 2])
    d["hy_w3"] = din("hy_w3", [64, 2, 2048])
    d["hy_ld"] = din("hy_ld", [1, 4096])
    d["fn_ch"] = din("fn_ch", [128, 256], BF16)
    for L in (256, 1024):
        d[f"fn_{L}"] = din(f"fn_{L}", [128, 2, L // 128, L], BF16)
        d[f"hy_{L}"] = din(f"hy_{L}", [128, 2, L // 128, L], BF16)
        d[f"hyph_{L}"] = din(f"hyph_{L}", [128, 2, L // 128])
        d[f"ntn_{L}"] = din(f"ntn_{L}", [128, L // 128])
        d[f"zf_{L}"] = din(f"zf_{L}", [33, L])
    d["rope"] = din("rope", [128, 2, 1024])
    d["amask"] = din("amask", [128, 384], BF16)
    d["ident"] = din("ident", [128, 128])
    o_y = dout("yT", [128, 8, NT])
    o_kv = dout("kv", [4, 2, 2, 128, 256])
    o_lru = dout("lruT", [128, 2, 2, 4, 4])
    o_dbg = dout("dbg", [depth, 128, 8, NT]) if debug else None
    o_dbg2 = dout("dbg2", [128, 2, 2, 4]) if debug else None

    C = Ctx(nc, es)
    sb = C.sb
    marks = []
    build.marks = marks

    def mark(name):
        marks.append((name, C.pe.n))
    for i in range(8):
        C.banks.append((es.enter_context(nc.psum_tensor(f"bank{i}", [128, 512], F32)), Tk(f"bank{i}")))

    x = sb("x", [128, 8, NT], F32)
    xk = [[Tk(f"x{c}_{t}") for t in range(4)] for c in range(8)]
    hm = sb("hm", [128, 8, NT], BF16)
    hk = [Tk("h0"), Tk("h1")]
    mixk = [Tk(f"mix{c}") for c in range(8)]

    class Off:
        def __init__(self, t, off, n):
            self.t, self.off, self.n = t, off, n

        def __getitem__(self, idx):
            p, c, sl = idx
            a = 0 if sl.start is None else sl.start
            b = self.n if sl.stop is None else sl.stop
            return self.t[p, c, self.off + a:self.off + b]

    hbuf = Off(hm, 0, NT)
    mixb = Off(hm, HALF, HALF)

    def hkl(lt):
        return [hk[lt]] if lt < 2 else list(mixk)
    RING_N = 6
    ring = [sb(f"ring{i}", [128, 4096], BF16) for i in range(RING_N)]
    ringk = [Tk(f"ring{i}") for i in range(RING_N)]
    ring_i = [0]
    ring_allowed = [list(range(RING_N))]
    side_gen = [None, False]

    class BigTab:
        def get(self, cs_, tl, a, b):
            si = 2 + cs_ * 2 + tl // 4
            o = (tl % 4) * 1024
            return ring[si][:, o + a:o + b], ringk[si]

        def load(self, name):
            for cs_ in range(2):
                for hh in range(2):
                    si = 2 + cs_ * 2 + hh
                    C.dma(C.sp, ring[si][:].rearrange("p (a b) -> p a b", a=4), d[name][:, cs_, hh * 4:(hh + 1) * 4, :],
                          wt=ringk[si])

    class SmallTab:
        def __init__(self, t, k):
            self.t, self.k = t, k

        def get(self, cs_, tl, a, b):
            return self.t[:, cs_, tl, a:b], self.k

    bigtab = BigTab()
    FP = SPool(C, "fp", 8, F32)
    BP = SPool(C, "bp", 7, BF16)

    def load_w(src2d, nk, ncols):
        al = ring_allowed[0]
        i = al[ring_i[0] % len(al)]
        ring_i[0] += 1
        slot, tk = ring[i], ringk[i]
        view = slot[:, 0:nk * ncols].rearrange("p (k n) -> p k n", k=nk)
        C.dma(C.pool, view, src2d.rearrange("(k p) n -> p k n", p=128), wt=tk)
        if side_gen[0] is not None and not side_gen[1]:
            side_gen[1] = True
            next(side_gen[0], None)
            side_gen[1] = False
        return view, tk

    def ld_const(name, shape, src, dt=F32, eng=None):
        t = sb(name, shape, dt)
        tk = Tk(name)
        C.dma(eng or C.sp, t[:], src, wt=tk)
        return t, tk

    xload = Tk("xload")
    for c in range(8):
        C.dma(C.sp, x[:, c, :], d["xT"][:, c, :], wt=xload)
    for c in range(8):
        for t in range(4):
            xk[c][t].w = xload.w

    cvec, cvec_k = ld_const("cvec", [128, 8, 2], d["cvec"])
    mod_b, mod_b_k = ld_const("mod_b", [128, 4, 48], d["mod_b"])
    nmix, nmix_k = ld_const("nmix", [128, 4, 8], d["norm_mix"])
    nmlp, nmlp_k = ld_const("nmlp", [128, 4, 8], d["norm_mlp"])
    nfin, nfin_k = ld_const("nfin", [128, 8], d["norm_final"])
    sink, sink_k = ld_const("sink", [128, 2, 4], d["sink"])
    lcw, lcw_k = ld_const("lcw", [128, 2, 4, 4], d["lru_conv_w"])
    lcb, lcb_k = ld_const("lcb", [128, 2, 4], d["lru_conv_b"])
    lbr, lbr_k = ld_const("lbr", [128, 2, 2, 4], d["lru_b_r"])
    lbi, lbi_k = ld_const("lbi", [128, 2, 2, 4], d["lru_b_i"])
    llam, llam_k = ld_const("llam", [128, 2, 2, 4], d["lru_lam"])
    hcw, hcw_k = ld_const("hcw", [128, 2, 3, 12], d["hy_conv_w"])
    hcb, hcb_k = ld_const("hcb", [128, 2, 12], d["hy_conv_b"])
    hbias, hbias_k = ld_const("hbias", [128, 2, 2, 4], d["hy_bias"])
    shcw, shcw_k = ld_const("shcw", [128, 2, 3, 3], d["s_hcw"])
    shcb, shcb_k = ld_const("shcb", [128, 2, 3], d["s_hcb"])
    shbias, shbias_k = ld_const("shbias", [128, 2, 2], d["s_hbias"])
    ccs_k, ccd_k = Tk("cc_src"), Tk("cc_dst")
    ars_k, ard_k = Tk("ar_src"), Tk("ar_dst")
    GROUPS = [[0, 1, 2, 3], [4, 5, 6, 7]]

    hw1, hw1_k = ld_const("hw1", [33, 2, 64], d["hy_w1"])
    hw2, hw2_k = ld_const("hw2", [64, 2, 64], d["hy_w2"])
    hb12, hb12_k = ld_const("hb12", [64, 2, 2], d["hy_b12"])
    hfreq, hfreq_k = ld_const("hfreq", [64, 2, 2], d["hy_freq"])
    amask, amask_k = ld_const("amask", [128, 384], d["amask"], BF16)
    ident, ident_k = ld_const("ident", [128, 128], d["ident"])
    fnch, fnch_k = ld_const("fnch", [128, 256], d["fn_ch"], BF16)
    fn256, fn256_k = ld_const("fn256", [128, 2, 2, 256], d["fn_256"], BF16)
    hy256, hy256_k = ld_const("hy256", [128, 2, 2, 256], d["hy_256"], BF16)
    hyph = {L: ld_const(f"hyph{L}", [128, 2, L // 128], d[f"hyph_{L}"]) for L in (256, 1024)}
    ntn = {L: ld_const(f"ntn{L}", [128, L // 128], d[f"ntn_{L}"]) for L in (256, 1024)}
    kctx, kctx_k = ld_const("kctx", [128, 2, 256], d["s_kctx"], BF16, eng=C.pool)
    vctx, vctx_k = ld_const("vctx", [128, 2, 2, 64], d["s_vctx"], BF16, eng=C.pool)
    ssink, ssink_k = ld_const("ssink", [128, 2], d["s_sink"])
    slcw, slcw_k = ld_const("slcw", [128, 2, 4], d["s_lcw"])
    slcb, slcb_k = ld_const("slcb", [128, 2], d["s_lcb"])
    slbr, slbr_k = ld_const("slbr", [128, 2, 2], d["s_lbr"])
    slbi, slbi_k = ld_const("slbi", [128, 2, 2], d["s_lbi"])
    sllam, sllam_k = ld_const("sllam", [128, 2, 2], d["s_llam"])
    sh0, sh0_k = ld_const("sh0", [128, 2, 2], d["s_h0"])

    def exchange_start():
        C.dma(C.pool, cc_src[0:128, :], mixb[:, 0, :], wt=ccs_k, rt=mixk[0])
        C.dma(C.pool, cc_src[128:256, :], mixb[:, 4, :], wt=ccs_k, rt=mixk[4])
        C.allgather(cc_src, cc_dst, GROUPS, rt=ccs_k, wt=ccd_k)

    def exchange_finish():
        for r_ in range(4):
            for w_ in range(2):
                cix = w_ * 4 + r_
                C.dma(C.sp if (cix % 2) else C.pool, mixb[:, cix, :], cc_dst[(r_ * 2 + w_) * 128:(r_ * 2 + w_ + 1) * 128, :],
                      wt=mixk[cix], rt=ccd_k)

    ones_b = sb("ones_b", [128, 128], BF16)
    ones_k = Tk("ones")
    C.op(C.dve, [], [ones_k], "memset", ap=ones_b[:], constant=1.0)
    epst = sb("epst", [128, 1], F32)
    eps_k = Tk("eps")
    C.op(C.dve, [], [eps_k], "memset", ap=epst[:], constant=EPS)
    zerot = sb("zerot", [128, 1], F32)
    zero_k = Tk("zero")
    C.op(C.dve, [], [zero_k], "memset", ap=zerot[:], constant=0.0)
    onet = sb("onet", [128, 1], F32)
    one_k = Tk("one")
    C.op(C.dve, [], [one_k], "memset", ap=onet[:], constant=1.0)

    modM_ = [sb(f"modM{i}", [128, 48, 2], F32) for i in range(2)]
    modMk_ = [Tk(f"modM{i}") for i in range(2)]
    modA_ = [sb(f"modA{i}", [128, 2, 8, 2], F32) for i in range(2)]
    modAk_ = [Tk(f"modA{i}") for i in range(2)]
    cur = {"par": 0}
    silu_c = sb("silu_c", [128, 8, 2], BF16)
    silu_k = Tk("silu_c")
    C.actv(silu_c[:], cvec[:], AF.Silu, [cvec_k], [silu_k])


    def modulation_gen(l, staged=False):
        par = l % 2
        mM, mMk, mA, mAk = modM_[par], modMk_[par], modA_[par], modAk_[par]
        pbb = C.bank(hold=True)
        pb, pk = pbb
        pv = pb[:, 0:96].rearrange("p (n v) -> p n v", v=2)

        def fin(c0, c1, nis):
            for v in range(2):
                C.tt(mM[:, c0:c1, v], pv[:, c0:c1, v], mod_b[:, l, c0:c1], ALU.add, [pk, mod_b_k], [mMk])
            for ni in nis:
                gt, gk, off = ((nmix, nmix_k, 8), (nmlp, nmlp_k, 32))[ni]
                for v in range(2):
                    C.stt(mA[:, ni, :, v], mM[:, off:off + 8, v], 1.0, gt[:, l, :], ALU.add, ALU.mult,
                          [mMk, gk], [mAk])

        for nb in range(12):
            wv, wk = load_w(d["mod_w"][l, 0:1024, nb * 512:(nb + 1) * 512], 8, 512)
            for nn in range(4):
                n = nb * 4 + nn
                for k in range(8):
                    C.mm(pb[:, 2 * n:2 * n + 2], wv[:, k, nn * 128:(nn + 1) * 128], silu_c[:, k, :],
                         k == 0, k == 7, [wk, silu_k], [pk])
            if staged and nb == 3:
                fin(0, 16, [0])
            if staged and nb == 5:
                fin(16, 24, [])
            if staged and nb == 11:
                fin(24, 48, [1])
            if nb == 11:
                if not staged:
                    fin(0, 48, [0, 1])
                C.release(pbb)
            yield

    def rstd_tile(t):
        ts_ = slice(t * TT, (t + 1) * TT)
        pb, pk = C.bank()
        sq = BP.get()
        for c in range(8):
            o = (c % 2) * TT
            C.actv(sq[0][:, o:o + TT], x[:, c, ts_], AF.Square, [xk[c][t]], [sq[1]])
            C.mm(pb[:], ones_b[:], sq[0][:, o:o + TT], c == 0, c == 7, [ones_k, sq[1]], [pk])
        BP.put(sq)
        rs = FP.get()
        C.actv(rs[0][:, TT:2 * TT], pb[:], AF.Ln, [pk, eps_k], [rs[1]], bias=epst[:, 0:1], scale=1.0 / D)
        C.actv(rs[0][:, 0:TT], rs[0][:, TT:2 * TT], AF.Exp, [rs[1]], [rs[1]], scale=-0.5)
        return rs

    def norm_mod(t, lt, ni, v, B_off):
        mark(f"norm t{t}")
        ts_ = slice(t * TT, (t + 1) * TT)
        ls_ = slice(lt * TT, (lt + 1) * TT)
        rs = rstd_tile(t)
        tmp = FP.get()
        for c in range(8):
            o = (c % 2) * TT
            C.tt(tmp[0][:, o:o + TT], x[:, c, ts_], rs[0][:, 0:TT], ALU.mult, [xk[c][t], rs[1]], [tmp[1]])
            C.actv(hbuf[:, c, ls_], tmp[0][:, o:o + TT], AF.Identity, [tmp[1], modAk_[cur['par']], modMk_[cur['par']]], hkl(lt),
                   bias=modM_[cur['par']][:, B_off + c, v:v + 1], scale=modA_[cur['par']][:, ni, c, v:v + 1])
        FP.put(rs, tmp)

    def proj_fm(wv, wk, col0, width=128):
        res = []
        for lt in range(2):
            pb, pk = C.bank()
            for k in range(8):
                C.mm(pb[0:width, :], wv[:, k, col0:col0 + width], hbuf[:, k, lt * TT:(lt + 1) * TT], k == 0, k == 7,
                     [wk, hk[lt]], [pk])
            res.append((pb, pk))
        return res

    def resid(pb, pk, c, t, gate_ap):
        C.stt(x[:, c, t * TT:(t + 1) * TT], pb[:], gate_ap, x[:, c, t * TT:(t + 1) * TT], ALU.mult, ALU.add,
              [pk, modMk_[cur['par']], xk[c][t]], [xk[c][t]])

    def w_out_phase(wdram, j, half, pre=None):
        mark(f'h{half} w_out')
        for nb in range(2):
            if pre is not None:
                wO, wOk = pre[nb]
            else:
                wO, wOk = load_w(wdram[j, 0:1024, nb * 512:(nb + 1) * 512], 8, 512)
            for nn in range(4):
                n = nb * 4 + nn
                for lt in range(2):
                    pb, pk = C.bank()
                    for k in range(8):
                        C.mm(pb[:], wO[:, k, nn * 128:(nn + 1) * 128], mixb[:, k, lt * TT:(lt + 1) * TT], k == 0, k == 7,
                             [wOk, mixk[k]], [pk])
                    resid(pb, pk, n, 2 * half + lt, modM_[cur['par']][:, 16 + n, half:half + 1])

    kvst = [sb("kvst0", [128, 256], F32)] * 2
    kvstk = [Tk("kvst0")] * 2
    lst = sb("lst", [128, 2, 2, 4, 4], F32)
    lstk = Tk("lst")
    lcst = sb("lcst", [128, 2, 2, 4], F32)
    lcstk = Tk("lcst")
    sinkE = sb("sinkE", [128, 2, 4], F32)
    sinkEk = Tk("sinkE")
    C.actv(sinkE[:], sink[:], AF.Exp, [sink_k], [sinkEk])
    ltmp = sb("ltmp", [128, 2, 2, 4], F32)
    ltmpk = Tk("ltmp")
    C.actv(ltmp[:], llam[:], AF.Exp, [llam_k], [ltmpk], scale=-1.0)
    C.actv(ltmp[:], ltmp[:], AF.Ln, [ltmpk, one_k], [ltmpk], bias=onet[:, 0:1])
    C.ts(lcst[:], ltmp[:], -8.0, None, ALU.mult, None, [ltmpk], [lcstk])
    if debug:
        C.dma(C.sp, o_dbg2, lcst[:], rt=lcstk)

    ssinkE = sb("ssinkE", [128, 2], F32)
    ssinkEk = Tk("ssinkE")
    C.actv(ssinkE[:], ssink[:], AF.Exp, [ssink_k], [ssinkEk])
    slcst = sb("slcst", [128, 2, 2], F32)
    slcstk = Tk("slcst")
    C.actv(slcst[:], sllam[:], AF.Exp, [sllam_k], [slcstk], scale=-1.0)
    C.actv(slcst[:], slcst[:], AF.Ln, [slcstk, one_k], [slcstk], bias=onet[:, 0:1])
    C.ts(slcst[:], slcst[:], -8.0, None, ALU.mult, None, [slcstk], [slcstk])

    def attn_norm(pn, pnk, num_ap, den_ap, prow, width, j, hq, out_ap):
        s1 = FP.get()
        C.actv(s1[0][prow, 0:width], den_ap, AF.Ln, [pnk, sinkEk], [s1[1]], bias=sinkE[prow, j, hq // 2:hq // 2 + 1])
        C.actv(s1[0][prow, 0:width], s1[0][prow, 0:width], AF.Exp, [s1[1]], [s1[1]], scale=-1.0)
        C.tt(out_ap, num_ap, s1[0][prow, 0:width], ALU.mult, [pnk, s1[1]], [mixk[hq // 2]])
        FP.put(s1)

    def even_mixer(l, half):
        j = l // 2
        mark(f'L{l}h{half} even:attn')
        jw = JOV.get('jw', j); jl = JOV.get('jl', j); jc = JOV.get('jc', j); jo = JOV.get('jo', j); js = JOV.get('js', j); jsa = JOV.get('jsa', js); jsb = JOV.get('jsb', js); jsc = JOV.get('jsc', js); jsc2 = JOV.get('jsc2', jsc)
        t0 = half * HALF
        sample = half == 1
        nseq = 1 if sample else 4
        Ls = 1024 if sample else 256
        scale = 0.125
        wKV, wKVk = load_w(d["ev_w_in"][jw, 0:1024, EV_KD:EV_KD + 512], 8, 512)
        vtm = BP.get()
        vt = vtm[0][:].rearrange("p (a b) -> p a b", a=8)
        for tl in range(8):
            pb, pk = C.bank()
            for k in range(8):
                C.mm(pb[:, 0:256], hbuf[:, k, tl * 128:(tl + 1) * 128], wKV[:, k, 256:512], k == 0, k == 7,
                     [wKVk, hk[tl // 4]], [pk])
            C.actv(vt[:, tl, :], pb[:, 128:256], AF.Copy, [pk], [vtm[1]])
            if not sample:
                i = tl % 2
                C.cp(kvst[i][:], pb[:, 0:256], [pk], [kvstk[i]])
                C.dma(C.sp, o_kv[tl // 2, jo, tl % 2], kvst[i][:], rt=kvstk[i])
        if sample:
            wKP, wKPk = load_w(d["ev_w_in"][jw, 0:1024, EV_KDP:EV_KDP + 256], 8, 256)
            rp = [FP.get(), FP.get()]
            for i_ in range(2):
                C.dma(C.sp, rp[i_][0][:], d["rope"][:, i_, :], wt=rp[i_][1])

        def roped(dst, ps, pp):
            for lt in range(2):
                sl = slice(lt * TT, (lt + 1) * TT)
                s1 = FP.get()
                C.tt(s1[0][:, 0:TT], ps[lt][0][:], rp[0][0][:, sl], ALU.mult, [ps[lt][1], rp[0][1]], [s1[1]])
                C.tt(s1[0][:, TT:2 * TT], pp[lt][0][:], rp[1][0][:, sl], ALU.mult, [pp[lt][1], rp[1][1]], [s1[1]])
                C.tt(dst[0][:, sl], s1[0][:, 0:TT], s1[0][:, TT:2 * TT], ALU.add, [s1[1]], [dst[1]])
                FP.put(s1)

        def plain(dst, ps):
            C.actv(dst[0][:, 0:TT], ps[0][0][:], AF.Copy, [ps[0][1]], [dst[1]])
            C.cp(dst[0][:, TT:2 * TT], ps[1][0][:], [ps[1][1]], [dst[1]])

        kTs = []
        for hkv in range(2):
            kT = BP.get()
            ps = proj_fm(wKV, wKVk, hkv * 128)
            if sample:
                pp = proj_fm(wKP, wKPk, hkv * 128)
                roped(kT, ps, pp)
            else:
                plain(kT, ps)
            kTs.append(kT)
        wQ, wQk = load_w(d["ev_w_in"][jw, 0:1024, EV_Q:EV_Q + 512], 8, 512)
        if sample:
            wQP, wQPk = load_w(d["ev_w_in"][jw, 0:1024, EV_QP:EV_QP + 512], 8, 512)
        for qc_i in range(4):
            hkv = qc_i // 2
            kT = kTs[hkv]
            qT = BP.get()
            ps = proj_fm(wQ, wQk, qc_i * 128)
            if sample:
                pp = proj_fm(wQP, wQPk, qc_i * 128)
                roped(qT, ps, pp)
            else:
                plain(qT, ps)
            for hq in (2 * qc_i, 2 * qc_i + 1):
                pbase = (hq % 2) * 64
                prow = slice(pbase, pbase + 64)
                vcol = slice(hkv * 64, (hkv + 1) * 64)
                if not sample:
                    def st1(s, prow=prow):
                        ssl = slice(s * 256, (s + 1) * 256)
                        pb, pk = C.bank()
                        for kt in range(2):
                            C.mm(pb[:, kt * 256:(kt + 1) * 256],
                                 kT[0][prow, s * 256 + kt * 128:s * 256 + (kt + 1) * 128],
                                 qT[0][prow, ssl], True, True, [kT[1], qT[1]], [pk])
                        eb = BP.get()
                        C.actv(eb[0][:, 0:512], pb[:], AF.Exp, [pk], [eb[1]], scale=scale)
                        return eb

                    def st2(s, eb, prow=prow, vcol=vcol, hq=hq):
                        ssl = slice(s * 256, (s + 1) * 256)
                        pn, pnk = C.bank()
                        for kt in range(2):
                            C.mm(pn[prow, 0:256], vt[:, s * 2 + kt, vcol], eb[0][:, kt * 256:(kt + 1) * 256],
                                 kt == 0, kt == 1, [vtm[1], eb[1]], [pnk])
                        for kt in range(2):
                            C.mm(pn[prow, 256:512], ones_b[:, 0:64], eb[0][:, kt * 256:(kt + 1) * 256],
                                 kt == 0, kt == 1, [ones_k, eb[1]], [pnk])
                        BP.put(eb)
                        attn_norm(pn, pnk, pn[prow, 0:256], pn[prow, 256:512], prow, 256, jsa, hq,
                                  mixb[prow, hq // 2, ssl])

                    ebs = st1(0)
                    for s in range(4):
                        nxt_eb = st1(s + 1) if s + 1 < 4 else None
                        st2(s, ebs)
                        ebs = nxt_eb
                else:
                    for qh in range(2):
                        el = [BP.get(), BP.get()]
                        ec = BP.get()
                        kts = [kt for kt in range(4 * qh - 1, 4 * qh + 5) if 0 <= kt <= 7]
                        einfo = {}
                        esi, eoff = 0, 0
                        for kt in kts:
                            q0 = max(kt - 1, 4 * qh)
                            q1 = min(kt + 1, 4 * qh + 3)
                            nq = (q1 - q0 + 1) * 128
                            m0 = (q0 - (kt - 1)) * 128
                            if eoff + nq > 1024:
                                esi += 1
                                eoff = 0
                            pb, pk = C.bank()
                            C.mm(pb[:, 0:nq], kT[0][prow, kt * 128:(kt + 1) * 128], qT[0][prow, q0 * 128:q0 * 128 + nq],
                                 True, True, [kT[1], qT[1]], [pk])
                            eh = el[esi]
                            eo = eoff
                            C.actv(eh[0][:, eo:eo + nq], pb[:, 0:nq], AF.Exp, [pk], [eh[1]], scale=scale)
                            C.tt(eh[0][:, eo:eo + nq], eh[0][:, eo:eo + nq], amask[:, m0:m0 + nq], ALU.mult,
                                 [eh[1], amask_k], [eh[1]])
                            einfo[kt] = (eh, eo, q0)
                            eoff += nq
                        for ct in range(2):
                            pb, pk = C.bank()
                            C.mm(pb[:], kctx[prow, jc, hkv, ct * 128:(ct + 1) * 128], qT[0][prow, qh * 512:(qh + 1) * 512],
                                 True, True, [kctx_k, qT[1]], [pk])
                            C.actv(ec[0][:, ct * 512:(ct + 1) * 512], pb[:], AF.Exp, [pk], [ec[1]], scale=scale)
                        pn, pnk = C.bank()
                        pd, pdk = C.bank()
                        srcs = []
                        for ct in range(2):
                            srcs.append((vctx[:, jc, ct, vcol], ec[0][:, ct * 512:(ct + 1) * 512], vctx_k, ec[1], 0, 512))
                        for kt in kts:
                            eh, eo, q0 = einfo[kt]
                            q1 = min(kt + 1, 4 * qh + 3)
                            nq = (q1 - q0 + 1) * 128
                            srcs.append((vt[:, kt, vcol], eh[0][:, eo:eo + nq], vtm[1], eh[1], (q0 - 4 * qh) * 128, nq))
                        for si, (vv, ee, vk_, ek_, c0, cn) in enumerate(srcs):
                            C.mm(pn[prow, c0:c0 + cn], vv, ee, si == 0, si == len(srcs) - 1, [vk_, ek_], [pnk])
                        for si, (vv, ee, vk_, ek_, c0, cn) in enumerate(srcs):
                            C.mm(pd[prow, c0:c0 + cn], ones_b[:, 0:64], ee, si == 0, si == len(srcs) - 1, [ones_k, ek_], [pdk])
                        BP.put(el[0], el[1], ec)
                        s1 = FP.get()
                        C.actv(s1[0][prow, 0:512], pd[prow, :], AF.Ln, [pdk, sinkEk], [s1[1]],
                               bias=sinkE[prow, jsa, hq // 2:hq // 2 + 1])
                        C.actv(s1[0][prow, 0:512], s1[0][prow, 0:512], AF.Exp, [s1[1]], [s1[1]], scale=-1.0)
                        C.tt(mixb[prow, hq // 2, qh * 512:(qh + 1) * 512], pn[prow, :], s1[0][prow, 0:512], ALU.mult,
                             [pnk, s1[1]], [mixk[hq // 2]])
                        FP.put(s1)
            BP.put(qT)
        BP.put(kTs[0], kTs[1])
        BP.put(vtm)
        if sample:
            FP.put(rp[0], rp[1])
        mark(f'L{l}h{half} even:lru')
        lbdt = [BP.get(), BP.get()]
        for ri in range(2):
            C.dma(C.pool, lbdt[ri][0][:], d["lru_bd"][:, jl, ri, :], wt=lbdt[ri][1])
        for c in range(4):
            if c % 2 == 0:
                wL, wLk = load_w(d["ev_w_in"][jw, 0:1024, EV_LRU + c * 256:EV_LRU + c * 256 + 512], 8, 512)
            cb = (c % 2) * 256
            psx = proj_fm(wL, wLk, cb)
            xr = FP.get()
            for lt in range(2):
                C.actv(xr[0][:, lt * TT:(lt + 1) * TT], psx[lt][0][:], AF.Copy, [psx[lt][1]], [xr[1]])
            xc = FP.get()
            C.actv(xc[0][:], xr[0][:], AF.Identity, [xr[1], lcw_k, lcb_k], [xc[1]], bias=lcb[:, jsb, c:c + 1],
                   scale=lcw[:, jsb, 2, c:c + 1])
            xrv = xr[0][:].rearrange("p (s t) -> p s t", s=nseq)
            xcv = xc[0][:].rearrange("p (s t) -> p s t", s=nseq)
            for tap, off in ((0, -2), (1, -1), (3, 1)):
                if off < 0:
                    o_sl, i_sl = slice(-off, Ls), slice(0, Ls + off)
                else:
                    o_sl, i_sl = slice(0, Ls - off), slice(off, Ls)
                C.stt(xcv[:, :, o_sl], xrv[:, :, i_sl], lcw[:, jsb, tap, c:c + 1], xcv[:, :, o_sl], ALU.mult, ALU.add,
                      [xr[1], xc[1], lcw_k], [xc[1]])
            FP.put(xr)
            xcb = BP.get()
            C.actv(xcb[0][:], xc[0][:], AF.Copy, [xc[1]], [xcb[1]])
            R, G, A = {}, {}, {}
            for dr in range(2):
                gr = []
                for ri in range(2):
                    st_ = FP.get()
                    bias_t, bias_k = ((lbr, lbr_k), (lbi, lbi_k))[ri]
                    mi = (dr * 4 + c) * 128
                    for lt in range(2):
                        pb, pk = C.bank()
                        C.mm(pb[:], lbdt[ri][0][:, mi:mi + 128], xcb[0][:, lt * TT:(lt + 1) * TT], True, True,
                             [lbdt[ri][1], xcb[1]], [pk])
                        C.actv(st_[0][:, lt * TT:(lt + 1) * TT], pb[:], AF.Sigmoid, [pk, bias_k], [st_[1]],
                               bias=bias_t[:, jsc, dr, c:c + 1])
                    gr.append(st_)
                R[dr], G[dr] = gr
                A[dr] = FP.get()
            for dr in range(2):
                C.tt(G[dr][0][:], G[dr][0][:], xc[0][:], ALU.mult, [G[dr][1], xc[1]], [G[dr][1]])
            for dr in range(2):
                C.actv(R[dr][0][:], R[dr][0][:], AF.Identity, [R[dr][1], lcstk, zero_k], [R[dr][1]], bias=zerot[:, 0:1],
                       scale=lcst[:, jsc2, dr, c:c + 1])
            for dr in range(2):
                C.actv(A[dr][0][:], R[dr][0][:], AF.Exp, [R[dr][1]], [A[dr][1]])
            for dr in range(2):
                C.tt(R[dr][0][:], A[dr][0][:], A[dr][0][:], ALU.mult, [A[dr][1]], [R[dr][1]])
            for dr in range(2):
                C.actv(R[dr][0][:], R[dr][0][:], AF.Sqrt, [R[dr][1], one_k], [R[dr][1]], bias=onet[:, 0:1], scale=-1.0)
            for dr in range(2):
                C.tt(G[dr][0][:], G[dr][0][:], R[dr][0][:], ALU.mult, [G[dr][1], R[dr][1]], [G[dr][1]])
            for dr in range(2):
                r_, g_, a_ = R[dr], G[dr], A[dr]
                for s in range(nseq):
                    sl = slice(s * Ls, (s + 1) * Ls)
                    if sample:
                        init = h0[:, jsa, dr, c:c + 1]
                        rd = [a_[1], g_[1], h0_k]
                    else:
                        init = 0.0
                        rd = [a_[1], g_[1]]
                    if dr == 0:
                        C.op(C.dve, rd, [r_[1]], "tensor_tensor_scan", out=r_[0][:, sl], data0=a_[0][:, sl],
                             data1=g_[0][:, sl], initial=init, op0=ALU.mult, op1=ALU.add)
                    else:
                        C.op(C.dve, rd, [r_[1]], "tensor_tensor_scan", out=r_[0][:, sl][:, ::-1],
                             data0=a_[0][:, sl][:, ::-1], data1=g_[0][:, sl][:, ::-1], initial=init,
                             op0=ALU.mult, op1=ALU.add)
                if not sample:
                    rv = r_[0][:].rearrange("p (s t) -> p s t", s=4)
                    col = Ls - 1 if dr == 0 else 0
                    C.cp(lst[:, jo, dr, c, :], rv[:, :, col], [r_[1]], [lstk])
            hsum = R[0]
            C.tt(hsum[0][:], R[0][0][:], R[1][0][:], ALU.add, [R[0][1], R[1][1]], [hsum[1]])
            FP.put(R[1], G[0], G[1], A[0], A[1])
            FP.put(xc)
            BP.put(xcb)
            psg = proj_fm(wL, wLk, cb + 128)
            gg = FP.get()
            for lt in range(2):
                C.actv(gg[0][:, lt * TT:(lt + 1) * TT], psg[lt][0][:], AF.Gelu, [psg[lt][1]], [gg[1]])
            C.tt(mixb[:, 4 + c, :], hsum[0][:], gg[0][:], ALU.mult, [hsum[1], gg[1]], [mixk[4 + c]])
            FP.put(hsum, gg)
        BP.put(lbdt[0], lbdt[1])
        w_out_phase(d["ev_w_out"], jw, half)

    def even_mixer_sample(l):
        j = l // 2
        half = 1
        mark(f'L{l}h1 even:attn')
        scale = 0.125
        Ls = 1024
        wKV, wKVk = load_w(d["s_kv_w"][j, 0:1024, 0:320], 8, 320)
        wS, wSk = load_w(d["s_ev_w"][j, 0:1024, 0:512], 8, 512)
        vtm = BP.get()
        vt = vtm[0][:].rearrange("p (a b) -> p a b", a=8)
        for tl in range(8):
            pb, pk = C.bank()
            for k in range(8):
                C.mm(pb[:, 0:64], hbuf[:, k, tl * 128:(tl + 1) * 128], wKV[:, k, 256:320], k == 0, k == 7,
                     [wKVk, hk[tl // 4]], [pk])
            C.actv(vt[:, tl, 0:64], pb[:, 0:64], AF.Copy, [pk], [vtm[1]])
        rp = [FP.get(), FP.get()]
        for i_ in range(2):
            C.dma(C.sp, rp[i_][0][:], d["rope"][:, i_, :], wt=rp[i_][1])

        def roped(dst, ps, pp):
            for lt in range(2):
                sl = slice(lt * TT, (lt + 1) * TT)
                s1 = FP.get()
                C.tt(s1[0][:, 0:TT], ps[lt][0][:], rp[0][0][:, sl], ALU.mult, [ps[lt][1], rp[0][1]], [s1[1]])
                C.tt(s1[0][:, TT:2 * TT], pp[lt][0][:], rp[1][0][:, sl], ALU.mult, [pp[lt][1], rp[1][1]], [s1[1]])
                C.tt(dst[0][:, sl], s1[0][:, 0:TT], s1[0][:, TT:2 * TT], ALU.add, [s1[1]], [dst[1]])
                FP.put(s1)

        kT = BP.get()
        roped(kT, proj_fm(wKV, wKVk, 0), proj_fm(wKV, wKVk, 128))
        qT = BP.get()
        roped(qT, proj_fm(wS, wSk, 0), proj_fm(wS, wSk, 128))
        FP.put(rp[0], rp[1])
        vcol = slice(0, 64)
        for hq in range(2):
            prow = slice(hq * 64, hq * 64 + 64)
            for qh in range(2):
                el = [BP.get(), BP.get()]
                ec = BP.get()
                kts = [kt for kt in range(4 * qh - 1, 4 * qh + 5) if 0 <= kt <= 7]
                einfo = {}
                esi, eoff = 0, 0
                for kt in kts:
                    q0 = max(kt - 1, 4 * qh)
                    q1 = min(kt + 1, 4 * qh + 3)
                    nq = (q1 - q0 + 1) * 128
                    m0 = (q0 - (kt - 1)) * 128
                    if eoff + nq > 1024:
                        esi += 1
                        eoff = 0
                    pb, pk = C.bank()
                    C.mm(pb[:, 0:nq], kT[0][prow, kt * 128:(kt + 1) * 128], qT[0][prow, q0 * 128:q0 * 128 + nq],
                         True, True, [kT[1], qT[1]], [pk])
                    eh = el[esi]
                    eo = eoff
                    C.actv(eh[0][:, eo:eo + nq], pb[:, 0:nq], AF.Exp, [pk], [eh[1]], scale=scale)
                    C.tt(eh[0][:, eo:eo + nq], eh[0][:, eo:eo + nq], amask[:, m0:m0 + nq], ALU.mult,
                         [eh[1], amask_k], [eh[1]])
                    einfo[kt] = (eh, eo, q0, nq)
                    eoff += nq
                for ct in range(2):
                    pb, pk = C.bank()
                    C.mm(pb[:], kctx[prow, j, ct * 128:(ct + 1) * 128], qT[0][prow, qh * 512:(qh + 1) * 512],
                         True, True, [kctx_k, qT[1]], [pk])
                    C.actv(ec[0][:, ct * 512:(ct + 1) * 512], pb[:], AF.Exp, [pk], [ec[1]], scale=scale)
                pn, pnk = C.bank()
                pd, pdk = C.bank()
                srcs = []
                for ct in range(2):
                    srcs.append((vctx[:, j, ct, :], ec[0][:, ct * 512:(ct + 1) * 512], vctx_k, ec[1], 0, 512))
                for kt in kts:
                    eh, eo, q0, nq = einfo[kt]
                    srcs.append((vt[:, kt, vcol], eh[0][:, eo:eo + nq], vtm[1], eh[1], (q0 - 4 * qh) * 128, nq))
                for si, (vv, ee, vk_, ek_, c0, cn) in enumerate(srcs):
                    C.mm(pn[prow, c0:c0 + cn], vv, ee, si == 0, si == len(srcs) - 1, [vk_, ek_], [pnk])
                for si, (vv, ee, vk_, ek_, c0, cn) in enumerate(srcs):
                    C.mm(pd[prow, c0:c0 + cn], ones_b[:, 0:64], ee, si == 0, si == len(srcs) - 1, [ones_k, ek_], [pdk])
                BP.put(el[0], el[1], ec)
                s1 = FP.get()
                C.actv(s1[0][prow, 0:512], pd[prow, :], AF.Ln, [pdk, ssinkEk], [s1[1]], bias=ssinkE[prow, j:j + 1])
                C.actv(s1[0][prow, 0:512], s1[0][prow, 0:512], AF.Exp, [s1[1]], [s1[1]], scale=-1.0)
                C.tt(mixb[prow, 0, qh * 512:(qh + 1) * 512], pn[prow, :], s1[0][prow, 0:512], ALU.mult,
                     [pnk, s1[1]], [mixk[0]])
                FP.put(s1)
        BP.put(qT, kT, vtm)
        mark(f'L{l}h1 even:lru')
        lbdt = [BP.get(), BP.get()]
        for ri in range(2):
            C.dma(C.pool, lbdt[ri][0][:, 0:256], d["s_lbd"][:, j, ri, :], wt=lbdt[ri][1])
        psx = proj_fm(wS, wSk, 256)
        xr = FP.get()
        for lt in range(2):
            C.actv(xr[0][:, lt * TT:(lt + 1) * TT], psx[lt][0][:], AF.Copy, [psx[lt][1]], [xr[1]])
        xc = FP.get()
        C.actv(xc[0][:], xr[0][:], AF.Identity, [xr[1], slcw_k, slcb_k], [xc[1]], bias=slcb[:, j:j + 1],
               scale=slcw[:, j, 2:3])
        for tap, off in ((0, -2), (1, -1), (3, 1)):
            if off < 0:
                o_sl, i_sl = slice(-off, Ls), slice(0, Ls + off)
            else:
                o_sl, i_sl = slice(0, Ls - off), slice(off, Ls)
            C.stt(xc[0][:, o_sl], xr[0][:, i_sl], slcw[:, j, tap:tap + 1], xc[0][:, o_sl], ALU.mult, ALU.add,
                  [xr[1], xc[1], slcw_k], [xc[1]])
        FP.put(xr)
        xcb = BP.get()
        C.actv(xcb[0][:], xc[0][:], AF.Copy, [xc[1]], [xcb[1]])
        R, G, A = {}, {}, {}
        for dr in range(2):
            gr = []
            for ri in range(2):
                st_ = FP.get()
                bias_t, bias_k = ((slbr, slbr_k), (slbi, slbi_k))[ri]
                for lt in range(2):
                    pb, pk = C.bank()
                    C.mm(pb[:], lbdt[ri][0][:, dr * 128:(dr + 1) * 128], xcb[0][:, lt * TT:(lt + 1) * TT], True, True,
                         [lbdt[ri][1], xcb[1]], [pk])
                    C.actv(st_[0][:, lt * TT:(lt + 1) * TT], pb[:], AF.Sigmoid, [pk, bias_k], [st_[1]],
                           bias=bias_t[:, j, dr:dr + 1])
                gr.append(st_)
            R[dr], G[dr] = gr
            A[dr] = FP.get()
        for dr in range(2):
            C.tt(G[dr][0][:], G[dr][0][:], xc[0][:], ALU.mult, [G[dr][1], xc[1]], [G[dr][1]])
        for dr in range(2):
            C.actv(R[dr][0][:], R[dr][0][:], AF.Identity, [R[dr][1], slcstk, zero_k], [R[dr][1]], bias=zerot[:, 0:1],
                   scale=slcst[:, j, dr:dr + 1])
        for dr in range(2):
            C.actv(A[dr][0][:], R[dr][0][:], AF.Exp, [R[dr][1]], [A[dr][1]])
        for dr in range(2):
            C.tt(R[dr][0][:], A[dr][0][:], A[dr][0][:], ALU.mult, [A[dr][1]], [R[dr][1]])
        for dr in range(2):
            C.actv(R[dr][0][:], R[dr][0][:], AF.Sqrt, [R[dr][1], one_k], [R[dr][1]], bias=onet[:, 0:1], scale=-1.0)
        for dr in range(2):
            C.tt(G[dr][0][:], G[dr][0][:], R[dr][0][:], ALU.mult, [G[dr][1], R[dr][1]], [G[dr][1]])
        for dr in range(2):
            r_, g_, a_ = R[dr], G[dr], A[dr]
            rd = [a_[1], g_[1], sh0_k]
            init = sh0[:, j, dr:dr + 1]
            if dr == 0:
                C.op(C.dve, rd, [r_[1]], "tensor_tensor_scan", out=r_[0][:], data0=a_[0][:], data1=g_[0][:],
                     initial=init, op0=ALU.mult, op1=ALU.add)
            else:
                C.op(C.dve, rd, [r_[1]], "tensor_tensor_scan", out=r_[0][:, ::-1], data0=a_[0][:, ::-1],
                     data1=g_[0][:, ::-1], initial=init, op0=ALU.mult, op1=ALU.add)
        hsum = R[0]
        C.tt(hsum[0][:], R[0][0][:], R[1][0][:], ALU.add, [R[0][1], R[1][1]], [hsum[1]])
        FP.put(R[1], G[0], G[1], A[0], A[1], xc)
        BP.put(xcb)
        psg = proj_fm(wS, wSk, 384)
        gg = FP.get()
        for lt in range(2):
            C.actv(gg[0][:, lt * TT:(lt + 1) * TT], psg[lt][0][:], AF.Gelu, [psg[lt][1]], [gg[1]])
        C.tt(mixb[:, 4, :], hsum[0][:], gg[0][:], ALU.mult, [hsum[1], gg[1]], [mixk[4]])
        FP.put(hsum, gg)
        BP.put(lbdt[0], lbdt[1])
        exchange_start()

    hid = {L: (sb(f"hid{L}", [64, L], BF16), Tk(f"hid{L}")) for L in (256, 1024)}
    fb = sb("fb", [64, 2], F32)
    fbk = Tk("fb")

    def sin_reduced(dst_ap, dstk, pb_ap, pk, n_, scale_ap, bias_ap, extra):
        a1 = FP.get()
        rs = slice(0, 64)
        A = a1[0][rs, 0:n_]
        Bv = a1[0][rs, 512:512 + n_]
        C.ts(A, pb_ap, scale_ap, bias_ap, ALU.mult, ALU.add, [pk] + extra, [a1[1]])
        C.ts(Bv, A, 1.0 / TWO_PI, MAGIC, ALU.mult, ALU.add, [a1[1]], [a1[1]])
        C.ts(Bv, Bv, MAGIC, TWO_PI, ALU.subtract, ALU.mult, [a1[1]], [a1[1]])
        C.tt(A, A, Bv, ALU.subtract, [a1[1]], [a1[1]])
        C.ts(A, A, 3.1415925, -3.1415925, ALU.min, ALU.max, [a1[1]], [a1[1]])
        C.actv(dst_ap, A, AF.Sin, [a1[1]], [dstk])
        FP.put(a1)

    def hyena_prep(l):
        j = l // 2
        C.tt(fb[:], hfreq[:, j, :], hb12[:, j, :], ALU.mult, [hfreq_k, hb12_k], [fbk])
        for L in (256, 1024):
            ht, htk = hid[L]
            zf = FP.get()
            C.dma(C.sp, zf[0][0:33, 0:L], d[f"zf_{L}"], wt=zf[1])
            h1 = FP.get()
            for o in range(0, L, 512):
                n_ = min(512, L - o)
                pb, pk = C.bank()
                C.mm(pb[0:64, 0:n_], hw1[:, j, :], zf[0][0:33, o:o + n_], True, True, [hw1_k, zf[1]], [pk])
                sin_reduced(h1[0][0:64, o:o + n_], h1[1], pb[0:64, 0:n_], pk, n_, hfreq[:, j, 0:1], fb[:, 0:1], [hfreq_k, fbk])
            for o in range(0, L, 512):
                n_ = min(512, L - o)
                pb, pk = C.bank()
                C.mm(pb[0:64, 0:n_], hw2[:, j, :], h1[0][0:64, o:o + n_], True, True, [hw2_k, h1[1]], [pk])
                sin_reduced(ht[:, o:o + n_], htk, pb[0:64, 0:n_], pk, n_, hfreq[:, j, 1:2], fb[:, 1:2], [hfreq_k, fbk])
            FP.put(zf, h1)

    def hyena_filter(l, c, n, L, tab, tabk_, kr, ki):
        j = l // 2
        nt_ = L // 128
        ht, htk = hid[L]
        ph, phk = hyph[L]
        nt, ntk = ntn[L]
        col0 = (c * 2 + n) * 256
        if L == 1024:
            w3_src = d["s_hw3"][:, j, n * 256:(n + 1) * 256]
            ld_src = d["s_hld"][:, j * 512 + n * 256:j * 512 + (n + 1) * 256]
        else:
            w3_src = d["hy_w3"][:, j, col0:col0 + 256]
            ld_src = d["hy_ld"][:, j * 2048 + col0:j * 2048 + col0 + 256]
        ks, kd = BP.get(), BP.get()
        ksv = ks[0][:].rearrange("p (a b) -> p a b", a=8)
        kdv = kd[0][:].rearrange("p (a b) -> p a b", a=8)
        w3 = BP.get()
        C.dma(C.pool, w3[0][0:64, 0:256], w3_src, wt=w3[1])
        eld = FP.get()
        C.dma(C.sp, eld[0][:, 0:256], ld_src.partition_broadcast(128), wt=eld[1])
        C.actv(eld[0][:, 0:256], eld[0][:, 0:256], AF.Exp, [eld[1]], [eld[1]])
        pss_b = C.bank(hold=True)
        pss, pssk = pss_b
        dec = FP.get()
        sq = BP.get()
        for tl in range(nt_):
            o = (tl % 4) * 256
            pb, pk = C.bank()
            C.mm(pb[:, 0:256], ht[:, tl * 128:(tl + 1) * 128], w3[0][0:64, 0:256], True, True, [htk, w3[1]], [pk])
            C.actv(dec[0][:, o:o + 256], eld[0][:, 0:256], AF.Exp, [eld[1], ntk], [dec[1]], scale=nt[:, tl:tl + 1])
            C.tt(dec[0][:, o:o + 256], pb[:, 0:256], dec[0][:, o:o + 256], ALU.mult, [pk, dec[1]], [dec[1]])
            C.actv(sq[0][:, o:o + 256], dec[0][:, o:o + 256], AF.Square, [dec[1]], [sq[1]])
            C.mm(pss[:, 0:256], ones_b[:], sq[0][:, o:o + 256], tl == 0, tl == nt_ - 1, [ones_k, sq[1]], [pssk])
            C.tt(ksv[:, tl, :], dec[0][:, o:o + 128], dec[0][:, o + 128:o + 256], ALU.add, [dec[1]], [ks[1]])
            C.tt(kdv[:, tl, :], dec[0][:, o:o + 128], dec[0][:, o + 128:o + 256], ALU.subtract, [dec[1]], [kd[1]])
        BP.put(sq, w3)
        FP.put(eld)
        n1 = dec
        C.cp(n1[0][:, 0:128], pss[:, 0:128], [pssk], [n1[1]])
        C.tt(n1[0][:, 0:128], n1[0][:, 0:128], pss[:, 128:256], ALU.add, [n1[1], pssk], [n1[1]])
        C.actv(n1[0][:, 0:128], n1[0][:, 0:128], AF.Sqrt, [n1[1], eps_k], [n1[1]], bias=epst[:, 0:1], scale=1.0)
        C.recip(n1[0][:, 128:256], n1[0][:, 0:128], [n1[1]], [n1[1]])
        nrm = n1[0][:, 128:256]
        C.release(pss_b)
        krv = kr[0][:].rearrange("p (a b) -> p a b", a=8)
        kiv = ki[0][:].rearrange("p (a b) -> p a b", a=8)
        for ft in range(nt_):
            pg, pgk = C.bank()
            for tl in range(nt_):
                ta, tak = tab.get(0, tl, ft * 128, (ft + 1) * 128)
                C.mm(pg[:, 0:128], ta, ksv[:, tl, :], tl == 0, tl == nt_ - 1, [tak, ks[1]], [pgk])
            for tl in range(nt_):
                ta, tak = tab.get(1, tl, ft * 128, (ft + 1) * 128)
                C.mm(pg[:, 128:256], ta, kdv[:, tl, :], tl == 0, tl == nt_ - 1, [tak, kd[1]], [pgk])
            ca = ph[:, 0, ft:ft + 1]
            sa = ph[:, 1, ft:ft + 1]
            g1 = n1[0][:, 256:512]
            C.ts(g1[:, 0:128], pg[:, 128:256], sa, None, ALU.mult, None, [pgk, phk], [n1[1]])
            C.stt(g1[:, 0:128], pg[:, 0:128], ca, g1[:, 0:128], ALU.mult, ALU.add, [pgk, phk, n1[1]], [n1[1]])
            C.ts(g1[:, 128:256], pg[:, 128:256], ca, None, ALU.mult, None, [pgk, phk], [n1[1]])
            C.stt(g1[:, 128:256], pg[:, 0:128], sa, g1[:, 128:256], ALU.mult, ALU.subtract, [pgk, phk, n1[1]], [n1[1]])
            if L == 256:
                for s_ in range(4):
                    C.stt(krv[:, ft * 4 + s_, :], g1[:, 0:128], 1.0 / L, nrm, ALU.mult, ALU.mult, [n1[1]], [kr[1]])
                    C.stt(kiv[:, ft * 4 + s_, :], g1[:, 128:256], 1.0 / L, nrm, ALU.mult, ALU.mult, [n1[1]], [ki[1]])
            else:
                C.stt(krv[:, ft, :], g1[:, 0:128], 1.0 / L, nrm, ALU.mult, ALU.mult, [n1[1]], [kr[1]])
                C.stt(kiv[:, ft, :], g1[:, 128:256], 1.0 / L, nrm, ALU.mult, ALU.mult, [n1[1]], [ki[1]])
        FP.put(dec)
        BP.put(ks, kd)

    def hyena_conv(zsrc, o0, L, tab, tabk_, kr, ki, cb):
        nt_ = L // 128
        krv = kr[0][:].rearrange("p (a b) -> p a b", a=8)
        kiv = ki[0][:].rearrange("p (a b) -> p a b", a=8)
        ztm = BP.get()
        ztv = ztm[0][:].rearrange("p (a b) -> p a b", a=8)
        for g0 in range(0, nt_, 4):
            gn = min(4, nt_ - g0)
            pb, pk = C.bank()
            for tl in range(g0, g0 + gn):
                C.op(C.pe, [zsrc[1], ident_k], [pk], "transpose", out=pb[:, (tl - g0) * 128:(tl - g0 + 1) * 128],
                     in_=zsrc[0][:, o0 + tl * 128:o0 + (tl + 1) * 128], identity=ident[:])
            C.actv(ztv[:, g0:g0 + gn, :], pb[:, 0:gn * 128].rearrange("p (a b) -> p a b", a=gn), AF.Copy, [pk], [ztm[1]])
        yr, yi = BP.get(), BP.get()
        yrv = yr[0][:].rearrange("p (a b) -> p a b", a=8)
        yiv = yi[0][:].rearrange("p (a b) -> p a b", a=8)
        tq = FP.get()
        for g0 in range(0, nt_, 2):
            gn = min(2, nt_ - g0)
            pz, pzk = C.bank()
            for fi in range(gn):
                ft = g0 + fi
                for cs_ in range(2):
                    for tl in range(nt_):
                        ta, tak = tab.get(cs_, tl, ft * 128, (ft + 1) * 128)
                        C.mm(pz[:, fi * 256 + cs_ * 128:fi * 256 + (cs_ + 1) * 128],
                             ta, ztv[:, tl, :], tl == 0, tl == nt_ - 1, [tak, ztm[1]], [pzk])
            pzv = pz[:, 0:gn * 256].rearrange("p (a b c) -> p a b c", a=gn, b=2)
            zc = pzv[:, :, 0, :]
            zs = pzv[:, :, 1, :]
            kr_ = krv[:, g0:g0 + gn, :]
            ki_ = kiv[:, g0:g0 + gn, :]
            ob = ((g0 // 2) % 2) * 512
            v1 = tq[0][:, ob:ob + gn * 128].rearrange("p (a b) -> p a b", a=gn)
            v2 = tq[0][:, ob + 256:ob + 256 + gn * 128].rearrange("p (a b) -> p a b", a=gn)
            C.tt(v1, zc, kr_, ALU.mult, [pzk, kr[1]], [tq[1]])
            C.tt(v2, zs, ki_, ALU.mult, [pzk, ki[1]], [tq[1]])
            C.tt(yrv[:, g0:g0 + gn, :], v1, v2, ALU.add, [tq[1]], [yr[1]])
            C.tt(v1, zs, kr_, ALU.mult, [pzk, kr[1]], [tq[1]])
            C.tt(v2, zc, ki_, ALU.mult, [pzk, ki[1]], [tq[1]])
            C.tt(yiv[:, g0:g0 + gn, :], v1, v2, ALU.subtract, [tq[1]], [yi[1]])
        FP.put(tq)
        BP.put(ztm)
        for th in range(0, L, 512):
            wd = min(512, L - th)
            pb, pk = C.bank()
            for ft in range(nt_):
                ta, tak = tab.get(0, ft, th, th + wd)
                C.mm(pb[:, 0:wd], yrv[:, ft, :], ta, ft == 0, False, [yr[1], tak], [pk])
                ta, tak = tab.get(1, ft, th, th + wd)
                C.mm(pb[:, 0:wd], yiv[:, ft, :], ta, False, ft == nt_ - 1, [yi[1], tak], [pk])
            cb(th, pb, pk, wd)
        BP.put(yr, yi)

    def hyena_conv_p(zsrc, tab, kr, ki, cb):
        L = 256
        krv = kr[0][:].rearrange("p (a b) -> p a b", a=8)
        kiv = ki[0][:].rearrange("p (a b) -> p a b", a=8)
        ztm = BP.get()
        ztv = ztm[0][:].rearrange("p (t b) -> p t b", t=2)
        for tl in range(2):
            pb, pk = C.bank()
            for s_ in range(4):
                C.op(C.pe, [zsrc[1], ident_k], [pk], "transpose", out=pb[:, s_ * 128:(s_ + 1) * 128],
                     in_=zsrc[0][:, s_ * L + tl * 128:s_ * L + (tl + 1) * 128], identity=ident[:])
            C.actv(ztv[:, tl, :], pb[:], AF.Copy, [pk], [ztm[1]])
        yr, yi = BP.get(), BP.get()
        yrv = yr[0][:].rearrange("p (a b) -> p a b", a=8)
        yiv = yi[0][:].rearrange("p (a b) -> p a b", a=8)
        tq = FP.get()
        for ft in range(2):
            pzs = []
            for cs_ in range(2):
                pz, pzk = C.bank()
                for tl in range(2):
                    ta, tak = tab.get(cs_, tl, ft * 128, (ft + 1) * 128)
                    C.mm(pz[:], ta, ztv[:, tl, :], tl == 0, tl == 1, [tak, ztm[1]], [pzk])
                pzs.append((pz, pzk))
            (zc, zck), (zs, zsk) = pzs
            kr_ = kr[0][:, ft * 512:(ft + 1) * 512]
            ki_ = ki[0][:, ft * 512:(ft + 1) * 512]
            v1 = tq[0][:, 0:512]
            v2 = tq[0][:, 512:1024]
            C.tt(v1, zc[:], kr_, ALU.mult, [zck, kr[1]], [tq[1]])
            C.tt(v2, zs[:], ki_, ALU.mult, [zsk, ki[1]], [tq[1]])
            C.tt(yr[0][:, ft * 512:(ft + 1) * 512], v1, v2, ALU.add, [tq[1]], [yr[1]])
            C.tt(v1, zs[:], kr_, ALU.mult, [zsk, kr[1]], [tq[1]])
            C.tt(v2, zc[:], ki_, ALU.mult, [zck, ki[1]], [tq[1]])
            C.tt(yi[0][:, ft * 512:(ft + 1) * 512], v1, v2, ALU.subtract, [tq[1]], [yi[1]])
        FP.put(tq)
        BP.put(ztm)
        for pair in range(2):
            pb, pk = C.bank()
            for si in range(2):
                s_ = pair * 2 + si
                for ft in range(2):
                    ta, tak = tab.get(0, ft, 0, L)
                    C.mm(pb[:, si * L:(si + 1) * L], yrv[:, ft * 4 + s_, :], ta, ft == 0, False, [yr[1], tak], [pk])
                    ta, tak = tab.get(1, ft, 0, L)
                    C.mm(pb[:, si * L:(si + 1) * L], yiv[:, ft * 4 + s_, :], ta, False, ft == 1, [yi[1], tak], [pk])
            cb(pair * 512, pb, pk, 512)
        BP.put(yr, yi)

    def odd_mixer(l, half):
        j = l // 2
        sample = half == 1
        nseq = 1 if sample else 4
        L = 1024 if sample else 256
        nt_ = L // 128
        mark(f'L{l}h{half} odd:fnet')
        if sample:
            ring_allowed[0] = [0, 1]
        if sample:
            wA, wAk = load_w(d["s_od_w"][j, 0:1024, 0:512], 8, 512)
        else:
            wA, wAk = load_w(d["od_w_in"][j, 0:1024, 0:512], 8, 512)
        if sample:
            bigtab.load("fn_1024")
            ftab = bigtab
        else:
            ftab = SmallTab(fn256, fn256_k)
        sc_ = 1.0 / math.sqrt(128.0 * L)
        for g in ([0] if sample else range(4)):
            ps = proj_fm(wA, wAk, g * 128)
            fT = BP.get()
            for lt in range(2):
                C.actv(fT[0][:, lt * TT:(lt + 1) * TT], ps[lt][0][:], AF.Copy, [ps[lt][1]], [fT[1]])
            ab = [BP.get(), BP.get()]
            for s in range(nseq):
                for tl in range(nt_):
                    pb, pk = C.bank()
                    C.mm(pb[:, 0:256], fT[0][:, s * L + tl * 128:s * L + (tl + 1) * 128], fnch[:], True, True,
                         [fT[1], fnch_k], [pk])
                    dst = ab[tl // 4][0][:, (tl % 4) * 256:(tl % 4 + 1) * 256]
                    if tl % 2 == 0:
                        C.actv(dst, pb[:, 0:256], AF.Copy, [pk], [ab[tl // 4][1]])
                    else:
                        C.cp(dst, pb[:, 0:256], [pk], [ab[tl // 4][1]])
                for th in range(0, L, 512):
                    wd = min(512, L - th)
                    pb, pk = C.bank()
                    for tl in range(nt_):
                        for cs_ in range(2):
                            o = (tl % 4) * 256 + cs_ * 128
                            ta, tak = ftab.get(cs_, tl, th, th + wd)
                            C.mm(pb[:, 0:wd], ab[tl // 4][0][:, o:o + 128], ta,
                                 tl == 0 and cs_ == 0, tl == nt_ - 1 and cs_ == 1, [ab[tl // 4][1], tak], [pk])
                    C.actv(mixb[:, g, s * L + th:s * L + th + wd], pb[:, 0:wd], AF.Copy, [pk], [mixk[g]], scale=sc_)
            BP.put(fT, ab[0], ab[1])
        mark(f'L{l}h{half} odd:hyena')
        if sample:
            bigtab.load("hy_1024")
            htab, htabk = bigtab, None
        else:
            htab, htabk = SmallTab(hy256, hy256_k), None
        for c in ([0] if sample else range(4)):
            mark(f'hy c{c} filt')
            K = [(FP.get(), FP.get()) for _ in range(2)]
            for n in range(2):
                hyena_filter(l, c, n, L, htab, htabk, K[n][0], K[n][1])
            if sample:
                wB, wBk, wbase = wA, wAk, 128
            elif c == 0:
                wB, wBk = load_w(d["od_w_in"][j, 0:1024, 512:1024], 8, 512)
                wbase = 0
            elif c == 1:
                wB, wBk = load_w(d["od_w_in"][j, 0:1024, 512 + 384:512 + 768], 8, 384)
                wbase = 0
            elif c == 2:
                wB, wBk = load_w(d["od_w_in"][j, 0:1024, 512 + 768:512 + 1152], 8, 384)
                wbase = 0
            else:
                wB, wBk = load_w(d["od_w_in"][j, 0:1024, 512 + 1152:512 + 1536], 8, 384)
                wbase = 0
            mark(f'hy c{c} proj')
            sig = []
            for si in range(3):
                ps = proj_fm(wB, wBk, wbase + si * 128)
                raw = FP.get()
                for lt in range(2):
                    C.actv(raw[0][:, lt * TT:(lt + 1) * TT], ps[lt][0][:], AF.Copy, [ps[lt][1]], [raw[1]])
                cv = FP.get()
                ch = si * 4 + c
                if sample:
                    cw_t, cw_k, cb_t, cb_k, ch = shcw, shcw_k, shcb, shcb_k, si
                else:
                    cw_t, cw_k, cb_t, cb_k = hcw, hcw_k, hcb, hcb_k
                C.actv(cv[0][:], raw[0][:], AF.Identity, [raw[1], cw_k, cb_k], [cv[1]], bias=cb_t[:, j, ch:ch + 1],
                       scale=cw_t[:, j, 1, ch:ch + 1])
                rv = raw[0][:].rearrange("p (s t) -> p s t", s=nseq)
                cvv = cv[0][:].rearrange("p (s t) -> p s t", s=nseq)
                for tap, off in ((0, -1), (2, 1)):
                    if off < 0:
                        o_sl, i_sl = slice(-off, L), slice(0, L + off)
                    else:
                        o_sl, i_sl = slice(0, L - off), slice(off, L)
                    C.stt(cvv[:, :, o_sl], rv[:, :, i_sl], cw_t[:, j, tap, ch:ch + 1], cvv[:, :, o_sl], ALU.mult, ALU.add,
                          [raw[1], cv[1], cw_k], [cv[1]])
                FP.put(raw)
                sig.append(cv)
            zv, x1, x2 = sig
            mark(f'hy c{c} conv')
            for n, gate in enumerate((x1, x2)):
                for s in range(nseq if sample else 1):
                    def cb(th, pb, pk, wd, s=s, n=n, gate=gate):
                        sl = slice(s * L + th, s * L + th + wd)
                        tmp = FP.get()
                        hb_ap, hb_k = (shbias[:, j, n:n + 1], shbias_k) if sample else (hbias[:, j, n, c:c + 1], hbias_k)
                        C.stt(tmp[0][:, 0:wd], zv[0][:, sl], hb_ap, pb[:, 0:wd], ALU.mult, ALU.add,
                              [zv[1], hb_k, pk], [tmp[1]])
                        if n == 0:
                            C.tt(zv[0][:, sl], tmp[0][:, 0:wd], gate[0][:, sl], ALU.mult, [tmp[1], gate[1]], [zv[1]])
                        else:
                            C.tt(mixb[:, 4 + c, sl], tmp[0][:, 0:wd], gate[0][:, sl], ALU.mult, [tmp[1], gate[1]],
                                 [mixk[4 + c]])
                        FP.put(tmp)
                    if sample:
                        hyena_conv(zv, s * L, L, htab, htabk, K[n][0], K[n][1], cb)
                    else:
                        hyena_conv_p(zv, htab, K[n][0], K[n][1], cb)
            FP.put(zv, x1, x2, K[0][0], K[0][1], K[1][0], K[1][1])
        ring_allowed[0] = list(range(RING_N))
        if sample:
            exchange_start()
        else:
            w_out_phase(d["od_w_out"], j, half)

    def mlp_sample(l, prefetch_fn=None):
        mark(f'L{l} mlp sample')
        par = l % 2
        for lt in (2, 3):
            norm_mod(lt, lt, 1, 1, 24)
        mark('mlp sample')
        w1s = [load_w(d["s_w1"][l, 0:1024, jb * 512:(jb + 1) * 512], 8, 512) for jb in range(2)]
        w2s = [load_w(d["s_w2"][l, jb * 512:(jb + 1) * 512, :], 4, 1024) for jb in range(2)]
        for lt in (2, 3):
            hs = [BP.get() for _ in range(4)]
            for hc8 in range(8):
                jb, hc = hc8 // 4, hc8 % 4
                w1, w1k = w1s[jb]
                pb, pk = C.bank()
                for k in range(8):
                    C.mm(pb[:], w1[:, k, hc * 128:(hc + 1) * 128], hbuf[:, k, lt * TT:(lt + 1) * TT], k == 0, k == 7,
                         [w1k] + hkl(lt), [pk])
                rl = FP.get()
                C.actv(rl[0][:, 0:TT], pb[:], AF.Relu, [pk], [rl[1]])
                hv = hs[hc8 // 2][0][:, (hc8 % 2) * TT:(hc8 % 2 + 1) * TT]
                if hc8 % 2 == 0:
                    C.tt(hv, rl[0][:, 0:TT], rl[0][:, 0:TT], ALU.mult, [rl[1]], [hs[hc8 // 2][1]])
                else:
                    C.actv(hv, rl[0][:, 0:TT], AF.Square, [rl[1]], [hs[hc8 // 2][1]])
                FP.put(rl)
            for n in range(8):
                pb, pk = C.bank()
                for hc8 in range(8):
                    jb, hc = hc8 // 4, hc8 % 4
                    w2, w2k = w2s[jb]
                    hv = hs[hc8 // 2][0][:, (hc8 % 2) * TT:(hc8 % 2 + 1) * TT]
                    C.mm(pb[:], w2[:, hc, n * 128:(n + 1) * 128], hv, hc8 == 0, hc8 == 7, [w2k, hs[hc8 // 2][1]], [pk])
                yst = FP.get()
                C.ts(yst[0][:, 0:TT], pb[:], modM_[par][:, 40 + n, 1:2], None, ALU.mult, None, [pk, modMk_[par]], [yst[1]])
                C.dma(C.sp, ar_src[n * 128:(n + 1) * 128, (lt - 2) * TT:(lt - 1) * TT], yst[0][:, 0:TT], wt=ars_k, rt=yst[1],
                      nowaw=True)
                FP.put(yst)
            BP.put(*hs)
        if prefetch_fn is not None:
            prefetch_fn()
        C.allgather(ar_src, ar_dst, GROUPS, rt=ars_k, wt=ard_k, kind="AllReduce", op=ALU.add)

    def mlp_prompt(l, side=None, jbs=range(8), do_norm=True, pre=None):
        mark(f'L{l} mlp prompt')
        par = l % 2
        if do_norm:
            for lt in range(2):
                norm_mod(lt, lt, 1, 0, 24)
        mark('mlp body')
        hb = [(BP.get(), BP.get()) for _ in range(2)]
        seq = [(jb, lt) for jb in jbs for lt in range(2)]
        wcache = dict(pre or {})

        def getw1(jb):
            if ("1", jb) not in wcache:
                wcache[("1", jb)] = load_w(d["mlp_w1"][l, 0:1024, jb * 512:(jb + 1) * 512], 8, 512)
            return wcache[("1", jb)]

        def getw2(jb):
            if ("2", jb) not in wcache:
                wcache[("2", jb)] = load_w(d["mlp_w2"][l, jb * 512:(jb + 1) * 512, :], 4, 1024)
            return wcache[("2", jb)]

        def hview(idx, hc):
            pair = hb[idx % 2]
            return pair[hc // 2][0][:, (hc % 2) * TT:(hc % 2 + 1) * TT], pair[hc // 2][1]

        def stageA(idx):
            jb, lt = seq[idx]
            w1, w1k = getw1(jb)
            for hc in range(4):
                pb, pk = C.bank()
                for k in range(8):
                    C.mm(pb[:], w1[:, k, hc * 128:(hc + 1) * 128], hbuf[:, k, lt * TT:(lt + 1) * TT], k == 0, k == 7,
                         [w1k] + hkl(lt), [pk])
                rl = FP.get()
                C.actv(rl[0][:, 0:TT], pb[:], AF.Relu, [pk], [rl[1]])
                hv, hvk = hview(idx, hc)
                if hc % 2 == 0:
                    C.tt(hv, rl[0][:, 0:TT], rl[0][:, 0:TT], ALU.mult, [rl[1]], [hvk])
                else:
                    C.actv(hv, rl[0][:, 0:TT], AF.Square, [rl[1]], [hvk])
                FP.put(rl)

        def stageB(idx):
            jb, lt = seq[idx]
            w2, w2k = getw2(jb)
            for n in range(8):
                pb, pk = C.bank()
                for k in range(4):
                    hv, hvk = hview(idx, k)
                    C.mm(pb[:], w2[:, k, n * 128:(n + 1) * 128], hv, k == 0, k == 3, [w2k, hvk], [pk])
                resid(pb, pk, n, lt, modM_[par][:, 40 + n, 0:1])

        stageA(0)
        for idx in range(len(seq)):
            if idx + 1 < len(seq):
                stageA(idx + 1)
            stageB(idx)
            if side is not None and idx % 2 == 1:
                next(side, None)
        if side is not None and not do_norm:
            for _ in side:
                pass
        for pair in hb:
            BP.put(pair[0], pair[1])

    def sample_add():
        mark('mlp sample add')
        for n in range(8):
            yb = FP.get()
            C.dma(C.sp if n % 2 else C.pool, yb[0][:], ar_dst[n * 128:(n + 1) * 128, :], wt=yb[1], rt=ard_k)
            for lt in (2, 3):
                C.tt(x[:, n, lt * TT:(lt + 1) * TT], yb[0][:, (lt - 2) * TT:(lt - 1) * TT], x[:, n, lt * TT:(lt + 1) * TT],
                     ALU.add, [yb[1], xk[n][lt]], [xk[n][lt]])
            FP.put(yb)

    lay = list(layers if layers is not None else range(depth))
    g0 = modulation_gen(lay[0], staged=True)
    for _ in range(4):
        next(g0)
    pending_add = False
    for li, l in enumerate(lay):
        cur["par"] = l % 2
        nxt = lay[li + 1] if li + 1 < len(lay) else None
        side = modulation_gen(nxt) if nxt is not None else None
        if li == 0:
            def chain(g0=g0, rest=side):
                for _ in g0:
                    yield
                if rest is not None:
                    for _ in rest:
                        yield
            side = chain()
        side_gen[0] = side
        if l % 2 == 1:
            hyena_prep(l)
        for lt in range(2):
            norm_mod(lt, lt, 0, 0, 0)
        if l % 2 == 0:
            even_mixer(l, 0)
        else:
            odd_mixer(l, 0)
        if pending_add:
            sample_add()
            pending_add = False
        for lt in range(2):
            norm_mod(2 + lt, lt, 0, 1, 0)
        if l % 2 == 0:
            even_mixer_sample(l)
        else:
            odd_mixer(l, 1)
        side_gen[0] = None
        if li == 0:
            for _ in g0:
                pass
        mlp_prompt(l, side, range(0, 2), True)
        wd_ = d["ev_w_out"] if l % 2 == 0 else d["od_w_out"]
        pre_o = [load_w(wd_[l // 2, 0:1024, nb * 512:(nb + 1) * 512], 8, 512) for nb in range(2)]
        exchange_finish()
        w_out_phase(wd_, l // 2, 1, pre_o)
        pre = {}

        def pf(l=l, pre=pre):
            pre[("1", 2)] = load_w(d["mlp_w1"][l, 0:1024, 2 * 512:3 * 512], 8, 512)
            pre[("2", 2)] = load_w(d["mlp_w2"][l, 2 * 512:3 * 512, :], 4, 1024)
        mlp_sample(l, pf)
        mlp_prompt(l, side, range(2, 8), False, pre)
        sample_add()
        if debug:
            for c in range(8):
                for t in range(4):
                    C.dma(C.sp, o_dbg[l, :, c, t * TT:(t + 1) * TT], x[:, c, t * TT:(t + 1) * TT], rt=xk[c][t])
    if pending_add:
        sample_add()

    mark('final')
    for t in range(4):
        ts_ = slice(t * TT, (t + 1) * TT)
        rs = rstd_tile(t)
        for c in range(8):
            yst = FP.get()
            C.stt(yst[0][:, 0:TT], x[:, c, ts_], nfin[:, c:c + 1], rs[0][:, 0:TT], ALU.mult, ALU.mult,
                  [xk[c][t], nfin_k, rs[1]], [yst[1]])
            C.dma(C.sp, o_y[:, c, ts_], yst[0][:, 0:TT], rt=yst[1])
            FP.put(yst)
        FP.put(rs)
    C.dma(C.sp, o_lru, lst[:], rt=lstk)
    deps = {}
    for tok in C.out_dmas:
        C._add(deps, tok)
    C._wait(C.sp, deps)

    block = es.enter_context(nc.Block())

    @block.tensor
    def _(e):
        C.replay(C.pe, e)

    @block.scalar
    def _(e):
        C.replay(C.act, e)

    @block.vector
    def _(e):
        C.replay(C.dve, e)

    @block.gpsimd
    def _(e):
        C.replay(C.pool, e)

    @block.sync
    def _(e):
        C.replay(C.sp, e)

    build.stats = {n: len(e.prog) for n, e in (("pe", C.pe), ("act", C.act), ("dve", C.dve), ("pool", C.pool), ("sp", C.sp))}


_NC_CACHE = {}


def _get_nc(depth=DEPTH, debug=False, skip=(), layers=None):
    key = (depth, debug, tuple(skip), tuple(layers) if layers is not None else None)
    if key not in _NC_CACHE:
        _NC_CACHE[key] = build(depth, debug, skip, layers)
    return _NC_CACHE[key]


def run(inputs, depth=DEPTH, debug=False, skip=(), layers=None):
    shared = _prep_shared(inputs)
    in_maps = []
    for i in range(NCORES):
        m = dict(shared)
        m.update(_prep_core(inputs, i))
        in_maps.append(m)
    nc = _get_nc(depth, debug, skip, layers)
    res = run_bass_kernel_spmd(nc, in_maps, core_ids=list(range(NCORES)))
    return res.results


def kernel(**inputs):
    r = run(inputs)
    B_, S_ = 32, 256
    y_prompt = np.zeros((B_, S_, D), np.float32)
    y_sample = np.zeros((2, 1024, D), np.float32)
    k_state = np.zeros((B_, 2, S_, 2, 64), np.float32)
    v_state = np.zeros((B_, 2, S_, 2, 64), np.float32)
    lru_state = np.zeros((B_, 2, 2, 512), np.float32)
    for i in range(NCORES):
        yT = np.asarray(r[i]["yT"])
        Y = yT.transpose(2, 1, 0).reshape(NT, D)
        y_prompt[4 * i:4 * i + 4] = Y[:1024].reshape(4, 256, D)
        if i % 4 == 0:
            y_sample[i // 4] = Y[1024:]
        kv = np.asarray(r[i]["kv"]).reshape(4, 2, 256, 256)
        k_state[4 * i:4 * i + 4] = kv[..., 0:128].reshape(4, 2, 256, 2, 64)
        v_state[4 * i:4 * i + 4] = kv[..., 128:256].reshape(4, 2, 256, 2, 64)
        lt = np.asarray(r[i]["lruT"])
        lru_state[4 * i:4 * i + 4] = lt.transpose(4, 1, 2, 3, 0).reshape(4, 2, 2, 512)
    return (y_prompt, y_sample, k_state, v_state, lru_state)
```
